# Optimizing a Trainium2 kernel written in Bass

```python
import math
import jax, jax.numpy as jnp
from jax import lax
import numpy as np

D_MODEL = 1024
BATCH = 4
SEQ = 8192
DEPTH = 1

GLA_HEADS = 4
GLA_DK = 64
GLA_DV = 128
GLA_RANK = 16
GLA_GATE_TEMP = 16.0
GLA_CHUNK = 64
GLA_WIDTH = GLA_HEADS * GLA_DV
DIFF_HEADS = 4
DIFF_DH = 64
DIFF_DV = 2 * DIFF_DH
DIFF_WIDTH = DIFF_HEADS * DIFF_DV
Q_BLOCK = 128
MIX_WIDTH = GLA_WIDTH + DIFF_WIDTH
REL_BUCKETS = 32
REL_MAX_DIST = 128
D_FF = 2816
CONV_WIDTH = 3
N_MOD = 6
EPS = 1e-6

IN_SIZES = (
    GLA_HEADS * GLA_DK,
    GLA_HEADS * GLA_DK,
    GLA_WIDTH,
    GLA_WIDTH,
    GLA_RANK,
    GLA_RANK,
    DIFF_HEADS * 2 * DIFF_DH,
    DIFF_HEADS * 2 * DIFF_DH,
    DIFF_WIDTH,
)
IN_TOTAL = sum(IN_SIZES)

kernel_name = "hymba_gla_diffattn_convffn_encoder"


def rms_norm(x, w):
    xf = x.astype(jnp.float32)
    y = xf * lax.rsqrt(jnp.mean(xf * xf, axis=-1, keepdims=True) + EPS)
    return (y * w.astype(jnp.float32)).astype(x.dtype)


def modulate(h, shift, scale):
    return h * (1 + scale[:, None, :]) + shift[:, None, :]


def split_columns(t, sizes):
    outs, start = [], 0
    for s in sizes:
        outs.append(t[..., start:start + s])
        start += s
    return outs


def t5_bucket(rel):
    half = REL_BUCKETS // 2
    max_exact = half // 2
    ret = jnp.where(rel > 0, half, 0)
    n = jnp.abs(rel)
    nf = jnp.maximum(n, 1).astype(jnp.float32)
    large = max_exact + (jnp.log(nf / max_exact) / math.log(REL_MAX_DIST / max_exact)
                         * (half - max_exact)).astype(jnp.int32)
    large = jnp.minimum(large, half - 1)
    return ret + jnp.where(n < max_exact, n, large)


def gla_chunked(q, k, v, log_a, strict):
    B, H, L, DK = q.shape
    DV = v.shape[-1]
    N, C = L // GLA_CHUNK, GLA_CHUNK
    q = q.reshape(B, H, N, C, DK)
    k = k.reshape(B, H, N, C, DK)
    v = v.reshape(B, H, N, C, DV).astype(jnp.float32)
    b = jnp.cumsum(log_a.reshape(B, H, N, C, DK), axis=3)
    b_last = b[..., -1:, :]
    q_in = q * jnp.exp(b)
    k_in = k * jnp.exp(-b)
    k_st = k * jnp.exp(b_last - b)
    mask = jnp.tril(jnp.ones((C, C), jnp.float32), k=-1 if strict else 0)
    att = jnp.einsum('bhncd,bhnsd->bhncs', q_in, k_in) * mask
    o_intra = jnp.einsum('bhncs,bhnsv->bhncv', att, v)
    chunk_update = jnp.einsum('bhnsd,bhnsv->nbhdv', k_st, v)
    chunk_decay = jnp.exp(b_last[..., 0, :]).transpose(2, 0, 1, 3)

    def step(S, inp):
        upd, dec = inp
        return dec[..., None] * S + upd, S

    S0 = jnp.zeros((B, H, DK, DV), jnp.float32)
    _, S_prev = lax.scan(step, S0, (chunk_update, chunk_decay))
    o_inter = jnp.einsum('bhncd,nbhdv->bhncv', q_in, S_prev)
    return (o_intra + o_inter).reshape(B, H, L, DV)


def diff_attention(q, k, v, positions, rel_table, lam):
    B, H, _, L, DH = q.shape
    nb = L // Q_BLOCK
    scale = DH ** -0.5
    qb = q.reshape(B, H, 2, nb, Q_BLOCK, DH).transpose(3, 0, 1, 2, 4, 5)
    pb = positions.reshape(B, nb, Q_BLOCK).transpose(1, 0, 2)

    def block(args):
        q_blk, p_blk = args
        s = jnp.einsum('bhiqd,bhikd->bhiqk', q_blk, k).astype(jnp.float32) * scale
        rel = positions[:, None, :] - p_blk[:, :, None]
        bias = rel_table.astype(jnp.float32)[t5_bucket(rel)]
        s = s + bias.transpose(0, 3, 1, 2)[:, :, None]
        p = jax.nn.softmax(s, axis=-1)
        a = p[:, :, 0] - lam * p[:, :, 1]
        return jnp.einsum('bhqk,bhkv->bhqv', a, v.astype(jnp.float32))

    o = lax.map(block, (qb, pb))
    return o.transpose(1, 2, 0, 3, 4).reshape(B, H, L, -1)


def depthwise_conv3(u, w, bias):
    up = jnp.pad(u, ((0, 0), (1, 1), (0, 0)))
    return w[0] * up[:, :-2] + w[1] * up[:, 1:-1] + w[2] * up[:, 2:] + bias


def setup_inputs(seed: int = 0) -> dict:
    key = jax.random.key(seed)
    ks = jax.random.split(key, 26)
    nrm = lambda k, shape, s: jax.random.normal(k, shape, jnp.float32) * s
    D = D_MODEL
    return {
        "x": nrm(ks[0], (BATCH, SEQ, D), 1.0),
        "c": nrm(ks[1], (BATCH, D), 1.0),
        "positions": (jnp.arange(SEQ, dtype=jnp.int32)[None, :]
                      + jax.random.randint(ks[2], (BATCH, 1), 0, 1024, jnp.int32)),
        "w_ada": nrm(ks[3], (DEPTH, D, N_MOD * D), 0.5 * D ** -0.5),
        "b_ada": nrm(ks[4], (DEPTH, N_MOD * D), 0.02),
        "attn_norm_w": 1.0 + nrm(ks[5], (DEPTH, D), 0.02),
        "w_in": nrm(ks[6], (DEPTH, D, IN_TOTAL), D ** -0.5),
        "gla_dec_w_fwd": nrm(ks[7], (DEPTH, GLA_RANK, GLA_HEADS * GLA_DK), GLA_RANK ** -0.5),
        "gla_dec_b_fwd": nrm(ks[8], (DEPTH, GLA_HEADS * GLA_DK), 0.1),
        "gla_dec_w_bwd": nrm(ks[9], (DEPTH, GLA_RANK, GLA_HEADS * GLA_DK), GLA_RANK ** -0.5),
        "gla_dec_b_bwd": nrm(ks[10], (DEPTH, GLA_HEADS * GLA_DK), 0.1),
        "gla_norm_w": 1.0 + nrm(ks[11], (DEPTH, GLA_WIDTH), 0.02),
        "diff_lambda_q1": nrm(ks[12], (DEPTH, DIFF_DH), 0.1),
        "diff_lambda_k1": nrm(ks[13], (DEPTH, DIFF_DH), 0.1),
        "diff_lambda_q2": nrm(ks[14], (DEPTH, DIFF_DH), 0.1),
        "diff_lambda_k2": nrm(ks[15], (DEPTH, DIFF_DH), 0.1),
        "diff_norm_w": 1.0 + nrm(ks[16], (DEPTH, DIFF_DV), 0.02),
        "rel_bias_table": nrm(ks[17], (REL_BUCKETS, DIFF_HEADS), 0.5),
        "w_out": nrm(ks[18], (DEPTH, MIX_WIDTH, D), MIX_WIDTH ** -0.5),
        "ffn_norm_w": 1.0 + nrm(ks[19], (DEPTH, D), 0.02),
        "w_up": nrm(ks[20], (DEPTH, D, 2 * D_FF), D ** -0.5),
        "conv_w": nrm(ks[21], (DEPTH, CONV_WIDTH, 2 * D_FF), CONV_WIDTH ** -0.5),
        "conv_b": nrm(ks[22], (DEPTH, 2 * D_FF), 0.02),
        "w_down": nrm(ks[23], (DEPTH, D_FF, D), D_FF ** -0.5),
        "final_norm_w": 1.0 + nrm(ks[24], (D,), 0.02),
    }


def reference(x, c, positions, w_ada, b_ada, attn_norm_w, w_in,
              gla_dec_w_fwd, gla_dec_b_fwd, gla_dec_w_bwd, gla_dec_b_bwd, gla_norm_w,
              diff_lambda_q1, diff_lambda_k1, diff_lambda_q2, diff_lambda_k2, diff_norm_w,
              rel_bias_table, w_out, ffn_norm_w, w_up, conv_w, conv_b, w_down, final_norm_w):
    B, L, _ = x.shape
    f32 = jnp.float32
    mod_in = jax.nn.silu(c)

    def heads(t, h):
        return t.reshape(B, L, h, -1).transpose(0, 2, 1, 3)

    for i in range(DEPTH):
        lambda_init = 0.8 - 0.6 * math.exp(-0.3 * i)
        mod = mod_in @ w_ada[i] + b_ada[i]
        sh_a, sc_a, g_a, sh_f, sc_f, g_f = jnp.split(mod, N_MOD, axis=-1)

        h = modulate(rms_norm(x, attn_norm_w[i]), sh_a, sc_a)
        proj = h @ w_in[i]
        q_g, k_g, v_g, r_g, lr_f, lr_b, q_d, k_d, v_d = split_columns(proj, IN_SIZES)

        la_f = jax.nn.log_sigmoid((lr_f @ gla_dec_w_fwd[i] + gla_dec_b_fwd[i]).astype(f32)) / GLA_GATE_TEMP
        la_b = jax.nn.log_sigmoid((lr_b @ gla_dec_w_bwd[i] + gla_dec_b_bwd[i]).astype(f32)) / GLA_GATE_TEMP
        qa = heads(q_g, GLA_HEADS).astype(f32) * GLA_DK ** -0.5
        ka = heads(k_g, GLA_HEADS).astype(f32)
        va = heads(v_g, GLA_HEADS).astype(f32)
        la_f = heads(la_f, GLA_HEADS)
        la_b = heads(la_b, GLA_HEADS)
        o_fwd = gla_chunked(qa, ka, va, la_f, strict=False)
        flip = lambda t: jnp.flip(t, axis=2)
        o_bwd = flip(gla_chunked(flip(qa), flip(ka), flip(va), flip(la_b), strict=True))
        o_a = (o_fwd + o_bwd).transpose(0, 2, 1, 3)
        o_a = rms_norm(o_a, gla_norm_w[i].reshape(GLA_HEADS, GLA_DV)).reshape(B, L, GLA_WIDTH)
        o_a = o_a.astype(x.dtype) * jax.nn.silu(r_g)

        lam = (jnp.exp(jnp.sum(diff_lambda_q1[i] * diff_lambda_k1[i]).astype(f32))
               - jnp.exp(jnp.sum(diff_lambda_q2[i] * diff_lambda_k2[i]).astype(f32))
               + lambda_init)
        qd = q_d.reshape(B, L, DIFF_HEADS, 2, DIFF_DH).transpose(0, 2, 3, 1, 4)
        kd = k_d.reshape(B, L, DIFF_HEADS, 2, DIFF_DH).transpose(0, 2, 3, 1, 4)
        vd = heads(v_d, DIFF_HEADS)
        o_b = diff_attention(qd, kd, vd, positions, rel_bias_table, lam)
        o_b = rms_norm(o_b.transpose(0, 2, 1, 3), diff_norm_w[i]) * (1 - lambda_init)
        o_b = o_b.reshape(B, L, DIFF_WIDTH).astype(x.dtype)

        mixed = jnp.concatenate([o_a, o_b], axis=-1) @ w_out[i]
        x = x + g_a[:, None, :] * mixed

        h2 = modulate(rms_norm(x, ffn_norm_w[i]), sh_f, sc_f)
        u = depthwise_conv3(h2 @ w_up[i], conv_w[i], conv_b[i])
        gate, val = jnp.split(u, 2, axis=-1)
        x = x + g_f[:, None, :] * ((jax.nn.silu(gate) * val) @ w_down[i])

    return rms_norm(x, final_norm_w)
```

```python
import math
import numpy as np
import ml_dtypes
from contextlib import ExitStack
import concourse.bass as bass
import concourse.mybir as mybir
from concourse.bass_utils import run_bass_kernel_spmd

F32 = mybir.dt.float32
BF16 = mybir.dt.bfloat16
I32 = mybir.dt.int32
AF = mybir.ActivationFunctionType
ALU = mybir.AluOpType
AX = mybir.AxisListType

D = 1024
L = 8192
NT = 64
NOWN = 32
NQT = 33
TQ = NQT * 128
DFF = 2816
NFC = 22
EPS = 1e-6
QG, KG, ZUP, ZDN, VG, RG, QD, VD, KD = 0, 256, 512, 768, 1024, 1536, 2048, 2560, 3072
WALLC = 3584

def _t5_bucket_np(rel):
    half, max_exact = 16, 8
    ret = np.where(rel > 0, half, 0)
    n = np.abs(rel)
    nf = np.maximum(n, 1).astype(np.float32)
    lg = (np.log(nf / np.float32(max_exact)) / np.float32(math.log(128 / max_exact))
          * np.float32(half - max_exact)).astype(np.float32)
    large = max_exact + lg.astype(np.int32)
    large = np.minimum(large, half - 1)
    return ret + np.where(n < max_exact, n, large)


def _t5_breaks():
    rel = np.arange(-400, 401)
    b = _t5_bucket_np(rel)
    out = []
    for i in range(1, len(rel)):
        if b[i] != b[i - 1]:
            out.append((int(rel[i]), int(b[i]), int(b[i - 1])))
    return out, int(b[0])


T5_BREAKS, T5_B0 = _t5_breaks()
NBRK = len(T5_BREAKS)


class Buf:
    __slots__ = ("name", "w", "r")

    def __init__(self, name):
        self.name = name
        self.w = {}
        self.r = {}


class Op:
    __slots__ = ("eng", "fn", "idx", "deps", "dma")

    def __init__(self, eng, fn, idx):
        self.eng = eng
        self.fn = fn
        self.idx = idx
        self.deps = {}
        self.dma = None


CENG = ("pe", "act", "dve", "pool")
ALLE = ("pe", "act", "dve", "pool", "sp")


class Sched:
    def __init__(self, rings):
        self.ops = {e: [] for e in ALLE}
        self.rings = rings
        self.ring_pos = {q: 0 for q in rings}
        self.ring_val = {q: [0] * n for q, n in rings.items()}

    def _deps(self, eng, r, w, dma):
        deps = {}

        def add(k, v):
            if deps.get(k, 0) < v:
                deps[k] = v
        for b in r:
            for k, v in b.w.items():
                if (not dma) and k == eng and eng == "pe":
                    continue
                add(k, v)
        for b in w:
            for k, v in b.w.items():
                if (not dma) and k == eng:
                    continue
                add(k, v)
            for k, v in b.r.items():
                if (not dma) and k == eng:
                    continue
                add(k, v)
        return deps

    def op(self, eng, fn, r=(), w=(), after_prev=False):
        o = Op(eng, fn, len(self.ops[eng]) + 1)
        o.deps = self._deps(eng, r, w, False)
        if after_prev and o.idx > 1:
            o.deps[eng] = max(o.deps.get(eng, 0), o.idx - 1)
        for b in w:
            b.w = {eng: o.idx}
            b.r = {}
        for b in r:
            b.r[eng] = o.idx
        self.ops[eng].append(o)
        return o

    def dma(self, q, fn, r=(), w=(), disjoint=()):
        i = self.ring_pos[q]
        self.ring_pos[q] = (i + 1) % self.rings[q]
        prev = self.ring_val[q][i]
        self.ring_val[q][i] = prev + 16
        key = ("dma", q, i)
        o = Op(q, fn, len(self.ops[q]) + 1)
        o.deps = self._deps(q, r, w, True)
        if prev > 0:
            o.deps[key] = max(o.deps.get(key, 0), prev)
        o.dma = key
        for b in w:
            b.w = {key: prev + 16}
            b.r = {}
        for b in disjoint:
            b.w[key] = prev + 16
        for b in r:
            b.r[key] = prev + 16
        self.ops[q].append(o)
        return o

    def barrier(self):
        last = {e: len(self.ops[e]) for e in CENG}
        dl = {}
        for q, vals in self.ring_val.items():
            for i, v in enumerate(vals):
                if v > 0:
                    dl[("dma", q, i)] = v
        for e in ALLE:
            o = Op(e, lambda eng: eng.nop(), len(self.ops[e]) + 1)
            for k, v in last.items():
                if k != e and v > 0 and not self.ops[k][v - 1].dma:
                    o.deps[k] = v
                elif k != e and v > 0:
                    j = v
                    while j > 0 and self.ops[k][j - 1].dma:
                        j -= 1
                    if j > 0:
                        o.deps[k] = j
            o.deps.update(dl)
            self.ops[e].append(o)

    def emit(self, nc, stack):
        need = {e: set() for e in CENG}
        for e in ALLE:
            for o in self.ops[e]:
                for k, v in o.deps.items():
                    if isinstance(k, str):
                        need[k].add(v)
        ms = {e: {idx: i + 1 for i, idx in enumerate(sorted(need[e]))} for e in CENG}
        self.stats = {e: (len(self.ops[e]), len(ms.get(e, ()))) for e in ALLE}
        csem = {e: stack.enter_context(nc.semaphore("c_" + e)) for e in CENG}
        dsem = {}
        for q, n in self.rings.items():
            for i in range(n):
                dsem[("dma", q, i)] = stack.enter_context(nc.semaphore("d_%s_%d" % (q, i)))
        final = {}
        for q, vals in self.ring_val.items():
            for i, v in enumerate(vals):
                if v > 0:
                    final[("dma", q, i)] = v
        block = stack.enter_context(nc.Block())

        def run(ename):
            def body(eng):
                seen = {}
                for o in self.ops[ename]:
                    for k, v in o.deps.items():
                        if isinstance(k, str):
                            val = ms[k][v]
                            sem = csem[k]
                        else:
                            val = v
                            sem = dsem[k]
                        if seen.get(k, 0) >= val:
                            continue
                        eng.wait_ge(sem, val)
                        seen[k] = val
                    ins = o.fn(eng)
                    if o.dma is not None:
                        ins.then_inc(dsem[o.dma], 16)
                    elif ename in ms and o.idx in ms[ename]:
                        ins.then_inc(csem[ename], 1)
                if ename == "sp":
                    for k, v in final.items():
                        if seen.get(k, 0) < v:
                            eng.wait_ge(dsem[k], v)
            return body
        block.tensor(run("pe"))
        block.scalar(run("act"))
        block.vector(run("dve"))
        block.gpsimd(run("pool"))
        block.sync(run("sp"))


def build_program(phases=5, nt1=NT, nt2=NQT, nheads=4, dbg=False):
    nc = bass.Bass("TRN2", target_bir_lowering=False)
    S = Sched({"sp": 8, "pool": 4})
    stack = ExitStack()

    def din(name, shape, dt=F32):
        return nc.dram_tensor(name, list(shape), dt, kind="ExternalInput").ap()

    x = din("x", [L, D])
    pos = din("pos", [L], I32)
    ccol = din("ccol", [128, 8])
    w_ada = din("w_ada", [D, 6 * D])
    b_ada = din("b_ada", [1, 6 * D])
    anw = din("anw", [1, D])
    fnw = din("fnw", [1, D])
    finw = din("finw", [D])
    w_in = din("w_in", [D, 3104])
    decw = [din("decw_up", [16, 256]), din("decw_dn", [16, 256])]
    decb = [din("decb_up", [1, 256]), din("decb_dn", [1, 256])]
    gnw = din("gnw", [512])
    lamv = [din("lam%d" % i, [64]) for i in range(4)]
    dnw = din("dnw", [128])
    tabA = din("tabA", [4, NBRK])
    tabB = din("tabB", [4, NBRK])
    tab0 = din("tab0", [4, 1])
    w_out = din("w_out", [D, D])
    w_up = din("w_up", [D, 2 * DFF])
    cwp = din("cwp", [128, 3 * 44])
    cbp = din("cbp", [128, 44])
    w_down = din("w_down", [DFF, D])
    identf_d = din("identf", [128, 128])
    tri_d = [din("tri%d" % i, [128, 128]) for i in range(4)]
    cind_d = din("cind", [128, 2])
    mask_d = [din("mask_up", [128, 128]), din("mask_dn", [128, 128])]
    y = nc.dram_tensor("y", [NOWN * 128, D], F32, kind="ExternalOutput").ap()
    KdT = nc.dram_tensor("KdT", [4, 128, L], BF16, kind="Internal").ap()
    Vd = nc.dram_tensor("Vd", [4, 128, NT, 129], BF16, kind="Internal").ap()
    QdT = nc.dram_tensor("QdT", [4, 128, TQ], BF16, kind="Internal").ap()
    X1 = nc.dram_tensor("X1", [NOWN * 128, D], F32, kind="Internal").ap()
    H2T = nc.dram_tensor("H2T", [8, 128, TQ + 2], BF16, kind="Internal").ap()
    SdnD = nc.dram_tensor("SdnD", [2 * NQT, 128, 256], BF16, kind="Internal").ap()
    CatT = nc.dram_tensor("CatT", [8, 128, TQ], BF16, kind="Internal").ap()
    dKdT, dVd, dQdT, dX1, dH2T = Buf("KdT"), Buf("Vd"), Buf("QdT"), Buf("X1"), Buf("H2T")
    dSdn, dCat = Buf("SdnD"), Buf("CatT")
    _dscr = [dKdT, dVd, dQdT, dX1, dH2T, dSdn, dCat]

    def sb(name, shape, dt=F32):
        return stack.enter_context(nc.sbuf_tensor("s_" + name, list(shape), dt))

    ps = stack.enter_context(nc.psum_tensor("ps", [128, 4096], F32))
    PB = [Buf("pb%d" % i) for i in range(8)]

    def bank(i, n=1):
        return ps[:, i * 512:(i + n) * 512]

    def bankb(i):
        return ps[:, i * 512:(i + 1) * 512].bitcast(BF16)

    pe_mode = [None]

    def _mode(ap):
        sh = ap.shape
        k = int(sh[0]); m = int(np.prod(sh[1:]))
        rt = 32 if k <= 32 else (64 if k <= 64 else 128)
        ct = 32 if m <= 32 else (64 if m <= 64 else 128)
        return (rt, ct)

    def _switch(ap):
        md = _mode(ap)
        sw = pe_mode[0] is not None and md != pe_mode[0]
        pe_mode[0] = md
        return sw

    def MM(out, lhsT, rhs, start=True, stop=True, r=(), w=(), sync=False, **kw):
        sync = _switch(lhsT) or sync
        S.op("pe", lambda e: e.matmul(out, lhsT, rhs, start=start, stop=stop, **kw), r, w, after_prev=sync)

    def TR(out, in_, ident, r=(), w=()):
        S.op("pe", lambda e: e.transpose(out, in_, ident), r, w, after_prev=_switch(in_))

    def ACT(out, in_, func, r=(), w=(), **kw):
        S.op("act", lambda e: e.activation(out, in_, func, **kw), r, w)

    def TT(eng, out, in0, in1, op, r=(), w=()):
        S.op(eng, lambda e: e.tensor_tensor(out, in0, in1, op), r, w)

    def TS(eng, out, in0, s1, s2, op0, op1=None, r=(), w=(), **kw):
        if op1 is None:
            S.op(eng, lambda e: e.tensor_scalar(out, in0, s1, s2, op0, **kw), r, w)
        else:
            S.op(eng, lambda e: e.tensor_scalar(out, in0, s1, s2, op0, op1, **kw), r, w)

    def STT(eng, out, in0, sc, in1, op0, op1, r=(), w=(), **kw):
        S.op(eng, lambda e: e.scalar_tensor_tensor(out, in0, sc, in1, op0, op1, **kw), r, w)

    def CP(eng, out, in_, r=(), w=()):
        if eng == "act":
            S.op("act", lambda e: e.copy(out, in_), r, w)
        else:
            S.op(eng, lambda e: e.tensor_copy(out, in_), r, w)

    def MSET(eng, ap, val, w=()):
        S.op(eng, lambda e: e.memset(ap, val), (), w)

    DSCR = set(id(b) for b in _dscr)

    def DMA(q, out, in_, r=(), w=()):
        wn = [b for b in w if id(b) not in DSCR]
        wd = [b for b in w if id(b) in DSCR]
        S.dma(q, lambda e: e.dma_start(out=out, in_=in_), r, wn, wd)

    identf = sb("identf", [128, 128]); b_identf = Buf("identf")
    identb = sb("identb", [128, 128], BF16); b_identb = Buf("identb")
    tri = [sb("tri%d" % i, [128, 128]) for i in range(4)]; b_tri = Buf("tri")
    cind = sb("cind", [128, 2])
    maskt = [sb("mask%d" % i, [128, 128]) for i in range(2)]
    modcol = sb("modcol", [128, 32]); b_modcol = Buf("modcol")
    GA = sb("GA", [128, D]); b_GA = Buf("GA")
    GF = sb("GF", [128, D]); b_GF = Buf("GF")
    WF = sb("WF", [128, D]); b_WF = Buf("WF")
    gnwb = sb("gnwb", [128, 512]); b_cst = Buf("cst")
    dnwb = sb("dnwb", [128, 128])
    lamc = sb("lamc", [128, 4]); b_lam = Buf("lam")
    onesK = sb("onesK", [128, 128], BF16)
    decbK = sb("decbK", [128, 512], BF16)
    onesf = sb("onesf", [1, 128])
    b_decbb = Buf("decbb")
    zerob = sb("zerob", [128, 512], BF16); b_zero = Buf("zero")
    cw = sb("cw", [128, 3 * 44]); cb = sb("cb", [128, 44]); b_cw = Buf("cw")
    ARENA = 180 * 1024
    arena = sb("arena", [128, ARENA // 2], BF16)
    apos = [0]

    def carve(shape, dt=F32):
        n = int(np.prod(shape[1:])) * (4 if dt in (F32, I32) else 2)
        n = (n + 63) // 64 * 64
        a = apos[0]
        assert a + n <= ARENA, ("arena overflow", a, n)
        apos[0] = a + n
        v = arena[0:shape[0], a // 2:(a + n) // 2]
        if dt != BF16:
            v = v.bitcast(dt)
        v = v[:, 0:int(np.prod(shape[1:]))]
        if len(shape) == 3:
            v = v.rearrange("p (a b) -> p a b", a=shape[1])
        elif len(shape) == 4:
            v = v.rearrange("p (a b c) -> p a b c", a=shape[1], b=shape[2])
        return v

    b_c0 = Buf("c0")
    DMA("sp", identf[:], identf_d, w=[b_identf])
    b_tris = [Buf("tri%d" % i) for i in range(4)]
    for i in range(4):
        DMA("sp", tri[i][:], tri_d[i], w=[b_tris[i]])
    b_tri2 = Buf("tri2")
    DMA("sp", cind[:], cind_d, w=[b_tri2])
    b_mask = [Buf("mask0"), Buf("mask1")]
    for i in range(2):
        DMA("sp", maskt[i][:], mask_d[i], w=[b_mask[i]])
    DMA("sp", WF[:], finw.partition_broadcast(128), w=[b_WF])
    DMA("sp", gnwb[:], gnw.partition_broadcast(128), w=[b_cst])
    b_dnw = Buf("dnw")
    DMA("sp", dnwb[:], dnw.partition_broadcast(128), w=[b_dnw])
    CP("dve", identb[:], identf[:], r=[b_identf], w=[b_identb])
    MSET("dve", onesK[:], 0.0, w=[b_c0])
    MSET("dve", onesK[0:1, :], 1.0, w=[b_c0])
    MSET("dve", decbK[:], 0.0, w=[b_decbb])
    MSET("dve", onesf[:], 1.0, w=[b_c0])
    MSET("dve", zerob[:], 0.0, w=[b_zero])
    TS("dve", dnwb[:], dnwb[:], 0.8, None, ALU.mult, r=[b_dnw], w=[b_dnw])

    strips = carve([128, 4, 11 * 128]); b_strips = [Buf("strip%d" % h) for h in range(4)]
    a0 = apos[0]
    KB = 1024
    ccs = carve([128, 8]); b_ccs = Buf("ccs")
    tmp8 = carve([128, 8]); b_tmp8 = Buf("tmp8")
    silc = carve([128, 8]); b_silc = Buf("silc")
    DMA("sp", ccs, ccol, w=[b_ccs])
    ACT(tmp8, ccs, AF.Exp, scale=-1.0, r=[b_ccs], w=[b_tmp8])
    TS("dve", tmp8, tmp8, 1.0, None, ALU.add, r=[b_tmp8], w=[b_tmp8])
    S.op("dve", lambda e: e.reciprocal(tmp8, tmp8), [b_tmp8], [b_tmp8])
    TT("dve", silc, ccs, tmp8, ALU.mult, r=[b_ccs, b_tmp8], w=[b_silc])
    lv = carve([128, 4, 64]); b_lv = Buf("lv")
    for i in range(4):
        DMA("sp", lv[:, i, :], lamv[i].partition_broadcast(128), w=[Buf("lvx%d" % i)])
    lp = carve([128, 2, 64]); b_lp = Buf("lp")
    ls = carve([128, 2]); b_ls = Buf("ls")
    S.barrier()
    TT("dve", lp[:, 0, :], lv[:, 0, :], lv[:, 1, :], ALU.mult, r=[b_lv], w=[b_lp])
    TT("dve", lp[:, 1, :], lv[:, 2, :], lv[:, 3, :], ALU.mult, r=[b_lv], w=[b_lp])
    S.op("dve", lambda e: e.tensor_reduce(ls, lp, AX.X, ALU.add), [b_lp], [b_ls])
    ACT(ls, ls, AF.Exp, r=[b_ls], w=[b_ls])
    TT("dve", lamc[:, 0:1], ls[:, 0:1], ls[:, 1:2], ALU.subtract, r=[b_ls], w=[b_lam])
    TS("dve", lamc[:, 0:1], lamc[:, 0:1], 0.2, None, ALU.add, r=[b_lam], w=[b_lam])
    TS("dve", lamc[:, 1:2], lamc[:, 0:1], -1.0, None, ALU.mult, r=[b_lam], w=[b_lam])

    _save_apos = apos[0]
    apos[0] = 102 * KB
    posq_i = carve([128, 640], I32); b_posq = Buf("posq")
    relf = carve([128, 640]); b_relf = Buf("relf")
    posk_i = carve([128, 1], I32)
    posk_f = carve([128, 1]); b_posk = Buf("posk")
    tA = carve([128, 4, NBRK]); tB = carve([128, 4, NBRK]); t0 = carve([128, 4]); b_tab = Buf("tab")
    tmpS = carve([128, 640]); b_tmpS = Buf("tmpS")
    s5 = carve([128, 640]); b_s5 = Buf("s5")
    DMA("sp", posq_i, pos[0:640].partition_broadcast(128), w=[Buf("pq")])
    DMA("sp", posk_i, pos[256:384].rearrange("(p o) -> p o", o=1), w=[Buf("pk")])
    for h in range(4):
        DMA("sp", tA[:, h, :], tabA[h].partition_broadcast(128), w=[Buf("ta")])
        DMA("sp", tB[:, h, :], tabB[h].partition_broadcast(128), w=[Buf("tb")])
        DMA("sp", t0[:, h:h + 1], tab0[h].partition_broadcast(128), w=[Buf("t0")])
    assert apos[0] <= 124 * KB, apos[0]
    S.barrier()
    CP("dve", relf, posq_i, r=[b_posq], w=[b_relf])
    CP("dve", posk_f, posk_i, r=[], w=[b_posk])
    TS("dve", relf, relf, posk_f[:, 0:1], -1.0, ALU.subtract, ALU.mult, r=[b_relf, b_posk], w=[b_relf])
    TT("dve", tA, tA, tB, ALU.subtract, r=[b_tab], w=[b_tab])
    for h in range(nheads):
        TS("dve", s5, relf, 0.0, t0[:, h:h + 1], ALU.mult, ALU.add, r=[b_relf, b_tab], w=[b_s5])
        for j, (rj, _, _) in enumerate(T5_BREAKS):
            TS("dve", tmpS, relf, float(rj) - 0.5, tA[:, h, j:j + 1], ALU.is_ge, ALU.mult,
               r=[b_relf, b_tab], w=[b_tmpS])
            TT("dve", s5, s5, tmpS, ALU.add, r=[b_tmpS, b_s5], w=[b_s5])
        CP("dve", strips[:, h, 0:512].rearrange("p (a b) -> p a b", a=4),
           s5[:, 0:128].unsqueeze(1).to_broadcast([128, 4, 128]), r=[b_s5], w=[b_strips[h]])
        CP("dve", strips[:, h, 512:896], s5[:, 128:512], r=[b_s5], w=[b_strips[h]])
        CP("dve", strips[:, h, 896:1408].rearrange("p (a b) -> p a b", a=4),
           s5[:, 512:640].unsqueeze(1).to_broadcast([128, 4, 128]), r=[b_s5], w=[b_strips[h]])

    apos[0] = _save_apos

    modrow = carve([1, 6 * D]); b_modrow = Buf("modrow")
    badar = [carve([1, 512]) for _ in range(2)]; b_badar = [Buf("badar0"), Buf("badar1")]
    wada = [carve([128, 8, 256]) for _ in range(2)]; b_wada = [Buf("wada0"), Buf("wada1")]
    w_ada_v = w_ada.rearrange("(k p) n -> p k n", p=128)
    for blk in range(24):
        sl = blk % 2
        cs = slice(blk * 256, (blk + 1) * 256)
        DMA("sp", wada[sl], w_ada_v[:, :, cs], w=[b_wada[sl]])
        DMA("sp", badar[sl][:, 0:256], b_ada[:, cs], w=[b_badar[sl]])
        for k in range(8):
            MM(bank(sl)[0:1, 0:256], silc[:, k:k + 1], wada[sl][:, k, :], start=(k == 0), stop=(k == 7),
               r=[b_silc, b_wada[sl]], w=[PB[sl]])
        TT("dve", modrow[:, cs], bank(sl)[0:1, 0:256], badar[sl][:, 0:256],
           ALU.add, r=[PB[sl], b_badar[sl]], w=[b_modrow])
    rows = carve([1, 2, D]); b_rows = Buf("rows")
    nwr = carve([1, 2, D]); b_nwr = Buf("nwr")
    DMA("sp", nwr[:, 0, :], anw, w=[Buf("nw0")])
    DMA("sp", nwr[:, 1, :], fnw, w=[b_nwr])
    S.barrier()
    STT("dve", rows[:, 0, :], modrow[:, D:2 * D], 1.0, nwr[:, 0, :], ALU.add, ALU.mult, r=[b_modrow, b_nwr], w=[b_rows])
    STT("dve", rows[:, 1, :], modrow[:, 4 * D:5 * D], 1.0, nwr[:, 1, :], ALU.add, ALU.mult, r=[b_modrow, b_nwr], w=[b_rows])
    srcs = [rows[:, 0, :], modrow[:, 0:D], rows[:, 1, :], modrow[:, 3 * D:4 * D]]
    for v in range(4):
        for j in range(8):
            MM(bank(2)[:, v * 8 + j:v * 8 + j + 1], srcs[v][:, j * 128:(j + 1) * 128], identf[0:1, 0:1],
               r=[b_rows, b_modrow, b_identf], w=[PB[2]])
    CP("dve", modcol[:], bank(2)[:, 0:32], r=[PB[2]], w=[b_modcol])
    for (G, bG, off) in ((GA, b_GA, 2 * D), (GF, b_GF, 5 * D)):
        for hh in range(2):
            MM(bank(3 + hh), onesf[0:1, :], modrow[:, off + hh * 512:off + (hh + 1) * 512],
               r=[b_c0, b_modrow], w=[PB[3 + hh]])
            CP("dve", G[:, hh * 512:(hh + 1) * 512], bank(3 + hh), r=[PB[3 + hh]], w=[bG])
    dbs = carve([1, 512]); b_dbs = Buf("dbs")
    DMA("sp", dbs[:, 0:256], decb[0], w=[Buf("dbx")])
    DMA("sp", dbs[:, 256:512], decb[1], w=[b_dbs])
    lrs = carve([128, 8, 32]); b_lrs = Buf("lrs")
    w_in_v = w_in.rearrange("(k p) n -> p k n", p=128)
    DMA("sp", lrs, w_in_v[:, :, 1536:1568], w=[b_lrs])
    dws = carve([16, 2, 256]); b_dws = Buf("dws")
    DMA("sp", dws[:, 0, :], decw[0], w=[Buf("dwx")])
    DMA("sp", dws[:, 1, :], decw[1], w=[b_dws])
    lrT = carve([16, 2, D]); b_lrT = Buf("lrT")
    S.barrier()
    CP("dve", decbK[0:1, :], dbs, r=[b_dbs], w=[b_decbb])
    assert apos[0] <= 102 * KB, apos[0]
    apos[0] = 124 * KB
    Wall = carve([128, 8, WALLC], BF16); b_Wall = Buf("Wall")
    S.barrier()
    for (dst, src) in ((QG, 0), (VG, 512), (RG, 1024), (QD, 1568), (KD, 2080), (VD, 2592)):
        DMA("pool", Wall[:, :, dst:dst + 512], w_in_v[:, :, src:src + 512], w=[Buf("wl%d" % dst)])
    S.barrier()
    for d in range(2):
        for k in range(8):
            TR(bank(5 + k // 4)[0:16, (k % 4) * 128:(k % 4 + 1) * 128], lrs[:, k, d * 16:(d + 1) * 16], identf[:],
               r=[b_lrs, b_identf], w=[PB[5 + k // 4]])
        CP("dve", lrT[:, d, 0:512], bank(5)[0:16, :], r=[PB[5]], w=[b_lrT])
        CP("dve", lrT[:, d, 512:1024], bank(6)[0:16, :], r=[PB[6]], w=[b_lrT])
        for k in range(8):
            bk = 5 + (k % 2)
            MM(bank(bk)[:, 0:256], lrT[:, d, k * 128:(k + 1) * 128], dws[:, d, :], r=[b_lrT, b_dws], w=[PB[bk]])
            CP("dve", Wall[:, k, ZUP + d * 256:ZUP + (d + 1) * 256], bank(bk)[:, 0:256], r=[PB[bk]], w=[b_Wall])
    S.barrier()

    apos[0] = a0
    xts = [carve([128, D]) for _ in range(2)]; b_xt = [Buf("xt0"), Buf("xt1")]
    junk = carve([128, D], BF16); b_junk = Buf("junk")
    xn = carve([128, D], BF16); b_xn = Buf("xn")
    st2 = carve([128, 2]); b_st = Buf("st")
    hT = [carve([128, 8, 128], BF16) for _ in range(2)]; b_hT = [Buf("hT0"), Buf("hT1")]
    cnt = {"x": 0}

    def front_end(xsrc, colbase, tbank, rd=()):
        i = cnt["x"] % 2
        cnt["x"] += 1
        ACT(junk, xsrc, AF.Square, accum_out=st2[:, 0:1], r=list(rd), w=[b_junk, b_st])
        TS("dve", st2[:, 1:2], st2[:, 0:1], 1.0 / D, EPS, ALU.mult, ALU.add, r=[b_st], w=[b_st])
        ACT(st2[:, 1:2], st2[:, 1:2], AF.Ln, r=[b_st], w=[b_st])
        ACT(st2[:, 1:2], st2[:, 1:2], AF.Exp, scale=-0.5, r=[b_st], w=[b_st])
        TS("dve", xn, xsrc, st2[:, 1:2], None, ALU.mult, r=list(rd) + [b_st], w=[b_xn])
        tb = bankb(tbank)
        for j in range(8):
            TR(tb[:, j * 128:(j + 1) * 128], xn[:, j * 128:(j + 1) * 128], identb[:], r=[b_xn, b_identb], w=[PB[tbank]])
        tv = tb[:, 0:1024].rearrange("p (a b) -> p a b", a=8)
        sc = modcol[:, colbase:colbase + 8].unsqueeze(2).to_broadcast([128, 8, 128])
        shf = modcol[:, colbase + 8:colbase + 16].unsqueeze(2).to_broadcast([128, 8, 128])
        TT("dve", hT[i], tv, sc, ALU.mult, r=[PB[tbank], b_modcol], w=[b_hT[i]])
        TT("dve", hT[i], hT[i], shf, ALU.add, r=[b_hT[i], b_modcol], w=[b_hT[i]])
        return hT[i], b_hT[i]

    def proj_tok(bk, h_ap, h_b, c0, n=512, extra=None):
        for k in range(8):
            MM(bank(bk)[:, 0:n], h_ap[:, k, :], Wall[:, k, c0:c0 + n], start=(k == 0), stop=(k == 7 and extra is None),
               r=[h_b, b_Wall], w=[PB[bk]])
        if extra is not None:
            extra()

    Sst = [carve([128, 2, 128]) for _ in range(2)]; b_S = [Buf("Sup"), Buf("Sdn")]
    sdn_t = [carve([128, 2, 2, 128], BF16) for _ in range(2)]; b_sdn = [Buf("sdn0"), Buf("sdn1")]
    Sup_b = [carve([128, 2, 128], BF16) for _ in range(2)]; b_Sub = [Buf("Sub0"), Buf("Sub1")]
    MSET("dve", Sst[0], 0.0, w=[b_S[0]])
    MSET("dve", Sst[1], 0.0, w=[b_S[1]])
    u_t = carve([128, 512]); b_u = Buf("u")
    ex_t = carve([128, 512]); b_ex = Buf("ex")
    est = carve([128, 256]); b_est = Buf("est")
    kst = carve([128, 256], BF16); b_kst = Buf("kst")
    vgb = carve([128, 512], BF16); b_vgb = Buf("vgb")
    dec = carve([128, 2, 2]); b_dec = Buf("dec")
    vaug = [carve([128, 4, 129], BF16) for _ in range(2)]; b_vaug = [Buf("va0"), Buf("va1")]
    kTs = [carve([128, 4, 128], BF16) for _ in range(2)]; b_kTs = [Buf("kT0"), Buf("kT1")]
    for i in range(2):
        MSET("dve", vaug[i], 1.0, w=[b_vaug[i]])

    def gla_state(d, ksrc, ksrc_b, ucols, t, store, ubank2=5):
        tS = tri[1] if d == 0 else tri[3]
        MM(bank(6)[:, 0:256], tS[:], u_t[:, ucols], r=[b_tris[1], b_tris[3], b_u], w=[PB[6]])
        for p in range(2):
            MM(bank(6)[:, 256 + 2 * p:258 + 2 * p], u_t[:, ucols.start + p * 128:ucols.start + (p + 1) * 128], cind[:],
               r=[b_u, b_tri2], w=[PB[6]])
        ACT(est, bank(6)[:, 0:256], AF.Exp, r=[PB[6]], w=[b_est])
        ACT(dec.rearrange("p a b -> p (a b)"), bank(6)[:, 256:260], AF.Exp, r=[PB[6]], w=[b_dec])
        TT("dve", kst, ksrc, est, ALU.mult, r=[ksrc_b, b_est], w=[b_kst])
        updb = [bank(7).rearrange("p (a c) -> p a c", a=4), bank(ubank2).rearrange("p (a c) -> p a c", a=4)]
        ubuf = [PB[7], PB[ubank2]]
        for n in range(2):
            for h in range(4):
                MM(updb[n][64 * (h % 2):64 * (h % 2) + 64, h // 2, :], kst[64 * n:64 * n + 64, h * 64:(h + 1) * 64],
                   vgb[64 * n:64 * n + 64, h * 128:(h + 1) * 128], r=[b_kst, b_vgb], w=[ubuf[n]])
        order = (0, 1) if d == 0 else (1, 0)
        for n in order:
            ch = 2 * t + n
            if d == 1 and store:
                CP("act", sdn_t[t % 2][:, n, :, :], Sst[1], r=[b_S[1]], w=[b_sdn[t % 2]])
            if d == 0:
                CP("act", Sup_b[n], Sst[0], r=[b_S[0]], w=[b_Sub[n]])
            for p in range(2):
                STT("dve", Sst[d][:, p, :], Sst[d][:, p, :], dec[:, p, n:n + 1], updb[n][:, p, :], ALU.mult, ALU.add,
                    r=[b_S[d], b_dec, ubuf[n]], w=[b_S[d]])

    def fe_x(t):
        i = cnt["x"] % 2
        DMA("sp", xts[i], x[t * 128:(t + 1) * 128, :], w=[b_xt[i]])
        return front_end(xts[i], 0, 0, rd=[b_xt[i]])

    def ph1_proj(t, h_ap, h_b):
        for k in range(8):
            MM(bank(1)[:, 0:256], h_ap[:, k, :], Wall[:, k, KG:KG + 256], start=(k == 0), stop=(k == 7), r=[h_b, b_Wall], w=[PB[1]])
        for k in range(8):
            MM(bank(1)[:, 256:512], h_ap[:, k, :], Wall[:, k, ZDN:ZDN + 256], start=(k == 0), stop=False, r=[h_b, b_Wall], w=[PB[1]])
        MM(bank(1)[:, 256:512], onesK[:], decbK[:, 256:512], start=False, stop=True, r=[b_c0, b_decbb], w=[PB[1]])
        proj_tok(2, h_ap, h_b, VG)
        proj_tok(3, h_ap, h_b, VD)
        for c in range(4):
            for k in range(8):
                MM(bank(4)[:, c * 128:(c + 1) * 128], Wall[:, k, KD + c * 128:KD + (c + 1) * 128], h_ap[:, k, :],
                   start=(k == 0), stop=(k == 7), r=[h_b, b_Wall], w=[PB[4]])

    def ph1_rest(t):
        own = t < NQT
        ACT(ex_t[:, 0:256], bank(1)[:, 256:512], AF.Exp, scale=-1.0, r=[PB[1]], w=[b_ex])
        ACT(u_t[:, 256:512], ex_t[:, 0:256], AF.Ln, bias=1.0, r=[b_ex], w=[b_u])
        CP("act", vgb, bank(2), r=[PB[2]], w=[b_vgb])
        vi = (t % 2)
        CP("dve", vaug[vi][:, :, 0:128], bank(3).rearrange("p (a b) -> p a b", a=4), r=[PB[3]], w=[b_vaug[vi]])
        DMA("sp", Vd[:, :, t, :].rearrange("h p c -> p h c"), vaug[vi], r=[b_vaug[vi]], w=[dVd])
        CP("act", kTs[vi], bank(4).rearrange("p (a b) -> p a b", a=4), r=[PB[4]], w=[b_kTs[vi]])
        DMA("sp", KdT[:, :, t * 128:(t + 1) * 128].rearrange("h p c -> p h c"), kTs[vi], r=[b_kTs[vi]], w=[dKdT])
        gla_state(1, bank(1)[:, 0:256], PB[1], slice(256, 512), t, own)
        if own:
            DMA("sp", SdnD[2 * t:2 * t + 2].rearrange("n p c -> p n c"),
                sdn_t[t % 2].rearrange("p n a b -> p n (a b)"), r=[b_sdn[t % 2]], w=[dSdn])

    tl1 = list(range(NT - 1, NT - 1 - nt1, -1)) if phases >= 1 else []
    if tl1:
        hcur = fe_x(tl1[0])
    for ii, t in enumerate(tl1):
        ph1_proj(t, *hcur)
        hnext = fe_x(tl1[ii + 1]) if ii + 1 < len(tl1) else None
        ph1_rest(t)
        hcur = hnext

    qk_s = carve([128, 512]); b_qk = Buf("qk")
    sr_s = carve([128, 512]); b_sr = Buf("sr")
    sr_t = carve([128, 512]); b_srt = Buf("srt")
    Eb = carve([128, 512]); b_Eb = Buf("Eb")
    Enb = carve([128, 512]); b_Enb = Buf("Enb")
    qkin = carve([128, 2, 2, 256], BF16); b_qkin = Buf("qkin")
    qkT = carve([128, 8, 128], BF16); b_qkT = Buf("qkT")
    attm = [carve([128, 4, 128], BF16) for _ in range(2)]; b_attm = [Buf("attm0"), Buf("attm1")]
    osq = carve([128, 512]); b_osq = Buf("osq")
    ost = carve([128, 8]); b_ost = Buf("ost")
    oab = carve([128, 512], BF16); b_oab = Buf("oab")
    oaf = carve([128, 512]); b_oaf = Buf("oaf")
    qTs = [carve([128, 4, 128], BF16) for _ in range(2)]; b_qTs = [Buf("qT0"), Buf("qT1")]
    oaTt = [carve([128, 4, 128], BF16) for _ in range(2)]; b_oaTt = [Buf("oaTt0"), Buf("oaTt1")]
    assert apos[0] <= 102 * KB, apos[0]
    TT("dve", sr_t, gnwb[:], gnwb[:], ALU.mult, r=[b_cst], w=[b_srt])
    def ph2_fe(t):
        i = cnt["x"] % 2
        DMA("sp", xts[i], x[t * 128:(t + 1) * 128, :], w=[b_xt[i]])
        DMA("sp", sdn_t[t % 2].rearrange("p n a b -> p n (a b)"), SdnD[2 * t:2 * t + 2].rearrange("n p c -> p n c"),
            r=[dSdn], w=[b_sdn[t % 2]])
        return front_end(xts[i], 0, 0, rd=[b_xt[i]])

    def ph2_proj(t, h_ap, h_b):
        proj_tok(1, h_ap, h_b, QG)
        for k in range(8):
            MM(bank(2), h_ap[:, k, :], Wall[:, k, ZUP:ZUP + 512], start=(k == 0), stop=False, r=[h_b, b_Wall], w=[PB[2]])
        MM(bank(2), onesK[:], decbK[:], start=False, stop=True, r=[b_c0, b_decbb], w=[PB[2]])
        proj_tok(3, h_ap, h_b, VG)
        proj_tok(4, h_ap, h_b, RG)
        for c in range(4):
            for k in range(8):
                MM(bank(5)[:, c * 128:(c + 1) * 128], Wall[:, k, QD + c * 128:QD + (c + 1) * 128], h_ap[:, k, :],
                   start=(k == 0), stop=(k == 7), r=[h_b, b_Wall], w=[PB[5]])

    def ph2_rest(t):
        CP("act", qk_s, bank(1), r=[PB[1]], w=[b_qk])
        ACT(ex_t, bank(2), AF.Exp, scale=-1.0, r=[PB[2]], w=[b_ex])
        ACT(u_t, ex_t, AF.Ln, bias=1.0, r=[b_ex], w=[b_u])
        CP("act", vgb, bank(3), r=[PB[3]], w=[b_vgb])
        ACT(sr_s, bank(4), AF.Exp, scale=-1.0, r=[PB[4]], w=[b_sr])
        TS("dve", sr_s, sr_s, 1.0, None, ALU.add, r=[b_sr], w=[b_sr])
        S.op("dve", lambda e: e.reciprocal(sr_s, sr_s), [b_sr], [b_sr])
        TT("dve", sr_s, sr_s, bank(4), ALU.mult, r=[b_sr, PB[4]], w=[b_sr])
        TT("dve", sr_t, sr_s, gnwb[:], ALU.mult, r=[b_sr, b_cst], w=[b_srt])
        qi = t % 2
        CP("act", qTs[qi], bank(5).rearrange("p (a b) -> p a b", a=4), r=[PB[5]], w=[b_qTs[qi]])
        DMA("sp", QdT[:, :, t * 128:(t + 1) * 128].rearrange("h p c -> p h c"), qTs[qi], r=[b_qTs[qi]], w=[dQdT])
        MM(bank(1)[:, 0:256], tri[0][:], u_t[:, 0:256], r=[b_tris[0], b_u], w=[PB[1]])
        MM(bank(1)[:, 256:512], tri[2][:], u_t[:, 256:512], r=[b_tris[2], b_u], w=[PB[1]])
        ACT(Eb, bank(1), AF.Exp, r=[PB[1]], w=[b_Eb])
        ACT(Enb, bank(1), AF.Exp, scale=-1.0, r=[PB[1]], w=[b_Enb])
        for d in range(2):
            STT("dve", qkin[:, d, 0, :], qk_s[:, 0:256], 0.125, Eb[:, d * 256:(d + 1) * 256], ALU.mult, ALU.mult,
                r=[b_qk, b_Eb], w=[b_qkin])
            TT("dve", qkin[:, d, 1, :], qk_s[:, 256:512], Enb[:, d * 256:(d + 1) * 256], ALU.mult, r=[b_qk, b_Enb], w=[b_qkin])
        tb = bankb(2)
        for d in range(2):
            for qk in range(2):
                for p in range(2):
                    j = d * 4 + qk * 2 + p
                    TR(tb[:, j * 128:(j + 1) * 128], qkin[:, d, qk, p * 128:(p + 1) * 128], identb[:], r=[b_qkin, b_identb], w=[PB[2]])
        CP("act", qkT, tb[:, 0:1024].rearrange("p (a b) -> p a b", a=8), r=[PB[2]], w=[b_qkT])
        for d in range(2):
            for h in range(4):
                p, o = h // 2, 64 * (h % 2)
                MM(bank(3 + h % 2)[:, (d * 2 + p) * 128:(d * 2 + p + 1) * 128], qkT[o:o + 64, d * 4 + 2 + p, :],
                   qkT[o:o + 64, d * 4 + p, :], r=[b_qkT], w=[PB[3 + h % 2]])
        for d in range(2):
            for par in range(2):
                TT("dve", attm[d][:, par::2, :], bank(3 + par)[:, d * 256:(d + 1) * 256].rearrange("p (a b) -> p a b", a=2),
                   maskt[d][:].unsqueeze(1).to_broadcast([128, 2, 128]), ALU.mult, r=[PB[3 + par], b_mask[d]], w=[b_attm[d]])
        gla_state(0, qk_s[:, 256:512], b_qk, slice(0, 256), t, False)
        ob5 = bank(5).rearrange("p (a b) -> p a b", a=4)
        for h in range(4):
            p, o = h // 2, 64 * (h % 2)
            MM(ob5[:, h, :], attm[0][:, h, :], vgb[:, h * 128:(h + 1) * 128], start=True, stop=False,
               r=[b_attm[0], b_vgb], w=[PB[5]])
            MM(ob5[:, h, :], attm[1][:, h, :], vgb[:, h * 128:(h + 1) * 128], start=False, stop=False,
               r=[b_attm[1], b_vgb], w=[PB[5]])
            for n in range(2):
                MM(ob5[64 * n:64 * n + 64, h, :], qkT[o:o + 64, p, 64 * n:64 * n + 64], Sup_b[n][o:o + 64, p, :],
                   start=False, stop=False, r=[b_qkT, b_Sub[n]], w=[PB[5]])
                MM(ob5[64 * n:64 * n + 64, h, :], qkT[o:o + 64, 4 + p, 64 * n:64 * n + 64],
                   sdn_t[t % 2][o:o + 64, n, p, :], start=False, stop=True, r=[b_qkT, b_sdn[t % 2]], w=[PB[5]])
        CP("act", oaf, bank(5), r=[PB[5]], w=[b_oaf])
        TT("dve", osq, oaf, oaf, ALU.mult, r=[b_oaf], w=[b_osq])
        S.op("dve", lambda e: e.tensor_reduce(ost[:, 0:4], osq.rearrange("p (a b) -> p a b", a=4), AX.X, ALU.add),
             [b_osq], [b_ost])
        TS("dve", ost[:, 4:8], ost[:, 0:4], 1.0 / 128, EPS, ALU.mult, ALU.add, r=[b_ost], w=[b_ost])
        ACT(ost[:, 4:8], ost[:, 4:8], AF.Ln, r=[b_ost], w=[b_ost])
        ACT(ost[:, 4:8], ost[:, 4:8], AF.Exp, scale=-0.5, r=[b_ost], w=[b_ost])
        TT("dve", oaf.rearrange("p (a b) -> p a b", a=4), oaf.rearrange("p (a b) -> p a b", a=4),
           ost[:, 4:8].unsqueeze(2).to_broadcast([128, 4, 128]), ALU.mult, r=[b_oaf, b_ost], w=[b_oaf])
        TT("dve", oab, oaf, sr_t, ALU.mult, r=[b_oaf, b_srt], w=[b_oab])
        tb = bankb(1)
        for h in range(4):
            TR(tb[:, h * 128:(h + 1) * 128], oab[:, h * 128:(h + 1) * 128], identb[:], r=[b_oab, b_identb], w=[PB[1]])
        CP("act", oaTt[t % 2], tb[:, 0:512].rearrange("p (a b) -> p a b", a=4), r=[PB[1]], w=[b_oaTt[t % 2]])
        DMA("sp", CatT[0:4, :, t * 128:(t + 1) * 128].rearrange("k p c -> p k c"), oaTt[t % 2], r=[b_oaTt[t % 2]], w=[dCat])


    tl2 = list(range(nt2)) if phases >= 2 else []
    if tl2:
        hcur = ph2_fe(tl2[0])
    for ii, t in enumerate(tl2):
        ph2_proj(t, *hcur)
        hnext = ph2_fe(tl2[ii + 1]) if ii + 1 < len(tl2) else None
        ph2_rest(t)
        hcur = hnext

    S.barrier()
    apos[0] = a0
    KTh = [carve([128, L], BF16) for _ in range(2)]; b_KTh = [Buf("KTh0"), Buf("KTh1")]
    Vh = [carve([128, NT, 129], BF16) for _ in range(2)]; b_Vh = [Buf("Vh0"), Buf("Vh1")]
    QTm = [[carve([128, TQ], BF16) for _ in range(2)] for _ in range(2)]
    b_QTh = [Buf("QTh0"), Buf("QTh1")]
    PT = [carve([128, 2, 512], BF16) for _ in range(3)]; b_PT = [Buf("PT%d" % i) for i in range(3)]
    sbias = [carve([128, 2, 512]) for _ in range(2)]; b_sbias = [Buf("sbias0"), Buf("sbias1")]
    ep = carve([128, 8]); b_ep = Buf("ep")
    ot0 = carve([128, 128]); b_ot0 = Buf("ot0")
    ob1 = carve([128, 128]); b_ob1 = Buf("ob1")
    obj = carve([128, 128]); b_obj = Buf("obj")
    obn = carve([128, 128], BF16); b_obn = Buf("obn")
    obTt = [carve([128, 128], BF16) for _ in range(2)]; b_obTt = [Buf("obTt0"), Buf("obTt1")]
    ocnt = [0]
    Osb = carve([128, 3, 512]); b_Osb = Buf("Osb")
    assert apos[0] <= ARENA, apos[0]
    bS = [Buf("S0"), Buf("S1")]
    bO = [Buf("O0"), Buf("O1"), Buf("O2")]
    def oacc(m, qt):
        idx = m * 4 + qt
        return bank(4 + idx // 3)[:, (idx % 3) * 129:(idx % 3) * 129 + 129], bO[idx // 3]

    if phases >= 3:
        def load_head(h):
            s = h % 2
            DMA("sp", KTh[s], KdT[h], r=[dKdT], w=[b_KTh[s]])
            DMA("sp", Vh[s], Vd[h], r=[dVd], w=[b_Vh[s]])
            DMA("sp", QTm[s][0][0:64, :], QdT[h, 0:64, :], r=[dQdT], w=[b_QTh[s]])
            DMA("sp", QTm[s][1][64:128, :], QdT[h, 64:128, :], r=[dQdT], w=[b_QTh[s]])
        for s_ in range(2):
            MSET("dve", QTm[s_][0][64:128, :], 0.0, w=[Buf("qz")])
            MSET("dve", QTm[s_][1][0:64, :], 0.0, w=[Buf("qz")])
        S.barrier()
        load_head(0)
        pcnt = 0
        scnt = 0
        ngrp = (nt2 * 128 + 511) // 512
        glist = [(h, g) for h in range(nheads) for g in range(ngrp)]

        def smm_hg(h, g, kb):
            s = h % 2
            nq = min(512, nt2 * 128 - g * 512)
            sl = kb % 2
            for m in range(2):
                MM(bank(2 * sl + m)[:, 0:nq], KTh[s][:, kb * 128:(kb + 1) * 128],
                   QTm[s][m][:, g * 512:g * 512 + nq], r=[b_KTh[s], b_QTh[s]], w=[bS[sl]])
        for gi, (h, g) in enumerate(glist):
            s = h % 2
            if g == 0 and h + 1 < nheads:
                load_head(h + 1)
            if True:
                nq = min(512, nt2 * 128 - g * 512)
                nqt = nq // 128
                if gi == 0:
                    smm_hg(h, g, 0)
                    smm_hg(h, g, 1)
                for b3 in range(3):
                    MM(bank(4 + b3), zerob[:, 0:128], zerob[:], r=[b_zero], w=[bO[b3]])

                def smm(kb, h=h, g=g):
                    smm_hg(h, g, kb)
                for kb in range(NT):
                    sl = kb % 2
                    pt = PT[pcnt % 3]; bpt = b_PT[pcnt % 3]
                    pcnt += 1
                    sv = ps[:, sl * 1024:sl * 1024 + 1024].rearrange("p (a b) -> p a b", a=2)[:, :, 0:nq]
                    e = kb - 4 * g
                    uni = (e >= 5) or (e <= -2)
                    if uni:
                        col = strips[:, h, 0:1] if e > 0 else strips[:, h, 10 * 128:10 * 128 + 1]
                        ACT(pt[:, :, 0:nq], sv, AF.Exp, scale=0.125, bias=col, r=[bS[sl], b_strips[h]], w=[bpt])
                    else:
                        sbv = sbias[scnt % 2]; bsb = b_sbias[scnt % 2]
                        scnt += 1
                        win = strips[:, h, (5 - e) * 128:(5 - e) * 128 + nq]
                        STT("dve", sbv[:, :, 0:nq], sv, 0.125, win.unsqueeze(1).to_broadcast([128, 2, nq]), ALU.mult, ALU.add,
                            r=[bS[sl], b_strips[h]], w=[bsb])
                        ACT(pt[:, :, 0:nq], sbv[:, :, 0:nq], AF.Exp, r=[bsb], w=[bpt])
                    if kb + 2 < NT:
                        smm(kb + 2)
                    for m in range(2):
                        for qt in range(nqt):
                            oap, ob = oacc(m, qt)
                            MM(oap, pt[:, m, qt * 128:(qt + 1) * 128], Vh[s][:, kb, :], start=False, stop=(kb == NT - 1),
                               r=[bpt, b_Vh[s]], w=[ob], skip_group_check=True)
                if gi + 1 < len(glist):
                    smm_hg(glist[gi + 1][0], glist[gi + 1][1], 0)
                    smm_hg(glist[gi + 1][0], glist[gi + 1][1], 1)
                for b3 in range(3):
                    CP("dve", Osb[:, b3, :], bank(4 + b3), r=[bO[b3]], w=[b_Osb])

                def osb(m, qt):
                    idx = m * 4 + qt
                    return Osb[:, idx // 3, (idx % 3) * 129:(idx % 3) * 129 + 129], b_Osb
                for qt in range(nqt):
                    o0, bo0 = osb(0, qt)
                    o1, bo1 = osb(1, qt)
                    S.op("dve", lambda e, o0=o0: e.reciprocal(ep[:, 0:1], o0[:, 128:129]), [bo0], [b_ep])
                    S.op("dve", lambda e, o1=o1: e.reciprocal(ep[:, 1:2], o1[:, 128:129]), [bo1], [b_ep])
                    TT("dve", ep[:, 2:3], ep[:, 1:2], lamc[:, 1:2], ALU.mult, r=[b_ep, b_lam], w=[b_ep])
                    TS("dve", ot0, o0[:, 0:128], ep[:, 0:1], None, ALU.mult, r=[bo0, b_ep], w=[b_ot0])
                    STT("dve", ob1, o1[:, 0:128], ep[:, 2:3], ot0, ALU.mult, ALU.add, r=[bo1, b_ep, b_ot0], w=[b_ob1])
                    TT("dve", obj, ob1, ob1, ALU.mult, r=[b_ob1], w=[b_obj])
                    S.op("dve", lambda e: e.tensor_reduce(ep[:, 3:4], obj, AX.X, ALU.add), [b_obj], [b_ep])
                    TS("dve", ep[:, 4:5], ep[:, 3:4], 1.0 / 128, EPS, ALU.mult, ALU.add, r=[b_ep], w=[b_ep])
                    ACT(ep[:, 4:5], ep[:, 4:5], AF.Ln, r=[b_ep], w=[b_ep])
                    ACT(ep[:, 4:5], ep[:, 4:5], AF.Exp, scale=-0.5, r=[b_ep], w=[b_ep])
                    STT("dve", obn, ob1, ep[:, 4:5], dnwb[:], ALU.mult, ALU.mult, r=[b_ob1, b_ep, b_dnw], w=[b_obn])
                    tb = bankb(7)
                    TR(tb[:, 0:128], obn, identb[:], r=[b_obn, b_identb], w=[PB[7]])
                    tok = g * 512 + qt * 128
                    oi = ocnt[0] % 2
                    ocnt[0] += 1
                    CP("act", obTt[oi], tb[:, 0:128], r=[PB[7]], w=[b_obTt[oi]])
                    DMA("sp", CatT[4 + h, :, tok:tok + 128], obTt[oi], r=[b_obTt[oi]], w=[dCat])

    S.barrier()
    apos[0] = a0
    Wout = carve([128, 8, D], BF16); b_Wout = Buf("Wout")
    x1s = [carve([128, D]) for _ in range(2)]; b_x1 = [Buf("x1a"), Buf("x1b")]
    xts4 = [carve([128, D]) for _ in range(2)]; b_xt4 = [Buf("xt4a"), Buf("xt4b")]
    junk = carve([128, D], BF16); b_junk = Buf("junk4")
    xn = carve([128, D], BF16); b_xn = Buf("xn4")
    st2 = carve([128, 2]); b_st = Buf("st4")
    hT = [carve([128, 8, 128], BF16) for _ in range(2)]; b_hT = [Buf("h2T0"), Buf("h2T1")]
    zcol = carve([128, 8, 2], BF16); b_zcol = Buf("zcol")
    catt = [carve([128, 8, 128], BF16) for _ in range(2)]; b_catt = [Buf("catt0"), Buf("catt1")]
    if phases >= 4:
        w_out_v = w_out.rearrange("(k p) n -> p k n", p=128)
        S.barrier()
        for hh in range(2):
            DMA("pool", Wout[:, :, hh * 512:(hh + 1) * 512], w_out_v[:, :, hh * 512:(hh + 1) * 512],
                w=[Buf("wo%d" % hh)] if hh == 0 else [b_Wout])
        MSET("dve", zcol, 0.0, w=[b_zcol])
        DMA("sp", H2T[:, :, 0:2].rearrange("k p c -> p k c"), zcol, r=[b_zcol], w=[dH2T])
        S.barrier()
        def ph4_mm(t):
            i = t % 2
            DMA("sp", xts4[i], x[t * 128:(t + 1) * 128, :], w=[b_xt4[i]])
            DMA("sp", catt[i], CatT[:, :, t * 128:(t + 1) * 128].rearrange("k p c -> p k c"), r=[dCat], w=[b_catt[i]])
            for hh in range(2):
                bk = 1 + 2 * i + hh
                for kc in range(8):
                    MM(bank(bk), catt[i][:, kc, :], Wout[:, kc, hh * 512:(hh + 1) * 512],
                       start=(kc == 0), stop=(kc == 7), r=[b_catt[i], b_Wout], w=[PB[bk]])

        def ph4_a(t):
            i = t % 2
            for hh in range(2):
                bk = 1 + 2 * i + hh
                cs = slice(hh * 512, (hh + 1) * 512)
                TT("dve", x1s[i][:, cs], bank(bk), GA[:, cs], ALU.mult, r=[PB[bk], b_GA], w=[b_x1[i]])
            TT("dve", x1s[i], x1s[i], xts4[i], ALU.add, r=[b_x1[i], b_xt4[i]], w=[b_x1[i]])
            if t < NOWN:
                DMA("sp", X1[t * 128:(t + 1) * 128, :], x1s[i], r=[b_x1[i]], w=[dX1])

        def ph4_b(t):
            i = t % 2
            h_ap, h_b = front_end(x1s[i], 16, 0, rd=[b_x1[i]])
            DMA("sp", H2T[:, :, 1 + t * 128:1 + (t + 1) * 128].rearrange("k p c -> p k c"), h_ap, r=[h_b], w=[dH2T])

        ph4_mm(0)
        if nt2 > 1:
            ph4_mm(1)
        ph4_a(0)
        for t in range(nt2):
            if t + 1 < nt2:
                ph4_a(t + 1)
            ph4_b(t)
            if t + 2 < nt2:
                ph4_mm(t + 2)

    S.barrier()
    apos[0] = 0
    if phases >= 5:
        Wup = carve([128, 8, 2 * DFF], BF16); b_Wup = Buf("Wup")
        w_up_v = w_up.rearrange("(k p) n -> p k n", p=128)
        for c in range(11):
            DMA("pool", Wup[:, :, c * 512:(c + 1) * 512], w_up_v[:, :, c * 512:(c + 1) * 512], w=[Buf("wu%d" % c)])
        Wdn = carve([128, NFC, D], BF16); b_Wdn = Buf("Wdn")

        def wdn(j):
            return Wdn[:, j, :]
        w_dn_v = w_down.rearrange("(k p) n -> p k n", p=128)
        for j in range(NFC):
            for hh in range(2):
                DMA("pool", wdn(j)[:, hh * 512:(hh + 1) * 512], w_dn_v[:, j, hh * 512:(hh + 1) * 512], w=[Buf("wd")])
        DMA("sp", cw[:], cwp, w=[Buf("cwx")])
        DMA("sp", cb[:], cbp, w=[b_cw])
        h2g = [carve([128, 8, 514], BF16)]; b_h2g = [Buf("h2g0")]
        gT = carve([128, NFC, 512], BF16); b_gT = Buf("gT")
        cc = [carve([128, 512]) for i in range(2)]; b_cc = [Buf("cc0"), Buf("cc1")]
        sg = carve([128, 512]); b_sg = Buf("sg")
        yt = carve([128, D]); b_yt = Buf("yt")
        x1l = carve([128, D]); b_x1l = Buf("x1l")
        fj = carve([128, D], BF16); b_fj = Buf("fj")
        fst = carve([128, 2]); b_fst = Buf("fst")
        S.barrier()
        bU = [Buf("U0"), Buf("U1"), Buf("U2")]
        ucnt = 0
        for g in range(8):
            DMA("sp", h2g[0], H2T[:, :, g * 512:g * 512 + 514].rearrange("k p c -> p k c"), r=[dH2T], w=[b_h2g[0]])
            for j in range(NFC):
                for which in range(2):
                    c = j + which * NFC
                    ui = ucnt % 3
                    ucnt += 1
                    U = ps[:, ui * 1024:ui * 1024 + 1024]
                    for k in range(8):
                        MM(U[:, 0:512], Wup[:, k, c * 128:(c + 1) * 128], h2g[0][:, k, 0:512], start=(k == 0), stop=(k == 7),
                           r=[b_h2g[0], b_Wup], w=[bU[ui]])
                    for k in range(8):
                        MM(U[:, 512:514], Wup[:, k, c * 128:(c + 1) * 128], h2g[0][:, k, 512:514], start=(k == 0), stop=(k == 7),
                           r=[b_h2g[0], b_Wup], w=[bU[ui]])
                    ACT(cc[which], U[:, 1:513], AF.Identity, scale=cw[:, 44 + c:44 + c + 1], bias=cb[:, c:c + 1],
                        r=[bU[ui], b_cw], w=[b_cc[which]])
                    STT("dve", cc[which], U[:, 0:512], cw[:, c:c + 1], cc[which], ALU.mult, ALU.add,
                        r=[bU[ui], b_cw, b_cc[which]], w=[b_cc[which]])
                    STT("dve", cc[which], U[:, 2:514], cw[:, 88 + c:88 + c + 1], cc[which], ALU.mult, ALU.add,
                        r=[bU[ui], b_cw, b_cc[which]], w=[b_cc[which]])
                ACT(sg, cc[0], AF.Silu, r=[b_cc[0]], w=[b_sg])
                TT("dve", gT[:, j, :], sg, cc[1], ALU.mult, r=[b_sg, b_cc[1]], w=[b_gT])
            for qt in range(4):
                t = g * 4 + qt
                DMA("sp", x1l[:], X1[t * 128:(t + 1) * 128, :], r=[dX1], w=[b_x1l])
                for hh in range(2):
                    for j in range(NFC):
                        MM(bank(6 + hh), gT[:, j, qt * 128:(qt + 1) * 128], wdn(j)[:, hh * 512:(hh + 1) * 512],
                           start=(j == 0), stop=(j == NFC - 1), r=[b_gT, b_Wdn], w=[PB[6 + hh]])
                for hh in range(2):
                    cs = slice(hh * 512, (hh + 1) * 512)
                    TT("dve", yt[:, cs], bank(6 + hh), GF[:, cs], ALU.mult, r=[PB[6 + hh], b_GF], w=[b_yt])
                TT("dve", yt[:], yt[:], x1l[:], ALU.add, r=[b_yt, b_x1l], w=[b_yt])
                ACT(fj[:], yt[:], AF.Square, accum_out=fst[:, 0:1], r=[b_yt], w=[b_fj, b_fst])
                TS("dve", fst[:, 1:2], fst[:, 0:1], 1.0 / D, EPS, ALU.mult, ALU.add, r=[b_fst], w=[b_fst])
                ACT(fst[:, 1:2], fst[:, 1:2], AF.Ln, r=[b_fst], w=[b_fst])
                ACT(fst[:, 1:2], fst[:, 1:2], AF.Exp, scale=-0.5, r=[b_fst], w=[b_fst])
                STT("dve", yt[:], yt[:], fst[:, 1:2], WF[:], ALU.mult, ALU.mult, r=[b_yt, b_fst, b_WF], w=[b_yt])
                DMA("sp", y[t * 128:(t + 1) * 128, :], yt[:], r=[b_yt], w=[Buf("yout")])
    elif dbg:
        pass

    S.emit(nc, stack)
    stack.close()
    nc._sched_stats = S.stats
    return nc


def _consts():
    j = np.arange(128)[:, None]
    c = np.arange(128)[None, :]
    same = (j // 64) == (c // 64)
    s = -1.0 / 16.0
    tri = [
        np.where(same & (j <= c), s, 0.0),
        np.where(same & (j > c), s, 0.0),
        np.where(same & (j >= c), s, 0.0),
        np.where(same & (j < c), s, 0.0),
    ]
    cind = np.stack([np.where(np.arange(128) < 64, s, 0.0), np.where(np.arange(128) >= 64, s, 0.0)], axis=1)
    return [t.astype(np.float32) for t in tri], cind.astype(np.float32), same


def make_in_maps(inputs):
    f = lambda a: np.ascontiguousarray(np.asarray(a))
    x = f(inputs["x"]); c = f(inputs["c"]); positions = f(inputs["positions"])
    w_in = f(inputs["w_in"])[0]
    tri, cind, same = _consts()
    s_idx = np.arange(128)[:, None]
    c_idx = np.arange(128)[None, :]
    table = f(inputs["rel_bias_table"])
    ia = [a for (_, a, _) in T5_BREAKS]
    ib = [b for (_, _, b) in T5_BREAKS]
    tabA = np.ascontiguousarray(table[ia, :].T)
    tabB = np.ascontiguousarray(table[ib, :].T)
    tab0 = np.ascontiguousarray(table[T5_B0:T5_B0 + 1, :].T)
    conv_w = f(inputs["conv_w"])[0]
    conv_b = f(inputs["conv_b"])[0]
    maps = []
    for core in range(8):
        b, hf = core // 2, core % 2
        xs = x[b]; ps_ = positions[b]
        cwv = conv_w
        wi = w_in
        names = ("fwd", "bwd")
        if hf == 1:
            xs = xs[::-1]; ps_ = ps_[::-1]
            cwv = conv_w[::-1]
            wi = w_in.copy()
            wi[:, 1536:1552] = w_in[:, 1552:1568]
            wi[:, 1552:1568] = w_in[:, 1536:1552]
            names = ("bwd", "fwd")
        if hf == 0:
            m_up = same & (s_idx <= c_idx); m_dn = same & (s_idx > c_idx)
        else:
            m_up = same & (s_idx < c_idx); m_dn = same & (s_idx >= c_idx)
        m = {
            "x": np.ascontiguousarray(xs), "pos": np.ascontiguousarray(ps_).astype(np.int32),
            "ccol": np.ascontiguousarray(c[b].reshape(8, 128).T),
            "w_ada": f(inputs["w_ada"])[0], "b_ada": f(inputs["b_ada"]),
            "anw": f(inputs["attn_norm_w"]), "fnw": f(inputs["ffn_norm_w"]), "finw": f(inputs["final_norm_w"]),
            "w_in": np.ascontiguousarray(wi),
            "decw_up": f(inputs["gla_dec_w_" + names[0]])[0], "decw_dn": f(inputs["gla_dec_w_" + names[1]])[0],
            "decb_up": f(inputs["gla_dec_b_" + names[0]]), "decb_dn": f(inputs["gla_dec_b_" + names[1]]),
            "gnw": f(inputs["gla_norm_w"])[0],
            "lam0": f(inputs["diff_lambda_q1"])[0], "lam1": f(inputs["diff_lambda_k1"])[0],
            "lam2": f(inputs["diff_lambda_q2"])[0], "lam3": f(inputs["diff_lambda_k2"])[0],
            "dnw": f(inputs["diff_norm_w"])[0],
            "tabA": tabA, "tabB": tabB, "tab0": tab0,
            "w_out": f(inputs["w_out"])[0], "w_up": f(inputs["w_up"])[0],
            "cwp": np.ascontiguousarray(cwv.reshape(3, 44, 128).transpose(2, 0, 1).reshape(128, 132)),
            "cbp": np.ascontiguousarray(conv_b.reshape(44, 128).T),
            "w_down": f(inputs["w_down"])[0],
            "identf": np.eye(128, dtype=np.float32),
            "tri0": tri[0], "tri1": tri[1], "tri2": tri[2], "tri3": tri[3], "cind": cind,
            "mask_up": m_up.astype(np.float32), "mask_dn": m_dn.astype(np.float32),
        }
        maps.append({k: np.ascontiguousarray(v) for k, v in m.items()})
    return maps


_NC = {}


def kernel(**inputs):
    if "nc" not in _NC:
        _NC["nc"] = build_program()
    nc = _NC["nc"]
    maps = make_in_maps(inputs)
    res = run_bass_kernel_spmd(nc, maps, core_ids=list(range(8)))
    out = np.empty((4, L, D), np.float32)
    for core in range(8):
        b, hf = core // 2, core % 2
        yl = np.asarray(res.results[core]["y"])
        if hf == 0:
            out[b, :4096] = yl
        else:
            out[b, 4096:] = yl[::-1]
    return out
```

```python
import math
import numpy as np
import ml_dtypes
from contextlib import ExitStack
import concourse.bass as bass
import concourse.mybir as mybir
from concourse.bass_utils import run_bass_kernel_spmd

F32 = mybir.dt.float32
BF16 = mybir.dt.bfloat16
I32 = mybir.dt.int32
AF = mybir.ActivationFunctionType
ALU = mybir.AluOpType
AX = mybir.AxisListType

D = 1024
L = 8192
NT = 64
NOWN = 32
NQT = 33
TQ = NQT * 128
DFF = 2816
NFC = 22
EPS = 1e-6
QG, KG, ZUP, ZDN, VG, RG, QD, VD, KD = 0, 256, 512, 768, 1024, 1536, 2048, 2560, 3072
WALLC = 3584

def _t5_bucket_np(rel):
    half, max_exact = 16, 8
    ret = np.where(rel > 0, half, 0)
    n = np.abs(rel)
    nf = np.maximum(n, 1).astype(np.float32)
    lg = (np.log(nf / np.float32(max_exact)) / np.float32(math.log(128 / max_exact))
          * np.float32(half - max_exact)).astype(np.float32)
    large = max_exact + lg.astype(np.int32)
    large = np.minimum(large, half - 1)
    return ret + np.where(n < max_exact, n, large)


def _t5_breaks():
    rel = np.arange(-400, 401)
    b = _t5_bucket_np(rel)
    out = []
    for i in range(1, len(rel)):
        if b[i] != b[i - 1]:
            out.append((int(rel[i]), int(b[i]), int(b[i - 1])))
    return out, int(b[0])


T5_BREAKS, T5_B0 = _t5_breaks()
NBRK = len(T5_BREAKS)


class Buf:
    __slots__ = ("name", "w", "r")

    def __init__(self, name):
        self.name = name
        self.w = {}
        self.r = {}


class Op:
    __slots__ = ("eng", "fn", "idx", "deps", "dma")

    def __init__(self, eng, fn, idx):
        self.eng = eng
        self.fn = fn
        self.idx = idx
        self.deps = {}
        self.dma = None


CENG = ("pe", "act", "dve", "pool")
ALLE = ("pe", "act", "dve", "pool", "sp")


class Sched:
    def __init__(self, rings):
        self.ops = {e: [] for e in ALLE}
        self.rings = rings
        self.ring_pos = {q: 0 for q in rings}
        self.ring_val = {q: [0] * n for q, n in rings.items()}

    def _deps(self, eng, r, w, dma):
        deps = {}

        def add(k, v):
            if deps.get(k, 0) < v:
                deps[k] = v
        for b in r:
            for k, v in b.w.items():
                if (not dma) and k == eng and eng == "pe":
                    continue
                add(k, v)
        for b in w:
            for k, v in b.w.items():
                if (not dma) and k == eng:
                    continue
                add(k, v)
            for k, v in b.r.items():
                if (not dma) and k == eng:
                    continue
                add(k, v)
        return deps

    def op(self, eng, fn, r=(), w=(), after_prev=False):
        o = Op(eng, fn, len(self.ops[eng]) + 1)
        o.deps = self._deps(eng, r, w, False)
        if after_prev and o.idx > 1:
            o.deps[eng] = max(o.deps.get(eng, 0), o.idx - 1)
        for b in w:
            b.w = {eng: o.idx}
            b.r = {}
        for b in r:
            b.r[eng] = o.idx
        self.ops[eng].append(o)
        return o

    def dma(self, q, fn, r=(), w=(), disjoint=()):
        i = self.ring_pos[q]
        self.ring_pos[q] = (i + 1) % self.rings[q]
        prev = self.ring_val[q][i]
        self.ring_val[q][i] = prev + 16
        key = ("dma", q, i)
        o = Op(q, fn, len(self.ops[q]) + 1)
        o.deps = self._deps(q, r, w, True)
        if prev > 0:
            o.deps[key] = max(o.deps.get(key, 0), prev)
        o.dma = key
        for b in w:
            b.w = {key: prev + 16}
            b.r = {}
        for b in disjoint:
            b.w[key] = prev + 16
        for b in r:
            b.r[key] = prev + 16
        self.ops[q].append(o)
        return o

    def barrier(self):
        last = {e: len(self.ops[e]) for e in CENG}
        dl = {}
        for q, vals in self.ring_val.items():
            for i, v in enumerate(vals):
                if v > 0:
                    dl[("dma", q, i)] = v
        for e in ALLE:
            o = Op(e, lambda eng: eng.nop(), len(self.ops[e]) + 1)
            for k, v in last.items():
                if k != e and v > 0 and not self.ops[k][v - 1].dma:
                    o.deps[k] = v
                elif k != e and v > 0:
                    j = v
                    while j > 0 and self.ops[k][j - 1].dma:
                        j -= 1
                    if j > 0:
                        o.deps[k] = j
            o.deps.update(dl)
            self.ops[e].append(o)

    def emit(self, nc, stack):
        need = {e: set() for e in CENG}
        for e in ALLE:
            for o in self.ops[e]:
                for k, v in o.deps.items():
                    if isinstance(k, str):
                        need[k].add(v)
        ms = {e: {idx: i + 1 for i, idx in enumerate(sorted(need[e]))} for e in CENG}
        self.stats = {e: (len(self.ops[e]), len(ms.get(e, ()))) for e in ALLE}
        csem = {e: stack.enter_context(nc.semaphore("c_" + e)) for e in CENG}
        dsem = {}
        for q, n in self.rings.items():
            for i in range(n):
                dsem[("dma", q, i)] = stack.enter_context(nc.semaphore("d_%s_%d" % (q, i)))
        final = {}
        for q, vals in self.ring_val.items():
            for i, v in enumerate(vals):
                if v > 0:
                    final[("dma", q, i)] = v
        block = stack.enter_context(nc.Block())

        def run(ename):
            def body(eng):
                seen = {}
                for o in self.ops[ename]:
                    for k, v in o.deps.items():
                        if isinstance(k, str):
                            val = ms[k][v]
                            sem = csem[k]
                        else:
                            val = v
                            sem = dsem[k]
                        if seen.get(k, 0) >= val:
                            continue
                        eng.wait_ge(sem, val)
                        seen[k] = val
                    ins = o.fn(eng)
                    if o.dma is not None:
                        ins.then_inc(dsem[o.dma], 16)
                    elif ename in ms and o.idx in ms[ename]:
                        ins.then_inc(csem[ename], 1)
                if ename == "sp":
                    for k, v in final.items():
                        if seen.get(k, 0) < v:
                            eng.wait_ge(dsem[k], v)
            return body
        block.tensor(run("pe"))
        block.scalar(run("act"))
        block.vector(run("dve"))
        block.gpsimd(run("pool"))
        block.sync(run("sp"))


def build_program(phases=5, nt1=NT, nt2=NQT, nheads=4, dbg=False):
    nc = bass.Bass("TRN2", target_bir_lowering=False)
    S = Sched({"sp": 8, "pool": 4})
    stack = ExitStack()

    def din(name, shape, dt=F32):
        return nc.dram_tensor(name, list(shape), dt, kind="ExternalInput").ap()

    x = din("x", [L, D])
    pos = din("pos", [L], I32)
    ccol = din("ccol", [128, 8])
    w_ada = din("w_ada", [D, 6 * D])
    b_ada = din("b_ada", [1, 6 * D])
    anw = din("anw", [1, D])
    fnw = din("fnw", [1, D])
    finw = din("finw", [D])
    w_in = din("w_in", [D, 3104])
    decw = [din("decw_up", [16, 256]), din("decw_dn", [16, 256])]
    decb = [din("decb_up", [1, 256]), din("decb_dn", [1, 256])]
    gnw = din("gnw", [512])
    lamv = [din("lam%d" % i, [64]) for i in range(4)]
    dnw = din("dnw", [128])
    tabA = din("tabA", [4, NBRK])
    tabB = din("tabB", [4, NBRK])
    tab0 = din("tab0", [4, 1])
    w_out = din("w_out", [D, D])
    w_up = din("w_up", [D, 2 * DFF])
    cwp = din("cwp", [128, 3 * 44])
    cbp = din("cbp", [128, 44])
    w_down = din("w_down", [DFF, D])
    identf_d = din("identf", [128, 128])
    tri_d = [din("tri%d" % i, [128, 128]) for i in range(4)]
    cind_d = din("cind", [128, 2])
    mask_d = [din("mask_up", [128, 128]), din("mask_dn", [128, 128])]
    y = nc.dram_tensor("y", [NOWN * 128, D], F32, kind="ExternalOutput").ap()
    KdT = nc.dram_tensor("KdT", [4, 128, L], BF16, kind="Internal").ap()
    Vd = nc.dram_tensor("Vd", [4, 128, NT, 129], BF16, kind="Internal").ap()
    QdT = nc.dram_tensor("QdT", [4, 128, TQ], BF16, kind="Internal").ap()
    X1 = nc.dram_tensor("X1", [NOWN * 128, D], F32, kind="Internal").ap()
    H2T = nc.dram_tensor("H2T", [8, 128, TQ + 2], BF16, kind="Internal").ap()
    SdnD = nc.dram_tensor("SdnD", [2 * NQT, 128, 256], BF16, kind="Internal").ap()
    CatT = nc.dram_tensor("CatT", [8, 128, TQ], BF16, kind="Internal").ap()
    dKdT, dVd, dQdT, dX1, dH2T = Buf("KdT"), Buf("Vd"), Buf("QdT"), Buf("X1"), Buf("H2T")
    dSdn, dCat = Buf("SdnD"), Buf("CatT")
    _dscr = [dKdT, dVd, dQdT, dX1, dH2T, dSdn, dCat]

    def sb(name, shape, dt=F32):
        return stack.enter_context(nc.sbuf_tensor("s_" + name, list(shape), dt))

    ps = stack.enter_context(nc.psum_tensor("ps", [128, 4096], F32))
    PB = [Buf("pb%d" % i) for i in range(8)]

    def bank(i, n=1):
        return ps[:, i * 512:(i + n) * 512]

    def bankb(i):
        return ps[:, i * 512:(i + 1) * 512].bitcast(BF16)

    pe_mode = [None]

    def _mode(ap):
        sh = ap.shape
        k = int(sh[0]); m = int(np.prod(sh[1:]))
        rt = 32 if k <= 32 else (64 if k <= 64 else 128)
        ct = 32 if m <= 32 else (64 if m <= 64 else 128)
        return (rt, ct)

    def _switch(ap):
        md = _mode(ap)
        sw = pe_mode[0] is not None and md != pe_mode[0]
        pe_mode[0] = md
        return sw

    def MM(out, lhsT, rhs, start=True, stop=True, r=(), w=(), sync=False, **kw):
        sync = _switch(lhsT) or sync
        S.op("pe", lambda e: e.matmul(out, lhsT, rhs, start=start, stop=stop, **kw), r, w, after_prev=sync)

    def TR(out, in_, ident, r=(), w=()):
        S.op("pe", lambda e: e.transpose(out, in_, ident), r, w, after_prev=_switch(in_))

    def ACT(out, in_, func, r=(), w=(), **kw):
        S.op("act", lambda e: e.activation(out, in_, func, **kw), r, w)

    def TT(eng, out, in0, in1, op, r=(), w=()):
        S.op(eng, lambda e: e.tensor_tensor(out, in0, in1, op), r, w)

    def TS(eng, out, in0, s1, s2, op0, op1=None, r=(), w=(), **kw):
        if op1 is None:
            S.op(eng, lambda e: e.tensor_scalar(out, in0, s1, s2, op0, **kw), r, w)
        else:
            S.op(eng, lambda e: e.tensor_scalar(out, in0, s1, s2, op0, op1, **kw), r, w)

    def STT(eng, out, in0, sc, in1, op0, op1, r=(), w=(), **kw):
        S.op(eng, lambda e: e.scalar_tensor_tensor(out, in0, sc, in1, op0, op1, **kw), r, w)

    def CP(eng, out, in_, r=(), w=()):
        if eng == "act":
            S.op("act", lambda e: e.copy(out, in_), r, w)
        else:
            S.op(eng, lambda e: e.tensor_copy(out, in_), r, w)

    def MSET(eng, ap, val, w=()):
        S.op(eng, lambda e: e.memset(ap, val), (), w)

    DSCR = set(id(b) for b in _dscr)

    def DMA(q, out, in_, r=(), w=()):
        wn = [b for b in w if id(b) not in DSCR]
        wd = [b for b in w if id(b) in DSCR]
        S.dma(q, lambda e: e.dma_start(out=out, in_=in_), r, wn, wd)

    identf = sb("identf", [128, 128]); b_identf = Buf("identf")
    identb = sb("identb", [128, 128], BF16); b_identb = Buf("identb")
    tri = [sb("tri%d" % i, [128, 128]) for i in range(4)]; b_tri = Buf("tri")
    cind = sb("cind", [128, 2])
    maskt = [sb("mask%d" % i, [128, 128]) for i in range(2)]
    modcol = sb("modcol", [128, 32]); b_modcol = Buf("modcol")
    GA = sb("GA", [128, D]); b_GA = Buf("GA")
    GF = sb("GF", [128, D]); b_GF = Buf("GF")
    WF = sb("WF", [128, D]); b_WF = Buf("WF")
    gnwb = sb("gnwb", [128, 512]); b_cst = Buf("cst")
    dnwb = sb("dnwb", [128, 128])
    lamc = sb("lamc", [128, 4]); b_lam = Buf("lam")
    onesK = sb("onesK", [128, 128], BF16)
    decbK = sb("decbK", [128, 512], BF16)
    onesf = sb("onesf", [1, 128])
    b_decbb = Buf("decbb")
    zerob = sb("zerob", [128, 512], BF16); b_zero = Buf("zero")
    cw = sb("cw", [128, 3 * 44]); cb = sb("cb", [128, 44]); b_cw = Buf("cw")
    ARENA = 180 * 1024
    arena = sb("arena", [128, ARENA // 2], BF16)
    apos = [0]

    def carve(shape, dt=F32):
        n = int(np.prod(shape[1:])) * (4 if dt in (F32, I32) else 2)
        n = (n + 63) // 64 * 64
        a = apos[0]
        assert a + n <= ARENA, ("arena overflow", a, n)
        apos[0] = a + n
        v = arena[0:shape[0], a // 2:(a + n) // 2]
        if dt != BF16:
            v = v.bitcast(dt)
        v = v[:, 0:int(np.prod(shape[1:]))]
        if len(shape) == 3:
            v = v.rearrange("p (a b) -> p a b", a=shape[1])
        elif len(shape) == 4:
            v = v.rearrange("p (a b c) -> p a b c", a=shape[1], b=shape[2])
        return v

    b_c0 = Buf("c0")
    DMA("sp", identf[:], identf_d, w=[b_identf])
    b_tris = [Buf("tri%d" % i) for i in range(4)]
    for i in range(4):
        DMA("sp", tri[i][:], tri_d[i], w=[b_tris[i]])
    b_tri2 = Buf("tri2")
    DMA("sp", cind[:], cind_d, w=[b_tri2])
    b_mask = [Buf("mask0"), Buf("mask1")]
    for i in range(2):
        DMA("sp", maskt[i][:], mask_d[i], w=[b_mask[i]])
    DMA("sp", WF[:], finw.partition_broadcast(128), w=[b_WF])
    DMA("sp", gnwb[:], gnw.partition_broadcast(128), w=[b_cst])
    b_dnw = Buf("dnw")
    DMA("sp", dnwb[:], dnw.partition_broadcast(128), w=[b_dnw])
    CP("dve", identb[:], identf[:], r=[b_identf], w=[b_identb])
    MSET("dve", onesK[:], 0.0, w=[b_c0])
    MSET("dve", onesK[0:1, :], 1.0, w=[b_c0])
    MSET("dve", decbK[:], 0.0, w=[b_decbb])
    MSET("dve", onesf[:], 1.0, w=[b_c0])
    MSET("dve", zerob[:], 0.0, w=[b_zero])
    TS("dve", dnwb[:], dnwb[:], 0.8, None, ALU.mult, r=[b_dnw], w=[b_dnw])

    strips = carve([128, 4, 11 * 128]); b_strips = [Buf("strip%d" % h) for h in range(4)]
    a0 = apos[0]
    KB = 1024
    ccs = carve([128, 8]); b_ccs = Buf("ccs")
    tmp8 = carve([128, 8]); b_tmp8 = Buf("tmp8")
    silc = carve([128, 8]); b_silc = Buf("silc")
    DMA("sp", ccs, ccol, w=[b_ccs])
    ACT(tmp8, ccs, AF.Exp, scale=-1.0, r=[b_ccs], w=[b_tmp8])
    TS("dve", tmp8, tmp8, 1.0, None, ALU.add, r=[b_tmp8], w=[b_tmp8])
    S.op("dve", lambda e: e.reciprocal(tmp8, tmp8), [b_tmp8], [b_tmp8])
    TT("dve", silc, ccs, tmp8, ALU.mult, r=[b_ccs, b_tmp8], w=[b_silc])
    lv = carve([128, 4, 64]); b_lv = Buf("lv")
    for i in range(4):
        DMA("sp", lv[:, i, :], lamv[i].partition_broadcast(128), w=[Buf("lvx%d" % i)])
    lp = carve([128, 2, 64]); b_lp = Buf("lp")
    ls = carve([128, 2]); b_ls = Buf("ls")
    S.barrier()
    TT("dve", lp[:, 0, :], lv[:, 0, :], lv[:, 1, :], ALU.mult, r=[b_lv], w=[b_lp])
    TT("dve", lp[:, 1, :], lv[:, 2, :], lv[:, 3, :], ALU.mult, r=[b_lv], w=[b_lp])
    S.op("dve", lambda e: e.tensor_reduce(ls, lp, AX.X, ALU.add), [b_lp], [b_ls])
    ACT(ls, ls, AF.Exp, r=[b_ls], w=[b_ls])
    TT("dve", lamc[:, 0:1], ls[:, 0:1], ls[:, 1:2], ALU.subtract, r=[b_ls], w=[b_lam])
    TS("dve", lamc[:, 0:1], lamc[:, 0:1], 0.2, None, ALU.add, r=[b_lam], w=[b_lam])
    TS("dve", lamc[:, 1:2], lamc[:, 0:1], -1.0, None, ALU.mult, r=[b_lam], w=[b_lam])

    _save_apos = apos[0]
    apos[0] = 102 * KB
    posq_i = carve([128, 640], I32); b_posq = Buf("posq")
    relf = carve([128, 640]); b_relf = Buf("relf")
    posk_i = carve([128, 1], I32)
    posk_f = carve([128, 1]); b_posk = Buf("posk")
    tA = carve([128, 4, NBRK]); tB = carve([128, 4, NBRK]); t0 = carve([128, 4]); b_tab = Buf("tab")
    tmpS = carve([128, 640]); b_tmpS = Buf("tmpS")
    s5 = carve([128, 640]); b_s5 = Buf("s5")
    DMA("sp", posq_i, pos[0:640].partition_broadcast(128), w=[Buf("pq")])
    DMA("sp", posk_i, pos[256:384].rearrange("(p o) -> p o", o=1), w=[Buf("pk")])
    for h in range(4):
        DMA("sp", tA[:, h, :], tabA[h].partition_broadcast(128), w=[Buf("ta")])
        DMA("sp", tB[:, h, :], tabB[h].partition_broadcast(128), w=[Buf("tb")])
        DMA("sp", t0[:, h:h + 1], tab0[h].partition_broadcast(128), w=[Buf("t0")])
    assert apos[0] <= 124 * KB, apos[0]
    S.barrier()
    CP("dve", relf, posq_i, r=[b_posq], w=[b_relf])
    CP("dve", posk_f, posk_i, r=[], w=[b_posk])
    TS("dve", relf, relf, posk_f[:, 0:1], -1.0, ALU.subtract, ALU.mult, r=[b_relf, b_posk], w=[b_relf])
    TT("dve", tA, tA, tB, ALU.subtract, r=[b_tab], w=[b_tab])
    for h in range(nheads):
        TS("dve", s5, relf, 0.0, t0[:, h:h + 1], ALU.mult, ALU.add, r=[b_relf, b_tab], w=[b_s5])
        for j, (rj, _, _) in enumerate(T5_BREAKS):
            TS("dve", tmpS, relf, float(rj) - 0.5, tA[:, h, j:j + 1], ALU.is_ge, ALU.mult,
               r=[b_relf, b_tab], w=[b_tmpS])
            TT("dve", s5, s5, tmpS, ALU.add, r=[b_tmpS, b_s5], w=[b_s5])
        CP("dve", strips[:, h, 0:512].rearrange("p (a b) -> p a b", a=4),
           s5[:, 0:128].unsqueeze(1).to_broadcast([128, 4, 128]), r=[b_s5], w=[b_strips[h]])
        CP("dve", strips[:, h, 512:896], s5[:, 128:512], r=[b_s5], w=[b_strips[h]])
        CP("dve", strips[:, h, 896:1408].rearrange("p (a b) -> p a b", a=4),
           s5[:, 512:640].unsqueeze(1).to_broadcast([128, 4, 128]), r=[b_s5], w=[b_strips[h]])

    apos[0] = _save_apos

    modrow = carve([1, 6 * D]); b_modrow = Buf("modrow")
    badar = [carve([1, 512]) for _ in range(2)]; b_badar = [Buf("badar0"), Buf("badar1")]
    wada = [carve([128, 8, 256]) for _ in range(2)]; b_wada = [Buf("wada0"), Buf("wada1")]
    w_ada_v = w_ada.rearrange("(k p) n -> p k n", p=128)
    for blk in range(24):
        sl = blk % 2
        cs = slice(blk * 256, (blk + 1) * 256)
        DMA("sp", wada[sl], w_ada_v[:, :, cs], w=[b_wada[sl]])
        DMA("sp", badar[sl][:, 0:256], b_ada[:, cs], w=[b_badar[sl]])
        for k in range(8):
            MM(bank(sl)[0:1, 0:256], silc[:, k:k + 1], wada[sl][:, k, :], start=(k == 0), stop=(k == 7),
               r=[b_silc, b_wada[sl]], w=[PB[sl]])
        TT("dve", modrow[:, cs], bank(sl)[0:1, 0:256], badar[sl][:, 0:256],
           ALU.add, r=[PB[sl], b_badar[sl]], w=[b_modrow])
    rows = carve([1, 2, D]); b_rows = Buf("rows")
    nwr = carve([1, 2, D]); b_nwr = Buf("nwr")
    DMA("sp", nwr[:, 0, :], anw, w=[Buf("nw0")])
    DMA("sp", nwr[:, 1, :], fnw, w=[b_nwr])
    S.barrier()
    STT("dve", rows[:, 0, :], modrow[:, D:2 * D], 1.0, nwr[:, 0, :], ALU.add, ALU.mult, r=[b_modrow, b_nwr], w=[b_rows])
    STT("dve", rows[:, 1, :], modrow[:, 4 * D:5 * D], 1.0, nwr[:, 1, :], ALU.add, ALU.mult, r=[b_modrow, b_nwr], w=[b_rows])
    srcs = [rows[:, 0, :], modrow[:, 0:D], rows[:, 1, :], modrow[:, 3 * D:4 * D]]
    for v in range(4):
        for j in range(8):
            MM(bank(2)[:, v * 8 + j:v * 8 + j + 1], srcs[v][:, j * 128:(j + 1) * 128], identf[0:1, 0:1],
               r=[b_rows, b_modrow, b_identf], w=[PB[2]])
    CP("dve", modcol[:], bank(2)[:, 0:32], r=[PB[2]], w=[b_modcol])
    for (G, bG, off) in ((GA, b_GA, 2 * D), (GF, b_GF, 5 * D)):
        for hh in range(2):
            MM(bank(3 + hh), onesf[0:1, :], modrow[:, off + hh * 512:off + (hh + 1) * 512],
               r=[b_c0, b_modrow], w=[PB[3 + hh]])
            CP("dve", G[:, hh * 512:(hh + 1) * 512], bank(3 + hh), r=[PB[3 + hh]], w=[bG])
    dbs = carve([1, 512]); b_dbs = Buf("dbs")
    DMA("sp", dbs[:, 0:256], decb[0], w=[Buf("dbx")])
    DMA("sp", dbs[:, 256:512], decb[1], w=[b_dbs])
    lrs = carve([128, 8, 32]); b_lrs = Buf("lrs")
    w_in_v = w_in.rearrange("(k p) n -> p k n", p=128)
    DMA("sp", lrs, w_in_v[:, :, 1536:1568], w=[b_lrs])
    dws = carve([16, 2, 256]); b_dws = Buf("dws")
    DMA("sp", dws[:, 0, :], decw[0], w=[Buf("dwx")])
    DMA("sp", dws[:, 1, :], decw[1], w=[b_dws])
    lrT = carve([16, 2, D]); b_lrT = Buf("lrT")
    S.barrier()
    CP("dve", decbK[0:1, :], dbs, r=[b_dbs], w=[b_decbb])
    assert apos[0] <= 102 * KB, apos[0]
    apos[0] = 124 * KB
    Wall = carve([128, 8, WALLC], BF16); b_Wall = Buf("Wall")
    S.barrier()
    for (dst, src) in ((QG, 0), (VG, 512), (RG, 1024), (QD, 1568), (KD, 2080), (VD, 2592)):
        DMA("pool", Wall[:, :, dst:dst + 512], w_in_v[:, :, src:src + 512], w=[Buf("wl%d" % dst)])
    S.barrier()
    for d in range(2):
        for k in range(8):
            TR(bank(5 + k // 4)[0:16, (k % 4) * 128:(k % 4 + 1) * 128], lrs[:, k, d * 16:(d + 1) * 16], identf[:],
               r=[b_lrs, b_identf], w=[PB[5 + k // 4]])
        CP("dve", lrT[:, d, 0:512], bank(5)[0:16, :], r=[PB[5]], w=[b_lrT])
        CP("dve", lrT[:, d, 512:1024], bank(6)[0:16, :], r=[PB[6]], w=[b_lrT])
        for k in range(8):
            bk = 5 + (k % 2)
            MM(bank(bk)[:, 0:256], lrT[:, d, k * 128:(k + 1) * 128], dws[:, d, :], r=[b_lrT, b_dws], w=[PB[bk]])
            CP("dve", Wall[:, k, ZUP + d * 256:ZUP + (d + 1) * 256], bank(bk)[:, 0:256], r=[PB[bk]], w=[b_Wall])
    S.barrier()

    apos[0] = a0
    xts = [carve([128, D]) for _ in range(2)]; b_xt = [Buf("xt0"), Buf("xt1")]
    junk = carve([128, D], BF16); b_junk = Buf("junk")
    xn = carve([128, D], BF16); b_xn = Buf("xn")
    st2 = carve([128, 2]); b_st = Buf("st")
    hT = [carve([128, 8, 128], BF16) for _ in range(2)]; b_hT = [Buf("hT0"), Buf("hT1")]
    cnt = {"x": 0}

    def front_end(xsrc, colbase, tbank, rd=(), out=None):
        i = cnt["x"] % 2
        cnt["x"] += 1
        h_out, h_buf = (hT[i], b_hT[i]) if out is None else out
        ACT(junk, xsrc, AF.Square, accum_out=st2[:, 0:1], r=list(rd), w=[b_junk, b_st])
        TS("dve", st2[:, 1:2], st2[:, 0:1], 1.0 / D, EPS, ALU.mult, ALU.add, r=[b_st], w=[b_st])
        ACT(st2[:, 1:2], st2[:, 1:2], AF.Ln, r=[b_st], w=[b_st])
        ACT(st2[:, 1:2], st2[:, 1:2], AF.Exp, scale=-0.5, r=[b_st], w=[b_st])
        TS("dve", xn, xsrc, st2[:, 1:2], None, ALU.mult, r=list(rd) + [b_st], w=[b_xn])
        tb = bankb(tbank)
        for j in range(8):
            TR(tb[:, j * 128:(j + 1) * 128], xn[:, j * 128:(j + 1) * 128], identb[:], r=[b_xn, b_identb], w=[PB[tbank]])
        tv = tb[:, 0:1024].rearrange("p (a b) -> p a b", a=8)
        sc = modcol[:, colbase:colbase + 8].unsqueeze(2).to_broadcast([128, 8, 128])
        shf = modcol[:, colbase + 8:colbase + 16].unsqueeze(2).to_broadcast([128, 8, 128])
        TT("dve", h_out, tv, sc, ALU.mult, r=[PB[tbank], b_modcol], w=[h_buf])
        TT("dve", h_out, h_out, shf, ALU.add, r=[h_buf, b_modcol], w=[h_buf])
        return h_out, h_buf

    def proj_tok(bk, h_ap, h_b, c0, n=512, extra=None):
        for k in range(8):
            MM(bank(bk)[:, 0:n], h_ap[:, k, :], Wall[:, k, c0:c0 + n], start=(k == 0), stop=(k == 7 and extra is None),
               r=[h_b, b_Wall], w=[PB[bk]])
        if extra is not None:
            extra()

    Sst = [carve([128, 2, 128]) for _ in range(2)]; b_S = [Buf("Sup"), Buf("Sdn")]
    sdn_t = [carve([128, 2, 2, 128], BF16) for _ in range(2)]; b_sdn = [Buf("sdn0"), Buf("sdn1")]
    Sup_b = [carve([128, 2, 128], BF16) for _ in range(2)]; b_Sub = [Buf("Sub0"), Buf("Sub1")]
    MSET("dve", Sst[0], 0.0, w=[b_S[0]])
    MSET("dve", Sst[1], 0.0, w=[b_S[1]])
    u_t = carve([128, 512]); b_u = Buf("u")
    ex_t = carve([128, 512]); b_ex = Buf("ex")
    est = carve([128, 256]); b_est = Buf("est")
    kst = carve([128, 256], BF16); b_kst = Buf("kst")
    vgb = carve([128, 512], BF16); b_vgb = Buf("vgb")
    dec = carve([128, 2, 2]); b_dec = Buf("dec")
    vaug = [carve([128, 4, 129], BF16) for _ in range(2)]; b_vaug = [Buf("va0"), Buf("va1")]
    kTs = [carve([128, 4, 128], BF16) for _ in range(2)]; b_kTs = [Buf("kT0"), Buf("kT1")]
    for i in range(2):
        MSET("dve", vaug[i], 1.0, w=[b_vaug[i]])

    def gla_state(d, ksrc, ksrc_b, ucols, t, store, ubank2=5):
        tS = tri[1] if d == 0 else tri[3]
        MM(bank(6)[:, 0:256], tS[:], u_t[:, ucols], r=[b_tris[1], b_tris[3], b_u], w=[PB[6]])
        for p in range(2):
            MM(bank(6)[:, 256 + 2 * p:258 + 2 * p], u_t[:, ucols.start + p * 128:ucols.start + (p + 1) * 128], cind[:],
               r=[b_u, b_tri2], w=[PB[6]])
        ACT(est, bank(6)[:, 0:256], AF.Exp, r=[PB[6]], w=[b_est])
        ACT(dec.rearrange("p a b -> p (a b)"), bank(6)[:, 256:260], AF.Exp, r=[PB[6]], w=[b_dec])
        TT("dve", kst, ksrc, est, ALU.mult, r=[ksrc_b, b_est], w=[b_kst])
        updb = [bank(7).rearrange("p (a c) -> p a c", a=4), bank(ubank2).rearrange("p (a c) -> p a c", a=4)]
        ubuf = [PB[7], PB[ubank2]]
        for n in range(2):
            for h in range(4):
                MM(updb[n][64 * (h % 2):64 * (h % 2) + 64, h // 2, :], kst[64 * n:64 * n + 64, h * 64:(h + 1) * 64],
                   vgb[64 * n:64 * n + 64, h * 128:(h + 1) * 128], r=[b_kst, b_vgb], w=[ubuf[n]])
        order = (0, 1) if d == 0 else (1, 0)
        for n in order:
            ch = 2 * t + n
            if d == 1 and store:
                CP("act", sdn_t[t % 2][:, n, :, :], Sst[1], r=[b_S[1]], w=[b_sdn[t % 2]])
            if d == 0:
                CP("act", Sup_b[n], Sst[0], r=[b_S[0]], w=[b_Sub[n]])
            for p in range(2):
                STT("dve", Sst[d][:, p, :], Sst[d][:, p, :], dec[:, p, n:n + 1], updb[n][:, p, :], ALU.mult, ALU.add,
                    r=[b_S[d], b_dec, ubuf[n]], w=[b_S[d]])

    def fe_x(t):
        i = cnt["x"] % 2
        DMA("sp", xts[i], x[t * 128:(t + 1) * 128, :], w=[b_xt[i]])
        return front_end(xts[i], 0, 0, rd=[b_xt[i]])

    def ph1_proj(t, h_ap, h_b):
        for k in range(8):
            MM(bank(1)[:, 0:256], h_ap[:, k, :], Wall[:, k, KG:KG + 256], start=(k == 0), stop=(k == 7), r=[h_b, b_Wall], w=[PB[1]])
        for k in range(8):
            MM(bank(1)[:, 256:512], h_ap[:, k, :], Wall[:, k, ZDN:ZDN + 256], start=(k == 0), stop=False, r=[h_b, b_Wall], w=[PB[1]])
        MM(bank(1)[:, 256:512], onesK[:], decbK[:, 256:512], start=False, stop=True, r=[b_c0, b_decbb], w=[PB[1]])
        proj_tok(2, h_ap, h_b, VG)
        proj_tok(3, h_ap, h_b, VD)
        for c in range(4):
            for k in range(8):
                MM(bank(4)[:, c * 128:(c + 1) * 128], Wall[:, k, KD + c * 128:KD + (c + 1) * 128], h_ap[:, k, :],
                   start=(k == 0), stop=(k == 7), r=[h_b, b_Wall], w=[PB[4]])

    def ph1_rest(t):
        own = t < NQT
        ACT(ex_t[:, 0:256], bank(1)[:, 256:512], AF.Exp, scale=-1.0, r=[PB[1]], w=[b_ex])
        ACT(u_t[:, 256:512], ex_t[:, 0:256], AF.Ln, bias=1.0, r=[b_ex], w=[b_u])
        CP("act", vgb, bank(2), r=[PB[2]], w=[b_vgb])
        vi = (t % 2)
        CP("dve", vaug[vi][:, :, 0:128], bank(3).rearrange("p (a b) -> p a b", a=4), r=[PB[3]], w=[b_vaug[vi]])
        DMA("sp", Vd[:, :, t, :].rearrange("h p c -> p h c"), vaug[vi], r=[b_vaug[vi]], w=[dVd])
        CP("act", kTs[vi], bank(4).rearrange("p (a b) -> p a b", a=4), r=[PB[4]], w=[b_kTs[vi]])
        DMA("sp", KdT[:, :, t * 128:(t + 1) * 128].rearrange("h p c -> p h c"), kTs[vi], r=[b_kTs[vi]], w=[dKdT])
        gla_state(1, bank(1)[:, 0:256], PB[1], slice(256, 512), t, own)
        if own:
            DMA("sp", SdnD[2 * t:2 * t + 2].rearrange("n p c -> p n c"),
                sdn_t[t % 2].rearrange("p n a b -> p n (a b)"), r=[b_sdn[t % 2]], w=[dSdn])

    tl1 = list(range(NT - 1, NT - 1 - nt1, -1)) if phases >= 1 else []
    if tl1:
        hcur = fe_x(tl1[0])
    for ii, t in enumerate(tl1):
        ph1_proj(t, *hcur)
        hnext = fe_x(tl1[ii + 1]) if ii + 1 < len(tl1) else None
        ph1_rest(t)
        hcur = hnext

    qk_s = carve([128, 512]); b_qk = Buf("qk")
    sr_s = carve([128, 512]); b_sr = Buf("sr")
    sr_t = carve([128, 512]); b_srt = Buf("srt")
    Eb = carve([128, 512]); b_Eb = Buf("Eb")
    Enb = carve([128, 512]); b_Enb = Buf("Enb")
    qkin = carve([128, 2, 2, 256], BF16); b_qkin = Buf("qkin")
    qkT = carve([128, 8, 128], BF16); b_qkT = Buf("qkT")
    attm = [carve([128, 4, 128], BF16) for _ in range(2)]; b_attm = [Buf("attm0"), Buf("attm1")]
    osq = carve([128, 512]); b_osq = Buf("osq")
    ost = carve([128, 8]); b_ost = Buf("ost")
    oab = carve([128, 512], BF16); b_oab = Buf("oab")
    oaf = carve([128, 512]); b_oaf = Buf("oaf")
    qTs = [carve([128, 4, 128], BF16) for _ in range(2)]; b_qTs = [Buf("qT0"), Buf("qT1")]
    oaTt = [carve([128, 4, 128], BF16) for _ in range(2)]; b_oaTt = [Buf("oaTt0"), Buf("oaTt1")]
    assert apos[0] <= 102 * KB, apos[0]
    TT("dve", sr_t, gnwb[:], gnwb[:], ALU.mult, r=[b_cst], w=[b_srt])
    def ph2_fe(t):
        i = cnt["x"] % 2
        DMA("sp", xts[i], x[t * 128:(t + 1) * 128, :], w=[b_xt[i]])
        DMA("sp", sdn_t[t % 2].rearrange("p n a b -> p n (a b)"), SdnD[2 * t:2 * t + 2].rearrange("n p c -> p n c"),
            r=[dSdn], w=[b_sdn[t % 2]])
        return front_end(xts[i], 0, 0, rd=[b_xt[i]])

    def ph2_proj(t, h_ap, h_b):
        proj_tok(1, h_ap, h_b, QG)
        for k in range(8):
            MM(bank(2), h_ap[:, k, :], Wall[:, k, ZUP:ZUP + 512], start=(k == 0), stop=False, r=[h_b, b_Wall], w=[PB[2]])
        MM(bank(2), onesK[:], decbK[:], start=False, stop=True, r=[b_c0, b_decbb], w=[PB[2]])
        proj_tok(3, h_ap, h_b, VG)
        proj_tok(4, h_ap, h_b, RG)
        for c in range(4):
            for k in range(8):
                MM(bank(5)[:, c * 128:(c + 1) * 128], Wall[:, k, QD + c * 128:QD + (c + 1) * 128], h_ap[:, k, :],
                   start=(k == 0), stop=(k == 7), r=[h_b, b_Wall], w=[PB[5]])

    def ph2_rest(t):
        CP("act", qk_s, bank(1), r=[PB[1]], w=[b_qk])
        ACT(ex_t, bank(2), AF.Exp, scale=-1.0, r=[PB[2]], w=[b_ex])
        ACT(u_t, ex_t, AF.Ln, bias=1.0, r=[b_ex], w=[b_u])
        CP("act", vgb, bank(3), r=[PB[3]], w=[b_vgb])
        ACT(sr_s, bank(4), AF.Exp, scale=-1.0, r=[PB[4]], w=[b_sr])
        TS("dve", sr_s, sr_s, 1.0, None, ALU.add, r=[b_sr], w=[b_sr])
        S.op("dve", lambda e: e.reciprocal(sr_s, sr_s), [b_sr], [b_sr])
        TT("dve", sr_s, sr_s, bank(4), ALU.mult, r=[b_sr, PB[4]], w=[b_sr])
        TT("dve", sr_t, sr_s, gnwb[:], ALU.mult, r=[b_sr, b_cst], w=[b_srt])
        qi = t % 2
        CP("act", qTs[qi], bank(5).rearrange("p (a b) -> p a b", a=4), r=[PB[5]], w=[b_qTs[qi]])
        DMA("sp", QdT[:, :, t * 128:(t + 1) * 128].rearrange("h p c -> p h c"), qTs[qi], r=[b_qTs[qi]], w=[dQdT])
        MM(bank(1)[:, 0:256], tri[0][:], u_t[:, 0:256], r=[b_tris[0], b_u], w=[PB[1]])
        MM(bank(1)[:, 256:512], tri[2][:], u_t[:, 256:512], r=[b_tris[2], b_u], w=[PB[1]])
        ACT(Eb, bank(1), AF.Exp, r=[PB[1]], w=[b_Eb])
        ACT(Enb, bank(1), AF.Exp, scale=-1.0, r=[PB[1]], w=[b_Enb])
        for d in range(2):
            STT("dve", qkin[:, d, 0, :], qk_s[:, 0:256], 0.125, Eb[:, d * 256:(d + 1) * 256], ALU.mult, ALU.mult,
                r=[b_qk, b_Eb], w=[b_qkin])
            TT("dve", qkin[:, d, 1, :], qk_s[:, 256:512], Enb[:, d * 256:(d + 1) * 256], ALU.mult, r=[b_qk, b_Enb], w=[b_qkin])
        tb = bankb(2)
        for d in range(2):
            for qk in range(2):
                for p in range(2):
                    j = d * 4 + qk * 2 + p
                    TR(tb[:, j * 128:(j + 1) * 128], qkin[:, d, qk, p * 128:(p + 1) * 128], identb[:], r=[b_qkin, b_identb], w=[PB[2]])
        CP("act", qkT, tb[:, 0:1024].rearrange("p (a b) -> p a b", a=8), r=[PB[2]], w=[b_qkT])
        for d in range(2):
            for h in range(4):
                p, o = h // 2, 64 * (h % 2)
                MM(bank(3 + h % 2)[:, (d * 2 + p) * 128:(d * 2 + p + 1) * 128], qkT[o:o + 64, d * 4 + 2 + p, :],
                   qkT[o:o + 64, d * 4 + p, :], r=[b_qkT], w=[PB[3 + h % 2]])
        for d in range(2):
            for par in range(2):
                TT("dve", attm[d][:, par::2, :], bank(3 + par)[:, d * 256:(d + 1) * 256].rearrange("p (a b) -> p a b", a=2),
                   maskt[d][:].unsqueeze(1).to_broadcast([128, 2, 128]), ALU.mult, r=[PB[3 + par], b_mask[d]], w=[b_attm[d]])
        gla_state(0, qk_s[:, 256:512], b_qk, slice(0, 256), t, False)
        ob5 = bank(5).rearrange("p (a b) -> p a b", a=4)
        for h in range(4):
            p, o = h // 2, 64 * (h % 2)
            MM(ob5[:, h, :], attm[0][:, h, :], vgb[:, h * 128:(h + 1) * 128], start=True, stop=False,
               r=[b_attm[0], b_vgb], w=[PB[5]])
            MM(ob5[:, h, :], attm[1][:, h, :], vgb[:, h * 128:(h + 1) * 128], start=False, stop=False,
               r=[b_attm[1], b_vgb], w=[PB[5]])
            for n in range(2):
                MM(ob5[64 * n:64 * n + 64, h, :], qkT[o:o + 64, p, 64 * n:64 * n + 64], Sup_b[n][o:o + 64, p, :],
                   start=False, stop=False, r=[b_qkT, b_Sub[n]], w=[PB[5]])
                MM(ob5[64 * n:64 * n + 64, h, :], qkT[o:o + 64, 4 + p, 64 * n:64 * n + 64],
                   sdn_t[t % 2][o:o + 64, n, p, :], start=False, stop=True, r=[b_qkT, b_sdn[t % 2]], w=[PB[5]])
        CP("act", oaf, bank(5), r=[PB[5]], w=[b_oaf])
        TT("dve", osq, oaf, oaf, ALU.mult, r=[b_oaf], w=[b_osq])
        S.op("dve", lambda e: e.tensor_reduce(ost[:, 0:4], osq.rearrange("p (a b) -> p a b", a=4), AX.X, ALU.add),
             [b_osq], [b_ost])
        TS("dve", ost[:, 4:8], ost[:, 0:4], 1.0 / 128, EPS, ALU.mult, ALU.add, r=[b_ost], w=[b_ost])
        ACT(ost[:, 4:8], ost[:, 4:8], AF.Ln, r=[b_ost], w=[b_ost])
        ACT(ost[:, 4:8], ost[:, 4:8], AF.Exp, scale=-0.5, r=[b_ost], w=[b_ost])
        TT("dve", oaf.rearrange("p (a b) -> p a b", a=4), oaf.rearrange("p (a b) -> p a b", a=4),
           ost[:, 4:8].unsqueeze(2).to_broadcast([128, 4, 128]), ALU.mult, r=[b_oaf, b_ost], w=[b_oaf])
        TT("dve", oab, oaf, sr_t, ALU.mult, r=[b_oaf, b_srt], w=[b_oab])
        tb = bankb(1)
        for h in range(4):
            TR(tb[:, h * 128:(h + 1) * 128], oab[:, h * 128:(h + 1) * 128], identb[:], r=[b_oab, b_identb], w=[PB[1]])
        CP("act", oaTt[t % 2], tb[:, 0:512].rearrange("p (a b) -> p a b", a=4), r=[PB[1]], w=[b_oaTt[t % 2]])
        DMA("sp", CatT[0:4, :, t * 128:(t + 1) * 128].rearrange("k p c -> p k c"), oaTt[t % 2], r=[b_oaTt[t % 2]], w=[dCat])


    tl2 = list(range(nt2)) if phases >= 2 else []
    if tl2:
        hcur = ph2_fe(tl2[0])
    for ii, t in enumerate(tl2):
        ph2_proj(t, *hcur)
        hnext = ph2_fe(tl2[ii + 1]) if ii + 1 < len(tl2) else None
        ph2_rest(t)
        hcur = hnext

    S.barrier()
    apos[0] = a0
    KTh = [carve([128, L], BF16) for _ in range(2)]; b_KTh = [Buf("KTh0"), Buf("KTh1")]
    Vh = [carve([128, NT, 129], BF16) for _ in range(2)]; b_Vh = [Buf("Vh0"), Buf("Vh1")]
    QTm = [[carve([128, TQ], BF16) for _ in range(2)] for _ in range(2)]
    b_QTh = [Buf("QTh0"), Buf("QTh1")]
    PT = [carve([128, 2, 512], BF16) for _ in range(3)]; b_PT = [Buf("PT%d" % i) for i in range(3)]
    sbias = [carve([128, 2, 512]) for _ in range(2)]; b_sbias = [Buf("sbias0"), Buf("sbias1")]
    ep = carve([128, 8]); b_ep = Buf("ep")
    ot0 = carve([128, 128]); b_ot0 = Buf("ot0")
    ob1 = carve([128, 128]); b_ob1 = Buf("ob1")
    obj = carve([128, 128]); b_obj = Buf("obj")
    obn = carve([128, 128], BF16); b_obn = Buf("obn")
    obTt = [carve([128, 128], BF16) for _ in range(2)]; b_obTt = [Buf("obTt0"), Buf("obTt1")]
    ocnt = [0]
    Osb = carve([128, 3, 512]); b_Osb = Buf("Osb")
    assert apos[0] <= ARENA, apos[0]
    bS = [Buf("S0"), Buf("S1")]
    bO = [Buf("O0"), Buf("O1"), Buf("O2")]
    def oacc(m, qt):
        idx = m * 4 + qt
        return bank(4 + idx // 3)[:, (idx % 3) * 129:(idx % 3) * 129 + 129], bO[idx // 3]

    if phases >= 3:
        def load_head(h):
            s = h % 2
            DMA("sp", KTh[s], KdT[h], r=[dKdT], w=[b_KTh[s]])
            DMA("sp", Vh[s], Vd[h], r=[dVd], w=[b_Vh[s]])
            DMA("sp", QTm[s][0][0:64, :], QdT[h, 0:64, :], r=[dQdT], w=[b_QTh[s]])
            DMA("sp", QTm[s][1][64:128, :], QdT[h, 64:128, :], r=[dQdT], w=[b_QTh[s]])
        for s_ in range(2):
            MSET("dve", QTm[s_][0][64:128, :], 0.0, w=[Buf("qz")])
            MSET("dve", QTm[s_][1][0:64, :], 0.0, w=[Buf("qz")])
        S.barrier()
        load_head(0)
        pcnt = 0
        scnt = 0
        ngrp = (nt2 * 128 + 511) // 512
        glist = [(h, g) for h in range(nheads) for g in range(ngrp)]

        def smm_hg(h, g, kb):
            s = h % 2
            nq = min(512, nt2 * 128 - g * 512)
            sl = kb % 2
            for m in range(2):
                MM(bank(2 * sl + m)[:, 0:nq], KTh[s][:, kb * 128:(kb + 1) * 128],
                   QTm[s][m][:, g * 512:g * 512 + nq], r=[b_KTh[s], b_QTh[s]], w=[bS[sl]])
        for gi, (h, g) in enumerate(glist):
            s = h % 2
            if g == 0 and h + 1 < nheads:
                load_head(h + 1)
            if True:
                nq = min(512, nt2 * 128 - g * 512)
                nqt = nq // 128
                if gi == 0:
                    smm_hg(h, g, 0)
                    smm_hg(h, g, 1)
                for b3 in range(3):
                    MM(bank(4 + b3), zerob[:, 0:128], zerob[:], r=[b_zero], w=[bO[b3]])

                def smm(kb, h=h, g=g):
                    smm_hg(h, g, kb)
                for kb in range(NT):
                    sl = kb % 2
                    pt = PT[pcnt % 3]; bpt = b_PT[pcnt % 3]
                    pcnt += 1
                    sv = ps[:, sl * 1024:sl * 1024 + 1024].rearrange("p (a b) -> p a b", a=2)[:, :, 0:nq]
                    e = kb - 4 * g
                    uni = (e >= 5) or (e <= -2)
                    if uni:
                        col = strips[:, h, 0:1] if e > 0 else strips[:, h, 10 * 128:10 * 128 + 1]
                        ACT(pt[:, :, 0:nq], sv, AF.Exp, scale=0.125, bias=col, r=[bS[sl], b_strips[h]], w=[bpt])
                    else:
                        sbv = sbias[scnt % 2]; bsb = b_sbias[scnt % 2]
                        scnt += 1
                        win = strips[:, h, (5 - e) * 128:(5 - e) * 128 + nq]
                        STT("dve", sbv[:, :, 0:nq], sv, 0.125, win.unsqueeze(1).to_broadcast([128, 2, nq]), ALU.mult, ALU.add,
                            r=[bS[sl], b_strips[h]], w=[bsb])
                        ACT(pt[:, :, 0:nq], sbv[:, :, 0:nq], AF.Exp, r=[bsb], w=[bpt])
                    if kb + 2 < NT:
                        smm(kb + 2)
                    for m in range(2):
                        for qt in range(nqt):
                            oap, ob = oacc(m, qt)
                            MM(oap, pt[:, m, qt * 128:(qt + 1) * 128], Vh[s][:, kb, :], start=False, stop=(kb == NT - 1),
                               r=[bpt, b_Vh[s]], w=[ob], skip_group_check=True)
                if gi + 1 < len(glist):
                    smm_hg(glist[gi + 1][0], glist[gi + 1][1], 0)
                    smm_hg(glist[gi + 1][0], glist[gi + 1][1], 1)
                for b3 in range(3):
                    CP("dve", Osb[:, b3, :], bank(4 + b3), r=[bO[b3]], w=[b_Osb])

                def osb(m, qt):
                    idx = m * 4 + qt
                    return Osb[:, idx // 3, (idx % 3) * 129:(idx % 3) * 129 + 129], b_Osb
                for qt in range(nqt):
                    o0, bo0 = osb(0, qt)
                    o1, bo1 = osb(1, qt)
                    S.op("dve", lambda e, o0=o0: e.reciprocal(ep[:, 0:1], o0[:, 128:129]), [bo0], [b_ep])
                    S.op("dve", lambda e, o1=o1: e.reciprocal(ep[:, 1:2], o1[:, 128:129]), [bo1], [b_ep])
                    TT("dve", ep[:, 2:3], ep[:, 1:2], lamc[:, 1:2], ALU.mult, r=[b_ep, b_lam], w=[b_ep])
                    TS("dve", ot0, o0[:, 0:128], ep[:, 0:1], None, ALU.mult, r=[bo0, b_ep], w=[b_ot0])
                    STT("dve", ob1, o1[:, 0:128], ep[:, 2:3], ot0, ALU.mult, ALU.add, r=[bo1, b_ep, b_ot0], w=[b_ob1])
                    TT("dve", obj, ob1, ob1, ALU.mult, r=[b_ob1], w=[b_obj])
                    S.op("dve", lambda e: e.tensor_reduce(ep[:, 3:4], obj, AX.X, ALU.add), [b_obj], [b_ep])
                    TS("dve", ep[:, 4:5], ep[:, 3:4], 1.0 / 128, EPS, ALU.mult, ALU.add, r=[b_ep], w=[b_ep])
                    ACT(ep[:, 4:5], ep[:, 4:5], AF.Ln, r=[b_ep], w=[b_ep])
                    ACT(ep[:, 4:5], ep[:, 4:5], AF.Exp, scale=-0.5, r=[b_ep], w=[b_ep])
                    STT("dve", obn, ob1, ep[:, 4:5], dnwb[:], ALU.mult, ALU.mult, r=[b_ob1, b_ep, b_dnw], w=[b_obn])
                    tb = bankb(7)
                    TR(tb[:, 0:128], obn, identb[:], r=[b_obn, b_identb], w=[PB[7]])
                    tok = g * 512 + qt * 128
                    oi = ocnt[0] % 2
                    ocnt[0] += 1
                    CP("act", obTt[oi], tb[:, 0:128], r=[PB[7]], w=[b_obTt[oi]])
                    DMA("sp", CatT[4 + h, :, tok:tok + 128], obTt[oi], r=[b_obTt[oi]], w=[dCat])

    S.barrier()
    apos[0] = a0
    Wout = carve([128, 8, D], BF16); b_Wout = Buf("Wout")
    x1s = [carve([128, D]) for _ in range(2)]; b_x1 = [Buf("x1a"), Buf("x1b")]
    xts4 = [carve([128, D]) for _ in range(2)]; b_xt4 = [Buf("xt4a"), Buf("xt4b")]
    junk = carve([128, D], BF16); b_junk = Buf("junk4")
    xn = carve([128, D], BF16); b_xn = Buf("xn4")
    st2 = carve([128, 2]); b_st = Buf("st4")
    hT = [carve([128, 8, 128], BF16) for _ in range(2)]; b_hT = [Buf("h2T0"), Buf("h2T1")]
    zcol = carve([128, 8, 2], BF16); b_zcol = Buf("zcol")
    catg = [carve([128, 8, 512], BF16) for _ in range(2)]; b_catg = [Buf("catg0"), Buf("catg1")]
    h2b = [carve([128, 8, 512], BF16) for _ in range(2)]; b_h2b = [Buf("h2b0"), Buf("h2b1")]
    if phases >= 4:
        w_out_v = w_out.rearrange("(k p) n -> p k n", p=128)
        S.barrier()
        for hh in range(2):
            DMA("pool", Wout[:, :, hh * 512:(hh + 1) * 512], w_out_v[:, :, hh * 512:(hh + 1) * 512],
                w=[Buf("wo%d" % hh)] if hh == 0 else [b_Wout])
        MSET("dve", zcol, 0.0, w=[b_zcol])
        DMA("sp", H2T[:, :, 0:2].rearrange("k p c -> p k c"), zcol, r=[b_zcol], w=[dH2T])
        S.barrier()
        def ph4_mm(t):
            i = t % 2
            DMA("sp", xts4[i], x[t * 128:(t + 1) * 128, :], w=[b_xt4[i]])
            g4, q4 = t // 4, t % 4
            gs = g4 % 2
            if q4 == 0:
                n4 = min(512, nt2 * 128 - g4 * 512)
                DMA("sp", catg[gs][:, :, 0:n4], CatT[:, :, g4 * 512:g4 * 512 + n4].rearrange("k p c -> p k c"),
                    r=[dCat], w=[b_catg[gs]])
            for hh in range(2):
                bk = 1 + 2 * i + hh
                for kc in range(8):
                    MM(bank(bk), catg[gs][:, kc, q4 * 128:(q4 + 1) * 128], Wout[:, kc, hh * 512:(hh + 1) * 512],
                       start=(kc == 0), stop=(kc == 7), r=[b_catg[gs], b_Wout], w=[PB[bk]])

        def ph4_a(t):
            i = t % 2
            for hh in range(2):
                bk = 1 + 2 * i + hh
                cs = slice(hh * 512, (hh + 1) * 512)
                TT("dve", x1s[i][:, cs], bank(bk), GA[:, cs], ALU.mult, r=[PB[bk], b_GA], w=[b_x1[i]])
            TT("dve", x1s[i], x1s[i], xts4[i], ALU.add, r=[b_x1[i], b_xt4[i]], w=[b_x1[i]])
            if t < NOWN:
                DMA("sp", X1[t * 128:(t + 1) * 128, :], x1s[i], r=[b_x1[i]], w=[dX1])

        def ph4_b(t):
            i = t % 2
            g4, q4 = t // 4, t % 4
            gs = g4 % 2
            front_end(x1s[i], 16, 0, rd=[b_x1[i]], out=(h2b[gs][:, :, q4 * 128:(q4 + 1) * 128], b_h2b[gs]))
            if q4 == 3 or t == nt2 - 1:
                n4 = (q4 + 1) * 128
                DMA("sp", H2T[:, :, 1 + g4 * 512:1 + g4 * 512 + n4].rearrange("k p c -> p k c"), h2b[gs][:, :, 0:n4],
                    r=[b_h2b[gs]], w=[dH2T])

        ph4_mm(0)
        if nt2 > 1:
            ph4_mm(1)
        ph4_a(0)
        for t in range(nt2):
            if t + 1 < nt2:
                ph4_a(t + 1)
            ph4_b(t)
            if t + 2 < nt2:
                ph4_mm(t + 2)

    S.barrier()
    apos[0] = 0
    if phases >= 5:
        Wup = carve([128, 8, 2 * DFF], BF16); b_Wup = Buf("Wup")
        w_up_v = w_up.rearrange("(k p) n -> p k n", p=128)
        for c in range(11):
            DMA("pool", Wup[:, :, c * 512:(c + 1) * 512], w_up_v[:, :, c * 512:(c + 1) * 512], w=[Buf("wu%d" % c)])
        Wdn = carve([128, NFC, D], BF16); b_Wdn = Buf("Wdn")

        def wdn(j):
            return Wdn[:, j, :]
        w_dn_v = w_down.rearrange("(k p) n -> p k n", p=128)
        for j in range(NFC):
            for hh in range(2):
                DMA("pool", wdn(j)[:, hh * 512:(hh + 1) * 512], w_dn_v[:, j, hh * 512:(hh + 1) * 512], w=[Buf("wd")])
        DMA("sp", cw[:], cwp, w=[Buf("cwx")])
        DMA("sp", cb[:], cbp, w=[b_cw])
        h2g = [carve([128, 8, 514], BF16)]; b_h2g = [Buf("h2g0")]
        gT = carve([128, NFC, 512], BF16); b_gT = Buf("gT")
        cc = [carve([128, 512]) for i in range(2)]; b_cc = [Buf("cc0"), Buf("cc1")]
        sg = carve([128, 512]); b_sg = Buf("sg")
        yt = carve([128, D]); b_yt = Buf("yt")
        x1l = carve([128, D]); b_x1l = Buf("x1l")
        fj = carve([128, D], BF16); b_fj = Buf("fj")
        fst = carve([128, 2]); b_fst = Buf("fst")
        S.barrier()
        bU = [Buf("U0"), Buf("U1"), Buf("U2")]
        ucnt = 0
        for g in range(8):
            DMA("sp", h2g[0], H2T[:, :, g * 512:g * 512 + 514].rearrange("k p c -> p k c"), r=[dH2T], w=[b_h2g[0]])
            for j in range(NFC):
                for which in range(2):
                    c = j + which * NFC
                    ui = ucnt % 3
                    ucnt += 1
                    U = ps[:, ui * 1024:ui * 1024 + 1024]
                    for k in range(8):
                        MM(U[:, 0:512], Wup[:, k, c * 128:(c + 1) * 128], h2g[0][:, k, 0:512], start=(k == 0), stop=(k == 7),
                           r=[b_h2g[0], b_Wup], w=[bU[ui]])
                    for k in range(8):
                        MM(U[:, 512:514], Wup[:, k, c * 128:(c + 1) * 128], h2g[0][:, k, 512:514], start=(k == 0), stop=(k == 7),
                           r=[b_h2g[0], b_Wup], w=[bU[ui]])
                    ACT(cc[which], U[:, 1:513], AF.Identity, scale=cw[:, 44 + c:44 + c + 1], bias=cb[:, c:c + 1],
                        r=[bU[ui], b_cw], w=[b_cc[which]])
                    STT("dve", cc[which], U[:, 0:512], cw[:, c:c + 1], cc[which], ALU.mult, ALU.add,
                        r=[bU[ui], b_cw, b_cc[which]], w=[b_cc[which]])
                    STT("dve", cc[which], U[:, 2:514], cw[:, 88 + c:88 + c + 1], cc[which], ALU.mult, ALU.add,
                        r=[bU[ui], b_cw, b_cc[which]], w=[b_cc[which]])
                ACT(sg, cc[0], AF.Silu, r=[b_cc[0]], w=[b_sg])
                TT("dve", gT[:, j, :], sg, cc[1], ALU.mult, r=[b_sg, b_cc[1]], w=[b_gT])
            for qt in range(4):
                t = g * 4 + qt
                DMA("sp", x1l[:], X1[t * 128:(t + 1) * 128, :], r=[dX1], w=[b_x1l])
                for hh in range(2):
                    for j in range(NFC):
                        MM(bank(6 + hh), gT[:, j, qt * 128:(qt + 1) * 128], wdn(j)[:, hh * 512:(hh + 1) * 512],
                           start=(j == 0), stop=(j == NFC - 1), r=[b_gT, b_Wdn], w=[PB[6 + hh]])
                for hh in range(2):
                    cs = slice(hh * 512, (hh + 1) * 512)
                    TT("dve", yt[:, cs], bank(6 + hh), GF[:, cs], ALU.mult, r=[PB[6 + hh], b_GF], w=[b_yt])
                TT("dve", yt[:], yt[:], x1l[:], ALU.add, r=[b_yt, b_x1l], w=[b_yt])
                ACT(fj[:], yt[:], AF.Square, accum_out=fst[:, 0:1], r=[b_yt], w=[b_fj, b_fst])
                TS("dve", fst[:, 1:2], fst[:, 0:1], 1.0 / D, EPS, ALU.mult, ALU.add, r=[b_fst], w=[b_fst])
                ACT(fst[:, 1:2], fst[:, 1:2], AF.Ln, r=[b_fst], w=[b_fst])
                ACT(fst[:, 1:2], fst[:, 1:2], AF.Exp, scale=-0.5, r=[b_fst], w=[b_fst])
                STT("dve", yt[:], yt[:], fst[:, 1:2], WF[:], ALU.mult, ALU.mult, r=[b_yt, b_fst, b_WF], w=[b_yt])
                DMA("sp", y[t * 128:(t + 1) * 128, :], yt[:], r=[b_yt], w=[Buf("yout")])
    elif dbg:
        pass

    S.emit(nc, stack)
    stack.close()
    nc._sched_stats = S.stats
    return nc


def _consts():
    j = np.arange(128)[:, None]
    c = np.arange(128)[None, :]
    same = (j // 64) == (c // 64)
    s = -1.0 / 16.0
    tri = [
        np.where(same & (j <= c), s, 0.0),
        np.where(same & (j > c), s, 0.0),
        np.where(same & (j >= c), s, 0.0),
        np.where(same & (j < c), s, 0.0),
    ]
    cind = np.stack([np.where(np.arange(128) < 64, s, 0.0), np.where(np.arange(128) >= 64, s, 0.0)], axis=1)
    return [t.astype(np.float32) for t in tri], cind.astype(np.float32), same


def make_in_maps(inputs):
    f = lambda a: np.ascontiguousarray(np.asarray(a))
    x = f(inputs["x"]); c = f(inputs["c"]); positions = f(inputs["positions"])
    w_in = f(inputs["w_in"])[0]
    tri, cind, same = _consts()
    s_idx = np.arange(128)[:, None]
    c_idx = np.arange(128)[None, :]
    table = f(inputs["rel_bias_table"])
    ia = [a for (_, a, _) in T5_BREAKS]
    ib = [b for (_, _, b) in T5_BREAKS]
    tabA = np.ascontiguousarray(table[ia, :].T)
    tabB = np.ascontiguousarray(table[ib, :].T)
    tab0 = np.ascontiguousarray(table[T5_B0:T5_B0 + 1, :].T)
    conv_w = f(inputs["conv_w"])[0]
    conv_b = f(inputs["conv_b"])[0]
    maps = []
    for core in range(8):
        b, hf = core // 2, core % 2
        xs = x[b]; ps_ = positions[b]
        cwv = conv_w
        wi = w_in
        names = ("fwd", "bwd")
        if hf == 1:
            xs = xs[::-1]; ps_ = ps_[::-1]
            cwv = conv_w[::-1]
            wi = w_in.copy()
            wi[:, 1536:1552] = w_in[:, 1552:1568]
            wi[:, 1552:1568] = w_in[:, 1536:1552]
            names = ("bwd", "fwd")
        if hf == 0:
            m_up = same & (s_idx <= c_idx); m_dn = same & (s_idx > c_idx)
        else:
            m_up = same & (s_idx < c_idx); m_dn = same & (s_idx >= c_idx)
        m = {
            "x": np.ascontiguousarray(xs), "pos": np.ascontiguousarray(ps_).astype(np.int32),
            "ccol": np.ascontiguousarray(c[b].reshape(8, 128).T),
            "w_ada": f(inputs["w_ada"])[0], "b_ada": f(inputs["b_ada"]),
            "anw": f(inputs["attn_norm_w"]), "fnw": f(inputs["ffn_norm_w"]), "finw": f(inputs["final_norm_w"]),
            "w_in": np.ascontiguousarray(wi),
            "decw_up": f(inputs["gla_dec_w_" + names[0]])[0], "decw_dn": f(inputs["gla_dec_w_" + names[1]])[0],
            "decb_up": f(inputs["gla_dec_b_" + names[0]]), "decb_dn": f(inputs["gla_dec_b_" + names[1]]),
            "gnw": f(inputs["gla_norm_w"])[0],
            "lam0": f(inputs["diff_lambda_q1"])[0], "lam1": f(inputs["diff_lambda_k1"])[0],
            "lam2": f(inputs["diff_lambda_q2"])[0], "lam3": f(inputs["diff_lambda_k2"])[0],
            "dnw": f(inputs["diff_norm_w"])[0],
            "tabA": tabA, "tabB": tabB, "tab0": tab0,
            "w_out": f(inputs["w_out"])[0], "w_up": f(inputs["w_up"])[0],
            "cwp": np.ascontiguousarray(cwv.reshape(3, 44, 128).transpose(2, 0, 1).reshape(128, 132)),
            "cbp": np.ascontiguousarray(conv_b.reshape(44, 128).T),
            "w_down": f(inputs["w_down"])[0],
            "identf": np.eye(128, dtype=np.float32),
            "tri0": tri[0], "tri1": tri[1], "tri2": tri[2], "tri3": tri[3], "cind": cind,
            "mask_up": m_up.astype(np.float32), "mask_dn": m_dn.astype(np.float32),
        }
        maps.append({k: np.ascontiguousarray(v) for k, v in m.items()})
    return maps


_NC = {}


def kernel(**inputs):
    if "nc" not in _NC:
        _NC["nc"] = build_program()
    nc = _NC["nc"]
    maps = make_in_maps(inputs)
    res = run_bass_kernel_spmd(nc, maps, core_ids=list(range(8)))
    out = np.empty((4, L, D), np.float32)
    for core in range(8):
        b, hf = core // 2, core % 2
        yl = np.asarray(res.results[core]["y"])
        if hf == 0:
            out[b, :4096] = yl
        else:
            out[b, 4096:] = yl[::-1]
    return out
```

```python
import math
import numpy as np
import ml_dtypes
from contextlib import ExitStack
import concourse.bass as bass
import concourse.mybir as mybir
from concourse.bass_utils import run_bass_kernel_spmd

F32 = mybir.dt.float32
BF16 = mybir.dt.bfloat16
I32 = mybir.dt.int32
AF = mybir.ActivationFunctionType
ALU = mybir.AluOpType
AX = mybir.AxisListType

D = 1024
L = 8192
NT = 64
NOWN = 32
NQT = 33
TQ = NQT * 128
DFF = 2816
NFC = 22
EPS = 1e-6
QG, KG, ZUP, ZDN, VG, RG, QD, VD, KD = 0, 256, 512, 768, 1024, 1536, 2048, 2560, 3072
WALLC = 3584

def _t5_bucket_np(rel):
    half, max_exact = 16, 8
    ret = np.where(rel > 0, half, 0)
    n = np.abs(rel)
    nf = np.maximum(n, 1).astype(np.float32)
    lg = (np.log(nf / np.float32(max_exact)) / np.float32(math.log(128 / max_exact))
          * np.float32(half - max_exact)).astype(np.float32)
    large = max_exact + lg.astype(np.int32)
    large = np.minimum(large, half - 1)
    return ret + np.where(n < max_exact, n, large)


def _t5_breaks():
    rel = np.arange(-400, 401)
    b = _t5_bucket_np(rel)
    out = []
    for i in range(1, len(rel)):
        if b[i] != b[i - 1]:
            out.append((int(rel[i]), int(b[i]), int(b[i - 1])))
    return out, int(b[0])


T5_BREAKS, T5_B0 = _t5_breaks()
NBRK = len(T5_BREAKS)


class Buf:
    __slots__ = ("name", "w", "r")

    def __init__(self, name):
        self.name = name
        self.w = {}
        self.r = {}


class Op:
    __slots__ = ("eng", "fn", "idx", "deps", "dma")

    def __init__(self, eng, fn, idx):
        self.eng = eng
        self.fn = fn
        self.idx = idx
        self.deps = {}
        self.dma = None


CENG = ("pe", "act", "dve", "pool")
ALLE = ("pe", "act", "dve", "pool", "sp")


class Sched:
    def __init__(self, rings):
        self.ops = {e: [] for e in ALLE}
        self.rings = rings
        self.ring_pos = {q: 0 for q in rings}
        self.ring_val = {q: [0] * n for q, n in rings.items()}

    def _deps(self, eng, r, w, dma):
        deps = {}

        def add(k, v):
            if deps.get(k, 0) < v:
                deps[k] = v
        for b in r:
            for k, v in b.w.items():
                if (not dma) and k == eng and eng == "pe":
                    continue
                add(k, v)
        for b in w:
            for k, v in b.w.items():
                if (not dma) and k == eng:
                    continue
                add(k, v)
            for k, v in b.r.items():
                if (not dma) and k == eng:
                    continue
                add(k, v)
        return deps

    def op(self, eng, fn, r=(), w=(), after_prev=False):
        o = Op(eng, fn, len(self.ops[eng]) + 1)
        o.deps = self._deps(eng, r, w, False)
        if after_prev and o.idx > 1:
            o.deps[eng] = max(o.deps.get(eng, 0), o.idx - 1)
        for b in w:
            b.w = {eng: o.idx}
            b.r = {}
        for b in r:
            b.r[eng] = o.idx
        self.ops[eng].append(o)
        return o

    def dma(self, q, fn, r=(), w=(), disjoint=()):
        i = self.ring_pos[q]
        self.ring_pos[q] = (i + 1) % self.rings[q]
        prev = self.ring_val[q][i]
        self.ring_val[q][i] = prev + 16
        key = ("dma", q, i)
        o = Op(q, fn, len(self.ops[q]) + 1)
        o.deps = self._deps(q, r, w, True)
        if prev > 0:
            o.deps[key] = max(o.deps.get(key, 0), prev)
        o.dma = key
        for b in w:
            b.w = {key: prev + 16}
            b.r = {}
        for b in disjoint:
            b.w[key] = prev + 16
        for b in r:
            b.r[key] = prev + 16
        self.ops[q].append(o)
        return o

    def barrier(self):
        last = {e: len(self.ops[e]) for e in CENG}
        dl = {}
        for q, vals in self.ring_val.items():
            for i, v in enumerate(vals):
                if v > 0:
                    dl[("dma", q, i)] = v
        for e in ALLE:
            o = Op(e, lambda eng: eng.nop(), len(self.ops[e]) + 1)
            for k, v in last.items():
                if k != e and v > 0 and not self.ops[k][v - 1].dma:
                    o.deps[k] = v
                elif k != e and v > 0:
                    j = v
                    while j > 0 and self.ops[k][j - 1].dma:
                        j -= 1
                    if j > 0:
                        o.deps[k] = j
            o.deps.update(dl)
            self.ops[e].append(o)

    def emit(self, nc, stack):
        need = {e: set() for e in CENG}
        for e in ALLE:
            for o in self.ops[e]:
                for k, v in o.deps.items():
                    if isinstance(k, str):
                        need[k].add(v)
        ms = {e: {idx: i + 1 for i, idx in enumerate(sorted(need[e]))} for e in CENG}
        self.stats = {e: (len(self.ops[e]), len(ms.get(e, ()))) for e in ALLE}
        csem = {e: stack.enter_context(nc.semaphore("c_" + e)) for e in CENG}
        dsem = {}
        for q, n in self.rings.items():
            for i in range(n):
                dsem[("dma", q, i)] = stack.enter_context(nc.semaphore("d_%s_%d" % (q, i)))
        final = {}
        for q, vals in self.ring_val.items():
            for i, v in enumerate(vals):
                if v > 0:
                    final[("dma", q, i)] = v
        block = stack.enter_context(nc.Block())

        def run(ename):
            def body(eng):
                seen = {}
                for o in self.ops[ename]:
                    for k, v in o.deps.items():
                        if isinstance(k, str):
                            val = ms[k][v]
                            sem = csem[k]
                        else:
                            val = v
                            sem = dsem[k]
                        if seen.get(k, 0) >= val:
                            continue
                        eng.wait_ge(sem, val)
                        seen[k] = val
                    ins = o.fn(eng)
                    if o.dma is not None:
                        ins.then_inc(dsem[o.dma], 16)
                    elif ename in ms and o.idx in ms[ename]:
                        ins.then_inc(csem[ename], 1)
                if ename == "sp":
                    for k, v in final.items():
                        if seen.get(k, 0) < v:
                            eng.wait_ge(dsem[k], v)
            return body
        block.tensor(run("pe"))
        block.scalar(run("act"))
        block.vector(run("dve"))
        block.gpsimd(run("pool"))
        block.sync(run("sp"))


def build_program(phases=5, nt1=NT, nt2=NQT, nheads=4, dbg=False):
    nc = bass.Bass("TRN2", target_bir_lowering=False)
    S = Sched({"sp": 8, "pool": 4})
    stack = ExitStack()

    def din(name, shape, dt=F32):
        return nc.dram_tensor(name, list(shape), dt, kind="ExternalInput").ap()

    x = din("x", [L, D])
    pos = din("pos", [L], I32)
    ccol = din("ccol", [128, 8])
    w_ada = din("w_ada", [D, 6 * D])
    b_ada = din("b_ada", [1, 6 * D])
    anw = din("anw", [1, D])
    fnw = din("fnw", [1, D])
    finw = din("finw", [D])
    w_in = din("w_in", [D, 3104])
    decw = [din("decw_up", [16, 256]), din("decw_dn", [16, 256])]
    decb = [din("decb_up", [1, 256]), din("decb_dn", [1, 256])]
    gnw = din("gnw", [512])
    lamv = [din("lam%d" % i, [64]) for i in range(4)]
    dnw = din("dnw", [128])
    tabA = din("tabA", [4, NBRK])
    tabB = din("tabB", [4, NBRK])
    tab0 = din("tab0", [4, 1])
    w_out = din("w_out", [D, D])
    w_up = din("w_up", [D, 2 * DFF])
    cwp = din("cwp", [128, 3 * 44])
    cbp = din("cbp", [128, 44])
    w_down = din("w_down", [DFF, D])
    identf_d = din("identf", [128, 128])
    tri_d = [din("tri%d" % i, [128, 128]) for i in range(4)]
    cind_d = din("cind", [128, 2])
    mask_d = [din("mask_up", [128, 128]), din("mask_dn", [128, 128])]
    y = nc.dram_tensor("y", [NOWN * 128, D], F32, kind="ExternalOutput").ap()
    KdT = nc.dram_tensor("KdT", [4, 128, L], BF16, kind="Internal").ap()
    Vd = nc.dram_tensor("Vd", [4, 128, NT, 129], BF16, kind="Internal").ap()
    QdT = nc.dram_tensor("QdT", [4, 128, TQ], BF16, kind="Internal").ap()
    X1 = nc.dram_tensor("X1", [NOWN * 128, D], F32, kind="Internal").ap()
    H2T = nc.dram_tensor("H2T", [8, 128, TQ + 2], BF16, kind="Internal").ap()
    SdnD = nc.dram_tensor("SdnD", [2 * NQT, 128, 256], BF16, kind="Internal").ap()
    CatT = nc.dram_tensor("CatT", [8, 128, TQ], BF16, kind="Internal").ap()
    dKdT, dVd, dQdT, dX1, dH2T = Buf("KdT"), Buf("Vd"), Buf("QdT"), Buf("X1"), Buf("H2T")
    dSdn, dCat = Buf("SdnD"), Buf("CatT")
    _dscr = [dKdT, dVd, dQdT, dX1, dH2T, dSdn, dCat]

    def sb(name, shape, dt=F32):
        return stack.enter_context(nc.sbuf_tensor("s_" + name, list(shape), dt))

    ps = stack.enter_context(nc.psum_tensor("ps", [128, 4096], F32))
    PB = [Buf("pb%d" % i) for i in range(8)]

    def bank(i, n=1):
        return ps[:, i * 512:(i + n) * 512]

    def bankb(i):
        return ps[:, i * 512:(i + 1) * 512].bitcast(BF16)

    pe_mode = [None]

    def _mode(ap):
        sh = ap.shape
        k = int(sh[0]); m = int(np.prod(sh[1:]))
        rt = 32 if k <= 32 else (64 if k <= 64 else 128)
        ct = 32 if m <= 32 else (64 if m <= 64 else 128)
        return (rt, ct)

    def _switch(ap):
        md = _mode(ap)
        sw = pe_mode[0] is not None and md != pe_mode[0]
        pe_mode[0] = md
        return sw

    def MM(out, lhsT, rhs, start=True, stop=True, r=(), w=(), sync=False, **kw):
        sync = _switch(lhsT) or sync
        S.op("pe", lambda e: e.matmul(out, lhsT, rhs, start=start, stop=stop, **kw), r, w, after_prev=sync)

    def TR(out, in_, ident, r=(), w=()):
        S.op("pe", lambda e: e.transpose(out, in_, ident), r, w, after_prev=_switch(in_))

    def ACT(out, in_, func, r=(), w=(), **kw):
        S.op("act", lambda e: e.activation(out, in_, func, **kw), r, w)

    def TT(eng, out, in0, in1, op, r=(), w=()):
        S.op(eng, lambda e: e.tensor_tensor(out, in0, in1, op), r, w)

    def TS(eng, out, in0, s1, s2, op0, op1=None, r=(), w=(), **kw):
        if op1 is None:
            S.op(eng, lambda e: e.tensor_scalar(out, in0, s1, s2, op0, **kw), r, w)
        else:
            S.op(eng, lambda e: e.tensor_scalar(out, in0, s1, s2, op0, op1, **kw), r, w)

    def STT(eng, out, in0, sc, in1, op0, op1, r=(), w=(), **kw):
        S.op(eng, lambda e: e.scalar_tensor_tensor(out, in0, sc, in1, op0, op1, **kw), r, w)

    def CP(eng, out, in_, r=(), w=()):
        if eng == "act":
            S.op("act", lambda e: e.copy(out, in_), r, w)
        else:
            S.op(eng, lambda e: e.tensor_copy(out, in_), r, w)

    def MSET(eng, ap, val, w=()):
        S.op(eng, lambda e: e.memset(ap, val), (), w)

    DSCR = set(id(b) for b in _dscr)

    def DMA(q, out, in_, r=(), w=()):
        wn = [b for b in w if id(b) not in DSCR]
        wd = [b for b in w if id(b) in DSCR]
        S.dma(q, lambda e: e.dma_start(out=out, in_=in_), r, wn, wd)

    identf = sb("identf", [128, 128]); b_identf = Buf("identf")
    identb = sb("identb", [128, 128], BF16); b_identb = Buf("identb")
    tri = [sb("tri%d" % i, [128, 128]) for i in range(4)]; b_tri = Buf("tri")
    cind = sb("cind", [128, 2])
    maskt = [sb("mask%d" % i, [128, 128]) for i in range(2)]
    modcol = sb("modcol", [128, 32]); b_modcol = Buf("modcol")
    GA = sb("GA", [128, D]); b_GA = Buf("GA")
    GF = sb("GF", [128, D]); b_GF = Buf("GF")
    WF = sb("WF", [128, D]); b_WF = Buf("WF")
    gnwb = sb("gnwb", [128, 512]); b_cst = Buf("cst")
    dnwb = sb("dnwb", [128, 128])
    lamc = sb("lamc", [128, 4]); b_lam = Buf("lam")
    onesK = sb("onesK", [128, 128], BF16)
    decbK = sb("decbK", [128, 512], BF16)
    onesf = sb("onesf", [1, 128])
    b_decbb = Buf("decbb")
    zerob = sb("zerob", [128, 512], BF16); b_zero = Buf("zero")
    cw = sb("cw", [128, 3 * 44]); cb = sb("cb", [128, 44]); b_cw = Buf("cw")
    ARENA = 180 * 1024
    arena = sb("arena", [128, ARENA // 2], BF16)
    apos = [0]

    def carve(shape, dt=F32):
        n = int(np.prod(shape[1:])) * (4 if dt in (F32, I32) else 2)
        n = (n + 63) // 64 * 64
        a = apos[0]
        assert a + n <= ARENA, ("arena overflow", a, n)
        apos[0] = a + n
        v = arena[0:shape[0], a // 2:(a + n) // 2]
        if dt != BF16:
            v = v.bitcast(dt)
        v = v[:, 0:int(np.prod(shape[1:]))]
        if len(shape) == 3:
            v = v.rearrange("p (a b) -> p a b", a=shape[1])
        elif len(shape) == 4:
            v = v.rearrange("p (a b c) -> p a b c", a=shape[1], b=shape[2])
        return v

    b_c0 = Buf("c0")
    DMA("sp", identf[:], identf_d, w=[b_identf])
    b_tris = [Buf("tri%d" % i) for i in range(4)]
    for i in range(4):
        DMA("sp", tri[i][:], tri_d[i], w=[b_tris[i]])
    b_tri2 = Buf("tri2")
    DMA("sp", cind[:], cind_d, w=[b_tri2])
    b_mask = [Buf("mask0"), Buf("mask1")]
    for i in range(2):
        DMA("sp", maskt[i][:], mask_d[i], w=[b_mask[i]])
    DMA("sp", WF[:], finw.partition_broadcast(128), w=[b_WF])
    DMA("sp", gnwb[:], gnw.partition_broadcast(128), w=[b_cst])
    b_dnw = Buf("dnw")
    DMA("sp", dnwb[:], dnw.partition_broadcast(128), w=[b_dnw])
    CP("dve", identb[:], identf[:], r=[b_identf], w=[b_identb])
    MSET("dve", onesK[:], 0.0, w=[b_c0])
    MSET("dve", onesK[0:1, :], 1.0, w=[b_c0])
    MSET("dve", decbK[:], 0.0, w=[b_decbb])
    MSET("dve", onesf[:], 1.0, w=[b_c0])
    MSET("dve", zerob[:], 0.0, w=[b_zero])
    TS("dve", dnwb[:], dnwb[:], 0.8, None, ALU.mult, r=[b_dnw], w=[b_dnw])

    strips = carve([128, 4, 11 * 128]); b_strips = [Buf("strip%d" % h) for h in range(4)]
    a0 = apos[0]
    KB = 1024
    ccs = carve([128, 8]); b_ccs = Buf("ccs")
    tmp8 = carve([128, 8]); b_tmp8 = Buf("tmp8")
    silc = carve([128, 8]); b_silc = Buf("silc")
    DMA("sp", ccs, ccol, w=[b_ccs])
    ACT(tmp8, ccs, AF.Exp, scale=-1.0, r=[b_ccs], w=[b_tmp8])
    TS("dve", tmp8, tmp8, 1.0, None, ALU.add, r=[b_tmp8], w=[b_tmp8])
    S.op("dve", lambda e: e.reciprocal(tmp8, tmp8), [b_tmp8], [b_tmp8])
    TT("dve", silc, ccs, tmp8, ALU.mult, r=[b_ccs, b_tmp8], w=[b_silc])
    lv = carve([128, 4, 64]); b_lv = Buf("lv")
    for i in range(4):
        DMA("sp", lv[:, i, :], lamv[i].partition_broadcast(128), w=[Buf("lvx%d" % i)])
    lp = carve([128, 2, 64]); b_lp = Buf("lp")
    ls = carve([128, 2]); b_ls = Buf("ls")
    S.barrier()
    TT("dve", lp[:, 0, :], lv[:, 0, :], lv[:, 1, :], ALU.mult, r=[b_lv], w=[b_lp])
    TT("dve", lp[:, 1, :], lv[:, 2, :], lv[:, 3, :], ALU.mult, r=[b_lv], w=[b_lp])
    S.op("dve", lambda e: e.tensor_reduce(ls, lp, AX.X, ALU.add), [b_lp], [b_ls])
    ACT(ls, ls, AF.Exp, r=[b_ls], w=[b_ls])
    TT("dve", lamc[:, 0:1], ls[:, 0:1], ls[:, 1:2], ALU.subtract, r=[b_ls], w=[b_lam])
    TS("dve", lamc[:, 0:1], lamc[:, 0:1], 0.2, None, ALU.add, r=[b_lam], w=[b_lam])
    TS("dve", lamc[:, 1:2], lamc[:, 0:1], -1.0, None, ALU.mult, r=[b_lam], w=[b_lam])

    _save_apos = apos[0]
    apos[0] = 102 * KB
    posq_i = carve([128, 640], I32); b_posq = Buf("posq")
    relf = carve([128, 640]); b_relf = Buf("relf")
    posk_i = carve([128, 1], I32)
    posk_f = carve([128, 1]); b_posk = Buf("posk")
    tA = carve([128, 4, NBRK]); tB = carve([128, 4, NBRK]); t0 = carve([128, 4]); b_tab = Buf("tab")
    tmpS = carve([128, 640]); b_tmpS = Buf("tmpS")
    s5 = carve([128, 640]); b_s5 = Buf("s5")
    DMA("sp", posq_i, pos[0:640].partition_broadcast(128), w=[Buf("pq")])
    DMA("sp", posk_i, pos[256:384].rearrange("(p o) -> p o", o=1), w=[Buf("pk")])
    for h in range(4):
        DMA("sp", tA[:, h, :], tabA[h].partition_broadcast(128), w=[Buf("ta")])
        DMA("sp", tB[:, h, :], tabB[h].partition_broadcast(128), w=[Buf("tb")])
        DMA("sp", t0[:, h:h + 1], tab0[h].partition_broadcast(128), w=[Buf("t0")])
    assert apos[0] <= 124 * KB, apos[0]
    S.barrier()
    sq = []
    sq.append(lambda: CP("dve", relf, posq_i, r=[b_posq], w=[b_relf]))
    sq.append(lambda: CP("dve", posk_f, posk_i, r=[], w=[b_posk]))
    sq.append(lambda: TS("dve", relf, relf, posk_f[:, 0:1], -1.0, ALU.subtract, ALU.mult, r=[b_relf, b_posk], w=[b_relf]))
    sq.append(lambda: TT("dve", tA, tA, tB, ALU.subtract, r=[b_tab], w=[b_tab]))
    for h in range(nheads):
        sq.append(lambda h=h: TS("dve", s5, relf, 0.0, t0[:, h:h + 1], ALU.mult, ALU.add, r=[b_relf, b_tab], w=[b_s5]))
        for j, (rj, _, _) in enumerate(T5_BREAKS):
            sq.append(lambda h=h, j=j, rj=rj: TS("dve", tmpS, relf, float(rj) - 0.5, tA[:, h, j:j + 1], ALU.is_ge, ALU.mult,
                                                 r=[b_relf, b_tab], w=[b_tmpS]))
            sq.append(lambda: TT("dve", s5, s5, tmpS, ALU.add, r=[b_tmpS, b_s5], w=[b_s5]))
        sq.append(lambda h=h: CP("dve", strips[:, h, 0:512].rearrange("p (a b) -> p a b", a=4),
                                 s5[:, 0:128].unsqueeze(1).to_broadcast([128, 4, 128]), r=[b_s5], w=[b_strips[h]]))
        sq.append(lambda h=h: CP("dve", strips[:, h, 512:896], s5[:, 128:512], r=[b_s5], w=[b_strips[h]]))
        sq.append(lambda h=h: CP("dve", strips[:, h, 896:1408].rearrange("p (a b) -> p a b", a=4),
                                 s5[:, 512:640].unsqueeze(1).to_broadcast([128, 4, 128]), r=[b_s5], w=[b_strips[h]]))
    sq_per_blk = (len(sq) + 23) // 24

    apos[0] = _save_apos

    modrow = carve([1, 6 * D]); b_modrow = Buf("modrow")
    badar = [carve([1, 512]) for _ in range(2)]; b_badar = [Buf("badar0"), Buf("badar1")]
    wada = [carve([128, 8, 256]) for _ in range(2)]; b_wada = [Buf("wada0"), Buf("wada1")]
    w_ada_v = w_ada.rearrange("(k p) n -> p k n", p=128)
    for blk in range(24):
        sl = blk % 2
        cs = slice(blk * 256, (blk + 1) * 256)
        DMA("sp", wada[sl], w_ada_v[:, :, cs], w=[b_wada[sl]])
        DMA("sp", badar[sl][:, 0:256], b_ada[:, cs], w=[b_badar[sl]])
        for k in range(8):
            MM(bank(sl)[0:1, 0:256], silc[:, k:k + 1], wada[sl][:, k, :], start=(k == 0), stop=(k == 7),
               r=[b_silc, b_wada[sl]], w=[PB[sl]])
        TT("dve", modrow[:, cs], bank(sl)[0:1, 0:256], badar[sl][:, 0:256],
           ALU.add, r=[PB[sl], b_badar[sl]], w=[b_modrow])
        for _ in range(sq_per_blk):
            if sq:
                sq.pop(0)()
    while sq:
        sq.pop(0)()
    rows = carve([1, 2, D]); b_rows = Buf("rows")
    nwr = carve([1, 2, D]); b_nwr = Buf("nwr")
    DMA("sp", nwr[:, 0, :], anw, w=[Buf("nw0")])
    DMA("sp", nwr[:, 1, :], fnw, w=[b_nwr])
    S.barrier()
    STT("dve", rows[:, 0, :], modrow[:, D:2 * D], 1.0, nwr[:, 0, :], ALU.add, ALU.mult, r=[b_modrow, b_nwr], w=[b_rows])
    STT("dve", rows[:, 1, :], modrow[:, 4 * D:5 * D], 1.0, nwr[:, 1, :], ALU.add, ALU.mult, r=[b_modrow, b_nwr], w=[b_rows])
    srcs = [rows[:, 0, :], modrow[:, 0:D], rows[:, 1, :], modrow[:, 3 * D:4 * D]]
    for v in range(4):
        for j in range(8):
            MM(bank(2)[:, v * 8 + j:v * 8 + j + 1], srcs[v][:, j * 128:(j + 1) * 128], identf[0:1, 0:1],
               r=[b_rows, b_modrow, b_identf], w=[PB[2]])
    CP("dve", modcol[:], bank(2)[:, 0:32], r=[PB[2]], w=[b_modcol])
    for (G, bG, off) in ((GA, b_GA, 2 * D), (GF, b_GF, 5 * D)):
        for hh in range(2):
            MM(bank(3 + hh), onesf[0:1, :], modrow[:, off + hh * 512:off + (hh + 1) * 512],
               r=[b_c0, b_modrow], w=[PB[3 + hh]])
            CP("dve", G[:, hh * 512:(hh + 1) * 512], bank(3 + hh), r=[PB[3 + hh]], w=[bG])
    dbs = carve([1, 512]); b_dbs = Buf("dbs")
    DMA("sp", dbs[:, 0:256], decb[0], w=[Buf("dbx")])
    DMA("sp", dbs[:, 256:512], decb[1], w=[b_dbs])
    lrs = carve([128, 8, 32]); b_lrs = Buf("lrs")
    w_in_v = w_in.rearrange("(k p) n -> p k n", p=128)
    DMA("sp", lrs, w_in_v[:, :, 1536:1568], w=[b_lrs])
    dws = carve([16, 2, 256]); b_dws = Buf("dws")
    DMA("sp", dws[:, 0, :], decw[0], w=[Buf("dwx")])
    DMA("sp", dws[:, 1, :], decw[1], w=[b_dws])
    lrT = carve([16, 2, D]); b_lrT = Buf("lrT")
    S.barrier()
    CP("dve", decbK[0:1, :], dbs, r=[b_dbs], w=[b_decbb])
    assert apos[0] <= 102 * KB, apos[0]
    apos[0] = 124 * KB
    Wall = carve([128, 8, WALLC], BF16); b_Wall = Buf("Wall")
    S.barrier()
    for (dst, src) in ((QG, 0), (VG, 512), (RG, 1024), (QD, 1568), (KD, 2080), (VD, 2592)):
        DMA("pool", Wall[:, :, dst:dst + 512], w_in_v[:, :, src:src + 512], w=[Buf("wl%d" % dst)])
    S.barrier()
    for d in range(2):
        for k in range(8):
            TR(bank(5 + k // 4)[0:16, (k % 4) * 128:(k % 4 + 1) * 128], lrs[:, k, d * 16:(d + 1) * 16], identf[:],
               r=[b_lrs, b_identf], w=[PB[5 + k // 4]])
        CP("dve", lrT[:, d, 0:512], bank(5)[0:16, :], r=[PB[5]], w=[b_lrT])
        CP("dve", lrT[:, d, 512:1024], bank(6)[0:16, :], r=[PB[6]], w=[b_lrT])
        for k in range(8):
            bk = 5 + (k % 2)
            MM(bank(bk)[:, 0:256], lrT[:, d, k * 128:(k + 1) * 128], dws[:, d, :], r=[b_lrT, b_dws], w=[PB[bk]])
            CP("dve", Wall[:, k, ZUP + d * 256:ZUP + (d + 1) * 256], bank(bk)[:, 0:256], r=[PB[bk]], w=[b_Wall])
    S.barrier()

    apos[0] = a0
    xts = [carve([128, D]) for _ in range(2)]; b_xt = [Buf("xt0"), Buf("xt1")]
    junk = carve([128, D], BF16); b_junk = Buf("junk")
    xn = carve([128, D], BF16); b_xn = Buf("xn")
    st2 = carve([128, 2]); b_st = Buf("st")
    hT = [carve([128, 8, 128], BF16) for _ in range(2)]; b_hT = [Buf("hT0"), Buf("hT1")]
    cnt = {"x": 0}

    def front_end(xsrc, colbase, tbank, rd=(), out=None):
        i = cnt["x"] % 2
        cnt["x"] += 1
        h_out, h_buf = (hT[i], b_hT[i]) if out is None else out
        ACT(junk, xsrc, AF.Square, accum_out=st2[:, 0:1], r=list(rd), w=[b_junk, b_st])
        TS("dve", st2[:, 1:2], st2[:, 0:1], 1.0 / D, EPS, ALU.mult, ALU.add, r=[b_st], w=[b_st])
        ACT(st2[:, 1:2], st2[:, 1:2], AF.Ln, r=[b_st], w=[b_st])
        ACT(st2[:, 1:2], st2[:, 1:2], AF.Exp, scale=-0.5, r=[b_st], w=[b_st])
        TS("dve", xn, xsrc, st2[:, 1:2], None, ALU.mult, r=list(rd) + [b_st], w=[b_xn])
        tb = bankb(tbank)
        for j in range(8):
            TR(tb[:, j * 128:(j + 1) * 128], xn[:, j * 128:(j + 1) * 128], identb[:], r=[b_xn, b_identb], w=[PB[tbank]])
        tv = tb[:, 0:1024].rearrange("p (a b) -> p a b", a=8)
        sc = modcol[:, colbase:colbase + 8].unsqueeze(2).to_broadcast([128, 8, 128])
        shf = modcol[:, colbase + 8:colbase + 16].unsqueeze(2).to_broadcast([128, 8, 128])
        TT("dve", h_out, tv, sc, ALU.mult, r=[PB[tbank], b_modcol], w=[h_buf])
        TT("dve", h_out, h_out, shf, ALU.add, r=[h_buf, b_modcol], w=[h_buf])
        return h_out, h_buf

    def proj_tok(bk, h_ap, h_b, c0, n=512, extra=None):
        for k in range(8):
            MM(bank(bk)[:, 0:n], h_ap[:, k, :], Wall[:, k, c0:c0 + n], start=(k == 0), stop=(k == 7 and extra is None),
               r=[h_b, b_Wall], w=[PB[bk]])
        if extra is not None:
            extra()

    Sst = [carve([128, 2, 128]) for _ in range(2)]; b_S = [Buf("Sup"), Buf("Sdn")]
    sdn_t = [carve([128, 2, 2, 128], BF16) for _ in range(2)]; b_sdn = [Buf("sdn0"), Buf("sdn1")]
    Sup_b = [carve([128, 2, 128], BF16) for _ in range(2)]; b_Sub = [Buf("Sub0"), Buf("Sub1")]
    MSET("dve", Sst[0], 0.0, w=[b_S[0]])
    MSET("dve", Sst[1], 0.0, w=[b_S[1]])
    u_t = carve([128, 512]); b_u = Buf("u")
    ex_t = carve([128, 512]); b_ex = Buf("ex")
    est = carve([128, 256]); b_est = Buf("est")
    kst = carve([128, 256], BF16); b_kst = Buf("kst")
    vgb = carve([128, 512], BF16); b_vgb = Buf("vgb")
    dec = carve([128, 2, 2]); b_dec = Buf("dec")
    vaug = [carve([128, 4, 129], BF16) for _ in range(2)]; b_vaug = [Buf("va0"), Buf("va1")]
    kTs = [carve([128, 4, 128], BF16) for _ in range(2)]; b_kTs = [Buf("kT0"), Buf("kT1")]
    for i in range(2):
        MSET("dve", vaug[i], 1.0, w=[b_vaug[i]])

    def gla_state(d, ksrc, ksrc_b, ucols, t, store, ubank2=5):
        tS = tri[1] if d == 0 else tri[3]
        MM(bank(6)[:, 0:256], tS[:], u_t[:, ucols], r=[b_tris[1], b_tris[3], b_u], w=[PB[6]])
        for p in range(2):
            MM(bank(6)[:, 256 + 2 * p:258 + 2 * p], u_t[:, ucols.start + p * 128:ucols.start + (p + 1) * 128], cind[:],
               r=[b_u, b_tri2], w=[PB[6]])
        ACT(est, bank(6)[:, 0:256], AF.Exp, r=[PB[6]], w=[b_est])
        ACT(dec.rearrange("p a b -> p (a b)"), bank(6)[:, 256:260], AF.Exp, r=[PB[6]], w=[b_dec])
        TT("dve", kst, ksrc, est, ALU.mult, r=[ksrc_b, b_est], w=[b_kst])
        updb = [bank(7).rearrange("p (a c) -> p a c", a=4), bank(ubank2).rearrange("p (a c) -> p a c", a=4)]
        ubuf = [PB[7], PB[ubank2]]
        for n in range(2):
            for h in range(4):
                MM(updb[n][64 * (h % 2):64 * (h % 2) + 64, h // 2, :], kst[64 * n:64 * n + 64, h * 64:(h + 1) * 64],
                   vgb[64 * n:64 * n + 64, h * 128:(h + 1) * 128], r=[b_kst, b_vgb], w=[ubuf[n]])
        order = (0, 1) if d == 0 else (1, 0)
        for n in order:
            ch = 2 * t + n
            if d == 1 and store:
                CP("act", sdn_t[t % 2][:, n, :, :], Sst[1], r=[b_S[1]], w=[b_sdn[t % 2]])
            if d == 0:
                CP("act", Sup_b[n], Sst[0], r=[b_S[0]], w=[b_Sub[n]])
            for p in range(2):
                STT("dve", Sst[d][:, p, :], Sst[d][:, p, :], dec[:, p, n:n + 1], updb[n][:, p, :], ALU.mult, ALU.add,
                    r=[b_S[d], b_dec, ubuf[n]], w=[b_S[d]])

    def fe_x(t):
        i = cnt["x"] % 2
        DMA("sp", xts[i], x[t * 128:(t + 1) * 128, :], w=[b_xt[i]])
        return front_end(xts[i], 0, 0, rd=[b_xt[i]])

    def ph1_proj(t, h_ap, h_b):
        for k in range(8):
            MM(bank(1)[:, 0:256], h_ap[:, k, :], Wall[:, k, KG:KG + 256], start=(k == 0), stop=(k == 7), r=[h_b, b_Wall], w=[PB[1]])
        for k in range(8):
            MM(bank(1)[:, 256:512], h_ap[:, k, :], Wall[:, k, ZDN:ZDN + 256], start=(k == 0), stop=False, r=[h_b, b_Wall], w=[PB[1]])
        MM(bank(1)[:, 256:512], onesK[:], decbK[:, 256:512], start=False, stop=True, r=[b_c0, b_decbb], w=[PB[1]])
        proj_tok(2, h_ap, h_b, VG)
        proj_tok(3, h_ap, h_b, VD)
        for c in range(4):
            for k in range(8):
                MM(bank(4)[:, c * 128:(c + 1) * 128], Wall[:, k, KD + c * 128:KD + (c + 1) * 128], h_ap[:, k, :],
                   start=(k == 0), stop=(k == 7), r=[h_b, b_Wall], w=[PB[4]])

    def ph1_rest(t):
        own = t < NQT
        ACT(ex_t[:, 0:256], bank(1)[:, 256:512], AF.Exp, scale=-1.0, r=[PB[1]], w=[b_ex])
        ACT(u_t[:, 256:512], ex_t[:, 0:256], AF.Ln, bias=1.0, r=[b_ex], w=[b_u])
        CP("act", vgb, bank(2), r=[PB[2]], w=[b_vgb])
        vi = (t % 2)
        CP("dve", vaug[vi][:, :, 0:128], bank(3).rearrange("p (a b) -> p a b", a=4), r=[PB[3]], w=[b_vaug[vi]])
        DMA("sp", Vd[:, :, t, :].rearrange("h p c -> p h c"), vaug[vi], r=[b_vaug[vi]], w=[dVd])
        CP("act", kTs[vi], bank(4).rearrange("p (a b) -> p a b", a=4), r=[PB[4]], w=[b_kTs[vi]])
        DMA("sp", KdT[:, :, t * 128:(t + 1) * 128].rearrange("h p c -> p h c"), kTs[vi], r=[b_kTs[vi]], w=[dKdT])
        gla_state(1, bank(1)[:, 0:256], PB[1], slice(256, 512), t, own)
        if own:
            DMA("sp", SdnD[2 * t:2 * t + 2].rearrange("n p c -> p n c"),
                sdn_t[t % 2].rearrange("p n a b -> p n (a b)"), r=[b_sdn[t % 2]], w=[dSdn])

    tl1 = list(range(NT - 1, NT - 1 - nt1, -1)) if phases >= 1 else []
    if tl1:
        hcur = fe_x(tl1[0])
    for ii, t in enumerate(tl1):
        ph1_proj(t, *hcur)
        hnext = fe_x(tl1[ii + 1]) if ii + 1 < len(tl1) else None
        ph1_rest(t)
        hcur = hnext

    qk_s = carve([128, 512]); b_qk = Buf("qk")
    sr_s = carve([128, 512]); b_sr = Buf("sr")
    sr_t = carve([128, 512]); b_srt = Buf("srt")
    Eb = carve([128, 512]); b_Eb = Buf("Eb")
    Enb = carve([128, 512]); b_Enb = Buf("Enb")
    qkin = carve([128, 2, 2, 256], BF16); b_qkin = Buf("qkin")
    qkT = carve([128, 8, 128], BF16); b_qkT = Buf("qkT")
    attm = [carve([128, 4, 128], BF16) for _ in range(2)]; b_attm = [Buf("attm0"), Buf("attm1")]
    osq = carve([128, 512]); b_osq = Buf("osq")
    ost = carve([128, 8]); b_ost = Buf("ost")
    oab = carve([128, 512], BF16); b_oab = Buf("oab")
    oaf = carve([128, 512]); b_oaf = Buf("oaf")
    qTs = [carve([128, 4, 128], BF16) for _ in range(2)]; b_qTs = [Buf("qT0"), Buf("qT1")]
    oaTt = [carve([128, 4, 128], BF16) for _ in range(2)]; b_oaTt = [Buf("oaTt0"), Buf("oaTt1")]
    assert apos[0] <= 102 * KB, apos[0]
    TT("dve", sr_t, gnwb[:], gnwb[:], ALU.mult, r=[b_cst], w=[b_srt])
    def ph2_fe(t):
        i = cnt["x"] % 2
        DMA("sp", xts[i], x[t * 128:(t + 1) * 128, :], w=[b_xt[i]])
        DMA("sp", sdn_t[t % 2].rearrange("p n a b -> p n (a b)"), SdnD[2 * t:2 * t + 2].rearrange("n p c -> p n c"),
            r=[dSdn], w=[b_sdn[t % 2]])
        return front_end(xts[i], 0, 0, rd=[b_xt[i]])

    def ph2_proj(t, h_ap, h_b):
        proj_tok(1, h_ap, h_b, QG)
        for k in range(8):
            MM(bank(2), h_ap[:, k, :], Wall[:, k, ZUP:ZUP + 512], start=(k == 0), stop=False, r=[h_b, b_Wall], w=[PB[2]])
        MM(bank(2), onesK[:], decbK[:], start=False, stop=True, r=[b_c0, b_decbb], w=[PB[2]])
        proj_tok(3, h_ap, h_b, VG)
        proj_tok(4, h_ap, h_b, RG)
        for c in range(4):
            for k in range(8):
                MM(bank(5)[:, c * 128:(c + 1) * 128], Wall[:, k, QD + c * 128:QD + (c + 1) * 128], h_ap[:, k, :],
                   start=(k == 0), stop=(k == 7), r=[h_b, b_Wall], w=[PB[5]])

    def ph2_rest(t):
        CP("act", qk_s, bank(1), r=[PB[1]], w=[b_qk])
        ACT(ex_t, bank(2), AF.Exp, scale=-1.0, r=[PB[2]], w=[b_ex])
        ACT(u_t, ex_t, AF.Ln, bias=1.0, r=[b_ex], w=[b_u])
        CP("act", vgb, bank(3), r=[PB[3]], w=[b_vgb])
        ACT(sr_s, bank(4), AF.Exp, scale=-1.0, r=[PB[4]], w=[b_sr])
        TS("dve", sr_s, sr_s, 1.0, None, ALU.add, r=[b_sr], w=[b_sr])
        S.op("dve", lambda e: e.reciprocal(sr_s, sr_s), [b_sr], [b_sr])
        TT("dve", sr_s, sr_s, bank(4), ALU.mult, r=[b_sr, PB[4]], w=[b_sr])
        TT("dve", sr_t, sr_s, gnwb[:], ALU.mult, r=[b_sr, b_cst], w=[b_srt])
        qi = t % 2
        CP("act", qTs[qi], bank(5).rearrange("p (a b) -> p a b", a=4), r=[PB[5]], w=[b_qTs[qi]])
        DMA("sp", QdT[:, :, t * 128:(t + 1) * 128].rearrange("h p c -> p h c"), qTs[qi], r=[b_qTs[qi]], w=[dQdT])
        MM(bank(1)[:, 0:256], tri[0][:], u_t[:, 0:256], r=[b_tris[0], b_u], w=[PB[1]])
        MM(bank(1)[:, 256:512], tri[2][:], u_t[:, 256:512], r=[b_tris[2], b_u], w=[PB[1]])
        ACT(Eb, bank(1), AF.Exp, r=[PB[1]], w=[b_Eb])
        ACT(Enb, bank(1), AF.Exp, scale=-1.0, r=[PB[1]], w=[b_Enb])
        for d in range(2):
            STT("dve", qkin[:, d, 0, :], qk_s[:, 0:256], 0.125, Eb[:, d * 256:(d + 1) * 256], ALU.mult, ALU.mult,
                r=[b_qk, b_Eb], w=[b_qkin])
            TT("dve", qkin[:, d, 1, :], qk_s[:, 256:512], Enb[:, d * 256:(d + 1) * 256], ALU.mult, r=[b_qk, b_Enb], w=[b_qkin])
        tb = bankb(2)
        for d in range(2):
            for qk in range(2):
                for p in range(2):
                    j = d * 4 + qk * 2 + p
                    TR(tb[:, j * 128:(j + 1) * 128], qkin[:, d, qk, p * 128:(p + 1) * 128], identb[:], r=[b_qkin, b_identb], w=[PB[2]])
        CP("act", qkT, tb[:, 0:1024].rearrange("p (a b) -> p a b", a=8), r=[PB[2]], w=[b_qkT])
        for d in range(2):
            for h in range(4):
                p, o = h // 2, 64 * (h % 2)
                MM(bank(3 + h % 2)[:, (d * 2 + p) * 128:(d * 2 + p + 1) * 128], qkT[o:o + 64, d * 4 + 2 + p, :],
                   qkT[o:o + 64, d * 4 + p, :], r=[b_qkT], w=[PB[3 + h % 2]])
        for d in range(2):
            for par in range(2):
                TT("dve", attm[d][:, par::2, :], bank(3 + par)[:, d * 256:(d + 1) * 256].rearrange("p (a b) -> p a b", a=2),
                   maskt[d][:].unsqueeze(1).to_broadcast([128, 2, 128]), ALU.mult, r=[PB[3 + par], b_mask[d]], w=[b_attm[d]])
        gla_state(0, qk_s[:, 256:512], b_qk, slice(0, 256), t, False)
        ob5 = bank(5).rearrange("p (a b) -> p a b", a=4)
        for h in range(4):
            p, o = h // 2, 64 * (h % 2)
            MM(ob5[:, h, :], attm[0][:, h, :], vgb[:, h * 128:(h + 1) * 128], start=True, stop=False,
               r=[b_attm[0], b_vgb], w=[PB[5]])
            MM(ob5[:, h, :], attm[1][:, h, :], vgb[:, h * 128:(h + 1) * 128], start=False, stop=False,
               r=[b_attm[1], b_vgb], w=[PB[5]])
            for n in range(2):
                MM(ob5[64 * n:64 * n + 64, h, :], qkT[o:o + 64, p, 64 * n:64 * n + 64], Sup_b[n][o:o + 64, p, :],
                   start=False, stop=False, r=[b_qkT, b_Sub[n]], w=[PB[5]])
                MM(ob5[64 * n:64 * n + 64, h, :], qkT[o:o + 64, 4 + p, 64 * n:64 * n + 64],
                   sdn_t[t % 2][o:o + 64, n, p, :], start=False, stop=True, r=[b_qkT, b_sdn[t % 2]], w=[PB[5]])
        CP("act", oaf, bank(5), r=[PB[5]], w=[b_oaf])
        TT("dve", osq, oaf, oaf, ALU.mult, r=[b_oaf], w=[b_osq])
        S.op("dve", lambda e: e.tensor_reduce(ost[:, 0:4], osq.rearrange("p (a b) -> p a b", a=4), AX.X, ALU.add),
             [b_osq], [b_ost])
        TS("dve", ost[:, 4:8], ost[:, 0:4], 1.0 / 128, EPS, ALU.mult, ALU.add, r=[b_ost], w=[b_ost])
        ACT(ost[:, 4:8], ost[:, 4:8], AF.Ln, r=[b_ost], w=[b_ost])
        ACT(ost[:, 4:8], ost[:, 4:8], AF.Exp, scale=-0.5, r=[b_ost], w=[b_ost])
        TT("dve", oaf.rearrange("p (a b) -> p a b", a=4), oaf.rearrange("p (a b) -> p a b", a=4),
           ost[:, 4:8].unsqueeze(2).to_broadcast([128, 4, 128]), ALU.mult, r=[b_oaf, b_ost], w=[b_oaf])
        TT("dve", oab, oaf, sr_t, ALU.mult, r=[b_oaf, b_srt], w=[b_oab])
        tb = bankb(1)
        for h in range(4):
            TR(tb[:, h * 128:(h + 1) * 128], oab[:, h * 128:(h + 1) * 128], identb[:], r=[b_oab, b_identb], w=[PB[1]])
        CP("act", oaTt[t % 2], tb[:, 0:512].rearrange("p (a b) -> p a b", a=4), r=[PB[1]], w=[b_oaTt[t % 2]])
        DMA("sp", CatT[0:4, :, t * 128:(t + 1) * 128].rearrange("k p c -> p k c"), oaTt[t % 2], r=[b_oaTt[t % 2]], w=[dCat])


    tl2 = list(range(nt2)) if phases >= 2 else []
    if tl2:
        hcur = ph2_fe(tl2[0])
    for ii, t in enumerate(tl2):
        ph2_proj(t, *hcur)
        hnext = ph2_fe(tl2[ii + 1]) if ii + 1 < len(tl2) else None
        ph2_rest(t)
        hcur = hnext

    S.barrier()
    apos[0] = a0
    KTh = [carve([128, L], BF16) for _ in range(2)]; b_KTh = [Buf("KTh0"), Buf("KTh1")]
    Vh = [carve([128, NT, 129], BF16) for _ in range(2)]; b_Vh = [Buf("Vh0"), Buf("Vh1")]
    QTm = [[carve([128, TQ], BF16) for _ in range(2)] for _ in range(2)]
    b_QTh = [Buf("QTh0"), Buf("QTh1")]
    PT = [carve([128, 2, 512], BF16) for _ in range(3)]; b_PT = [Buf("PT%d" % i) for i in range(3)]
    sbias = [carve([128, 2, 512]) for _ in range(2)]; b_sbias = [Buf("sbias0"), Buf("sbias1")]
    ep = carve([128, 8]); b_ep = Buf("ep")
    ot0 = carve([128, 128]); b_ot0 = Buf("ot0")
    ob1 = carve([128, 128]); b_ob1 = Buf("ob1")
    obj = carve([128, 128]); b_obj = Buf("obj")
    obn = carve([128, 128], BF16); b_obn = Buf("obn")
    obTt = [carve([128, 128], BF16) for _ in range(2)]; b_obTt = [Buf("obTt0"), Buf("obTt1")]
    ocnt = [0]
    Osb = carve([128, 3, 512]); b_Osb = Buf("Osb")
    assert apos[0] <= ARENA, apos[0]
    bS = [Buf("S0"), Buf("S1")]
    bO = [Buf("O0"), Buf("O1"), Buf("O2")]
    def oacc(m, qt):
        idx = m * 4 + qt
        return bank(4 + idx // 3)[:, (idx % 3) * 129:(idx % 3) * 129 + 129], bO[idx // 3]

    if phases >= 3:
        def load_head(h):
            s = h % 2
            DMA("sp", KTh[s], KdT[h], r=[dKdT], w=[b_KTh[s]])
            DMA("sp", Vh[s], Vd[h], r=[dVd], w=[b_Vh[s]])
            DMA("sp", QTm[s][0][0:64, :], QdT[h, 0:64, :], r=[dQdT], w=[b_QTh[s]])
            DMA("sp", QTm[s][1][64:128, :], QdT[h, 64:128, :], r=[dQdT], w=[b_QTh[s]])
        for s_ in range(2):
            MSET("dve", QTm[s_][0][64:128, :], 0.0, w=[Buf("qz")])
            MSET("dve", QTm[s_][1][0:64, :], 0.0, w=[Buf("qz")])
        S.barrier()
        load_head(0)
        pcnt = 0
        scnt = 0
        ngrp = (nt2 * 128 + 511) // 512
        glist = [(h, g) for h in range(nheads) for g in range(ngrp)]

        def smm_hg(h, g, kb):
            s = h % 2
            nq = min(512, nt2 * 128 - g * 512)
            sl = kb % 2
            for m in range(2):
                MM(bank(2 * sl + m)[:, 0:nq], KTh[s][:, kb * 128:(kb + 1) * 128],
                   QTm[s][m][:, g * 512:g * 512 + nq], r=[b_KTh[s], b_QTh[s]], w=[bS[sl]])
        for gi, (h, g) in enumerate(glist):
            s = h % 2
            if g == 0 and h + 1 < nheads:
                load_head(h + 1)
            if True:
                nq = min(512, nt2 * 128 - g * 512)
                nqt = nq // 128
                if gi == 0:
                    smm_hg(h, g, 0)
                    smm_hg(h, g, 1)
                for b3 in range(3):
                    MM(bank(4 + b3), zerob[:, 0:128], zerob[:], r=[b_zero], w=[bO[b3]])

                def smm(kb, h=h, g=g):
                    smm_hg(h, g, kb)
                for kb in range(NT):
                    sl = kb % 2
                    pt = PT[pcnt % 3]; bpt = b_PT[pcnt % 3]
                    pcnt += 1
                    sv = ps[:, sl * 1024:sl * 1024 + 1024].rearrange("p (a b) -> p a b", a=2)[:, :, 0:nq]
                    e = kb - 4 * g
                    uni = (e >= 5) or (e <= -2)
                    if uni:
                        col = strips[:, h, 0:1] if e > 0 else strips[:, h, 10 * 128:10 * 128 + 1]
                        ACT(pt[:, :, 0:nq], sv, AF.Exp, scale=0.125, bias=col, r=[bS[sl], b_strips[h]], w=[bpt])
                    else:
                        sbv = sbias[scnt % 2]; bsb = b_sbias[scnt % 2]
                        scnt += 1
                        win = strips[:, h, (5 - e) * 128:(5 - e) * 128 + nq]
                        STT("dve", sbv[:, :, 0:nq], sv, 0.125, win.unsqueeze(1).to_broadcast([128, 2, nq]), ALU.mult, ALU.add,
                            r=[bS[sl], b_strips[h]], w=[bsb])
                        ACT(pt[:, :, 0:nq], sbv[:, :, 0:nq], AF.Exp, r=[bsb], w=[bpt])
                    if kb + 2 < NT:
                        smm(kb + 2)
                    for m in range(2):
                        for qt in range(nqt):
                            oap, ob = oacc(m, qt)
                            MM(oap, pt[:, m, qt * 128:(qt + 1) * 128], Vh[s][:, kb, :], start=False, stop=(kb == NT - 1),
                               r=[bpt, b_Vh[s]], w=[ob], skip_group_check=True)
                if gi + 1 < len(glist):
                    smm_hg(glist[gi + 1][0], glist[gi + 1][1], 0)
                    smm_hg(glist[gi + 1][0], glist[gi + 1][1], 1)
                for b3 in range(3):
                    CP("dve", Osb[:, b3, :], bank(4 + b3), r=[bO[b3]], w=[b_Osb])

                def osb(m, qt):
                    idx = m * 4 + qt
                    return Osb[:, idx // 3, (idx % 3) * 129:(idx % 3) * 129 + 129], b_Osb
                for qt in range(nqt):
                    o0, bo0 = osb(0, qt)
                    o1, bo1 = osb(1, qt)
                    S.op("dve", lambda e, o0=o0: e.reciprocal(ep[:, 0:1], o0[:, 128:129]), [bo0], [b_ep])
                    S.op("dve", lambda e, o1=o1: e.reciprocal(ep[:, 1:2], o1[:, 128:129]), [bo1], [b_ep])
                    TT("dve", ep[:, 2:3], ep[:, 1:2], lamc[:, 1:2], ALU.mult, r=[b_ep, b_lam], w=[b_ep])
                    TS("dve", ot0, o0[:, 0:128], ep[:, 0:1], None, ALU.mult, r=[bo0, b_ep], w=[b_ot0])
                    STT("dve", ob1, o1[:, 0:128], ep[:, 2:3], ot0, ALU.mult, ALU.add, r=[bo1, b_ep, b_ot0], w=[b_ob1])
                    TT("dve", obj, ob1, ob1, ALU.mult, r=[b_ob1], w=[b_obj])
                    S.op("dve", lambda e: e.tensor_reduce(ep[:, 3:4], obj, AX.X, ALU.add), [b_obj], [b_ep])
                    TS("dve", ep[:, 4:5], ep[:, 3:4], 1.0 / 128, EPS, ALU.mult, ALU.add, r=[b_ep], w=[b_ep])
                    ACT(ep[:, 4:5], ep[:, 4:5], AF.Ln, r=[b_ep], w=[b_ep])
                    ACT(ep[:, 4:5], ep[:, 4:5], AF.Exp, scale=-0.5, r=[b_ep], w=[b_ep])
                    STT("dve", obn, ob1, ep[:, 4:5], dnwb[:], ALU.mult, ALU.mult, r=[b_ob1, b_ep, b_dnw], w=[b_obn])
                    tb = bankb(7)
                    TR(tb[:, 0:128], obn, identb[:], r=[b_obn, b_identb], w=[PB[7]])
                    tok = g * 512 + qt * 128
                    oi = ocnt[0] % 2
                    ocnt[0] += 1
                    CP("act", obTt[oi], tb[:, 0:128], r=[PB[7]], w=[b_obTt[oi]])
                    DMA("sp", CatT[4 + h, :, tok:tok + 128], obTt[oi], r=[b_obTt[oi]], w=[dCat])

    S.barrier()
    apos[0] = a0
    Wout = carve([128, 8, D], BF16); b_Wout = Buf("Wout")
    x1s = [carve([128, D]) for _ in range(2)]; b_x1 = [Buf("x1a"), Buf("x1b")]
    xts4 = [carve([128, D]) for _ in range(2)]; b_xt4 = [Buf("xt4a"), Buf("xt4b")]
    junk = carve([128, D], BF16); b_junk = Buf("junk4")
    xn = carve([128, D], BF16); b_xn = Buf("xn4")
    st2 = carve([128, 2]); b_st = Buf("st4")
    hT = [carve([128, 8, 128], BF16) for _ in range(2)]; b_hT = [Buf("h2T0"), Buf("h2T1")]
    zcol = carve([128, 8, 2], BF16); b_zcol = Buf("zcol")
    catg = [carve([128, 8, 512], BF16) for _ in range(2)]; b_catg = [Buf("catg0"), Buf("catg1")]
    h2b = [carve([128, 8, 512], BF16) for _ in range(2)]; b_h2b = [Buf("h2b0"), Buf("h2b1")]
    if phases >= 4:
        w_out_v = w_out.rearrange("(k p) n -> p k n", p=128)
        S.barrier()
        for hh in range(2):
            DMA("pool", Wout[:, :, hh * 512:(hh + 1) * 512], w_out_v[:, :, hh * 512:(hh + 1) * 512],
                w=[Buf("wo%d" % hh)] if hh == 0 else [b_Wout])
        MSET("dve", zcol, 0.0, w=[b_zcol])
        DMA("sp", H2T[:, :, 0:2].rearrange("k p c -> p k c"), zcol, r=[b_zcol], w=[dH2T])
        S.barrier()
        def ph4_mm(t):
            i = t % 2
            DMA("sp", xts4[i], x[t * 128:(t + 1) * 128, :], w=[b_xt4[i]])
            g4, q4 = t // 4, t % 4
            gs = g4 % 2
            if q4 == 0:
                n4 = min(512, nt2 * 128 - g4 * 512)
                DMA("sp", catg[gs][:, :, 0:n4], CatT[:, :, g4 * 512:g4 * 512 + n4].rearrange("k p c -> p k c"),
                    r=[dCat], w=[b_catg[gs]])
            for hh in range(2):
                bk = 1 + 2 * i + hh
                for kc in range(8):
                    MM(bank(bk), catg[gs][:, kc, q4 * 128:(q4 + 1) * 128], Wout[:, kc, hh * 512:(hh + 1) * 512],
                       start=(kc == 0), stop=(kc == 7), r=[b_catg[gs], b_Wout], w=[PB[bk]])

        def ph4_a(t):
            i = t % 2
            for hh in range(2):
                bk = 1 + 2 * i + hh
                cs = slice(hh * 512, (hh + 1) * 512)
                TT("dve", x1s[i][:, cs], bank(bk), GA[:, cs], ALU.mult, r=[PB[bk], b_GA], w=[b_x1[i]])
            TT("dve", x1s[i], x1s[i], xts4[i], ALU.add, r=[b_x1[i], b_xt4[i]], w=[b_x1[i]])
            if t < NOWN:
                DMA("sp", X1[t * 128:(t + 1) * 128, :], x1s[i], r=[b_x1[i]], w=[dX1])

        def ph4_b(t):
            i = t % 2
            g4, q4 = t // 4, t % 4
            gs = g4 % 2
            front_end(x1s[i], 16, 0, rd=[b_x1[i]], out=(h2b[gs][:, :, q4 * 128:(q4 + 1) * 128], b_h2b[gs]))
            if q4 == 3 or t == nt2 - 1:
                n4 = (q4 + 1) * 128
                DMA("sp", H2T[:, :, 1 + g4 * 512:1 + g4 * 512 + n4].rearrange("k p c -> p k c"), h2b[gs][:, :, 0:n4],
                    r=[b_h2b[gs]], w=[dH2T])

        ph4_mm(0)
        if nt2 > 1:
            ph4_mm(1)
        ph4_a(0)
        for t in range(nt2):
            if t + 1 < nt2:
                ph4_a(t + 1)
            ph4_b(t)
            if t + 2 < nt2:
                ph4_mm(t + 2)

    S.barrier()
    apos[0] = 0
    if phases >= 5:
        Wup = carve([128, 8, 2 * DFF], BF16); b_Wup = Buf("Wup")
        w_up_v = w_up.rearrange("(k p) n -> p k n", p=128)
        for c in range(11):
            DMA("pool", Wup[:, :, c * 512:(c + 1) * 512], w_up_v[:, :, c * 512:(c + 1) * 512], w=[Buf("wu%d" % c)])
        Wdn = carve([128, NFC, D], BF16); b_Wdn = Buf("Wdn")

        def wdn(j):
            return Wdn[:, j, :]
        w_dn_v = w_down.rearrange("(k p) n -> p k n", p=128)
        for j in range(NFC):
            for hh in range(2):
                DMA("pool", wdn(j)[:, hh * 512:(hh + 1) * 512], w_dn_v[:, j, hh * 512:(hh + 1) * 512], w=[Buf("wd")])
        DMA("sp", cw[:], cwp, w=[Buf("cwx")])
        DMA("sp", cb[:], cbp, w=[b_cw])
        h2g = [carve([128, 8, 514], BF16)]; b_h2g = [Buf("h2g0")]
        gT = carve([128, NFC, 512], BF16); b_gT = Buf("gT")
        cc = [carve([128, 512]) for i in range(2)]; b_cc = [Buf("cc0"), Buf("cc1")]
        sg = carve([128, 512]); b_sg = Buf("sg")
        yt = carve([128, D]); b_yt = Buf("yt")
        x1l = carve([128, D]); b_x1l = Buf("x1l")
        fj = carve([128, D], BF16); b_fj = Buf("fj")
        fst = carve([128, 2]); b_fst = Buf("fst")
        S.barrier()
        bU = [Buf("U0"), Buf("U1"), Buf("U2")]
        ucnt = 0
        for g in range(8):
            DMA("sp", h2g[0], H2T[:, :, g * 512:g * 512 + 514].rearrange("k p c -> p k c"), r=[dH2T], w=[b_h2g[0]])
            for j in range(NFC):
                for which in range(2):
                    c = j + which * NFC
                    ui = ucnt % 3
                    ucnt += 1
                    U = ps[:, ui * 1024:ui * 1024 + 1024]
                    for k in range(8):
                        MM(U[:, 0:512], Wup[:, k, c * 128:(c + 1) * 128], h2g[0][:, k, 0:512], start=(k == 0), stop=(k == 7),
                           r=[b_h2g[0], b_Wup], w=[bU[ui]])
                    for k in range(8):
                        MM(U[:, 512:514], Wup[:, k, c * 128:(c + 1) * 128], h2g[0][:, k, 512:514], start=(k == 0), stop=(k == 7),
                           r=[b_h2g[0], b_Wup], w=[bU[ui]])
                    ACT(cc[which], U[:, 1:513], AF.Identity, scale=cw[:, 44 + c:44 + c + 1], bias=cb[:, c:c + 1],
                        r=[bU[ui], b_cw], w=[b_cc[which]])
                    STT("dve", cc[which], U[:, 0:512], cw[:, c:c + 1], cc[which], ALU.mult, ALU.add,
                        r=[bU[ui], b_cw, b_cc[which]], w=[b_cc[which]])
                    STT("dve", cc[which], U[:, 2:514], cw[:, 88 + c:88 + c + 1], cc[which], ALU.mult, ALU.add,
                        r=[bU[ui], b_cw, b_cc[which]], w=[b_cc[which]])
                ACT(sg, cc[0], AF.Silu, r=[b_cc[0]], w=[b_sg])
                TT("dve", gT[:, j, :], sg, cc[1], ALU.mult, r=[b_sg, b_cc[1]], w=[b_gT])
            for qt in range(4):
                t = g * 4 + qt
                DMA("sp", x1l[:], X1[t * 128:(t + 1) * 128, :], r=[dX1], w=[b_x1l])
                for hh in range(2):
                    for j in range(NFC):
                        MM(bank(6 + hh), gT[:, j, qt * 128:(qt + 1) * 128], wdn(j)[:, hh * 512:(hh + 1) * 512],
                           start=(j == 0), stop=(j == NFC - 1), r=[b_gT, b_Wdn], w=[PB[6 + hh]])
                for hh in range(2):
                    cs = slice(hh * 512, (hh + 1) * 512)
                    TT("dve", yt[:, cs], bank(6 + hh), GF[:, cs], ALU.mult, r=[PB[6 + hh], b_GF], w=[b_yt])
                TT("dve", yt[:], yt[:], x1l[:], ALU.add, r=[b_yt, b_x1l], w=[b_yt])
                ACT(fj[:], yt[:], AF.Square, accum_out=fst[:, 0:1], r=[b_yt], w=[b_fj, b_fst])
                TS("dve", fst[:, 1:2], fst[:, 0:1], 1.0 / D, EPS, ALU.mult, ALU.add, r=[b_fst], w=[b_fst])
                ACT(fst[:, 1:2], fst[:, 1:2], AF.Ln, r=[b_fst], w=[b_fst])
                ACT(fst[:, 1:2], fst[:, 1:2], AF.Exp, scale=-0.5, r=[b_fst], w=[b_fst])
                STT("dve", yt[:], yt[:], fst[:, 1:2], WF[:], ALU.mult, ALU.mult, r=[b_yt, b_fst, b_WF], w=[b_yt])
                DMA("sp", y[t * 128:(t + 1) * 128, :], yt[:], r=[b_yt], w=[Buf("yout")])
    elif dbg:
        pass

    S.emit(nc, stack)
    stack.close()
    nc._sched_stats = S.stats
    return nc


def _consts():
    j = np.arange(128)[:, None]
    c = np.arange(128)[None, :]
    same = (j // 64) == (c // 64)
    s = -1.0 / 16.0
    tri = [
        np.where(same & (j <= c), s, 0.0),
        np.where(same & (j > c), s, 0.0),
        np.where(same & (j >= c), s, 0.0),
        np.where(same & (j < c), s, 0.0),
    ]
    cind = np.stack([np.where(np.arange(128) < 64, s, 0.0), np.where(np.arange(128) >= 64, s, 0.0)], axis=1)
    return [t.astype(np.float32) for t in tri], cind.astype(np.float32), same


def make_in_maps(inputs):
    f = lambda a: np.ascontiguousarray(np.asarray(a))
    x = f(inputs["x"]); c = f(inputs["c"]); positions = f(inputs["positions"])
    w_in = f(inputs["w_in"])[0]
    tri, cind, same = _consts()
    s_idx = np.arange(128)[:, None]
    c_idx = np.arange(128)[None, :]
    table = f(inputs["rel_bias_table"])
    ia = [a for (_, a, _) in T5_BREAKS]
    ib = [b for (_, _, b) in T5_BREAKS]
    tabA = np.ascontiguousarray(table[ia, :].T)
    tabB = np.ascontiguousarray(table[ib, :].T)
    tab0 = np.ascontiguousarray(table[T5_B0:T5_B0 + 1, :].T)
    conv_w = f(inputs["conv_w"])[0]
    conv_b = f(inputs["conv_b"])[0]
    maps = []
    for core in range(8):
        b, hf = core // 2, core % 2
        xs = x[b]; ps_ = positions[b]
        cwv = conv_w
        wi = w_in
        names = ("fwd", "bwd")
        if hf == 1:
            xs = xs[::-1]; ps_ = ps_[::-1]
            cwv = conv_w[::-1]
            wi = w_in.copy()
            wi[:, 1536:1552] = w_in[:, 1552:1568]
            wi[:, 1552:1568] = w_in[:, 1536:1552]
            names = ("bwd", "fwd")
        if hf == 0:
            m_up = same & (s_idx <= c_idx); m_dn = same & (s_idx > c_idx)
        else:
            m_up = same & (s_idx < c_idx); m_dn = same & (s_idx >= c_idx)
        m = {
            "x": np.ascontiguousarray(xs), "pos": np.ascontiguousarray(ps_).astype(np.int32),
            "ccol": np.ascontiguousarray(c[b].reshape(8, 128).T),
            "w_ada": f(inputs["w_ada"])[0], "b_ada": f(inputs["b_ada"]),
            "anw": f(inputs["attn_norm_w"]), "fnw": f(inputs["ffn_norm_w"]), "finw": f(inputs["final_norm_w"]),
            "w_in": np.ascontiguousarray(wi),
            "decw_up": f(inputs["gla_dec_w_" + names[0]])[0], "decw_dn": f(inputs["gla_dec_w_" + names[1]])[0],
            "decb_up": f(inputs["gla_dec_b_" + names[0]]), "decb_dn": f(inputs["gla_dec_b_" + names[1]]),
            "gnw": f(inputs["gla_norm_w"])[0],
            "lam0": f(inputs["diff_lambda_q1"])[0], "lam1": f(inputs["diff_lambda_k1"])[0],
            "lam2": f(inputs["diff_lambda_q2"])[0], "lam3": f(inputs["diff_lambda_k2"])[0],
            "dnw": f(inputs["diff_norm_w"])[0],
            "tabA": tabA, "tabB": tabB, "tab0": tab0,
            "w_out": f(inputs["w_out"])[0], "w_up": f(inputs["w_up"])[0],
            "cwp": np.ascontiguousarray(cwv.reshape(3, 44, 128).transpose(2, 0, 1).reshape(128, 132)),
            "cbp": np.ascontiguousarray(conv_b.reshape(44, 128).T),
            "w_down": f(inputs["w_down"])[0],
            "identf": np.eye(128, dtype=np.float32),
            "tri0": tri[0], "tri1": tri[1], "tri2": tri[2], "tri3": tri[3], "cind": cind,
            "mask_up": m_up.astype(np.float32), "mask_dn": m_dn.astype(np.float32),
        }
        maps.append({k: np.ascontiguousarray(v) for k, v in m.items()})
    return maps


_NC = {}


def kernel(**inputs):
    if "nc" not in _NC:
        _NC["nc"] = build_program()
    nc = _NC["nc"]
    maps = make_in_maps(inputs)
    res = run_bass_kernel_spmd(nc, maps, core_ids=list(range(8)))
    out = np.empty((4, L, D), np.float32)
    for core in range(8):
        b, hf = core // 2, core % 2
        yl = np.asarray(res.results[core]["y"])
        if hf == 0:
            out[b, :4096] = yl
        else:
            out[b, 4096:] = yl[::-1]
    return out
```
